# Optimizing a Trainium2 kernel written in Bass

```python
import math
import jax, jax.numpy as jnp
from jax import lax
import numpy as np

D_MODEL = 1024
BATCH = 32
SEQ = 2048
DEPTH = 4
DEC_BATCH = 8
DEC_SEQ = 4096
PAST_LEN = 128

HEAD_DIM = 64
CHUNK = 128
A_GROUPS = 8
W_A = A_GROUPS * HEAD_DIM
DILATED_GROUPS = ((128, 1), (512, 4), (2048, 16))
HEADS_PER_GROUP = 2
H_B = HEADS_PER_GROUP * len(DILATED_GROUPS)
W_B = H_B * HEAD_DIM
H_C = 4
W_C_QK = 2 * H_C * HEAD_DIM
W_C = H_C * 2 * HEAD_DIM
N_ATTN_HEADS = H_B + H_C
NUM_BUCKETS = 32
REL_MAX_DIST = 1024
QBLK = 128
D_FF = ((8 * D_MODEL + 3 * 256 - 1) // (3 * 256)) * 256
N_BRANCH = 3
SPLIT_SIZES = (2 * W_A, 3 * W_B, 2 * W_C_QK + W_C, N_BRANCH * D_MODEL)
IN_COLS = sum(SPLIT_SIZES)
SPLIT_POINTS = tuple(int(c) for c in np.cumsum(SPLIT_SIZES)[:-1])
RMS_EPS = 1e-6
LN_EPS = 1e-5
NEG_INF = -1e30

kernel_name = 'hybrid_gated_gmlp_dilated_diff_encoder'


def rms_norm(x, g):
    xf = x.astype(jnp.float32)
    y = xf * lax.rsqrt(jnp.mean(xf * xf, axis=-1, keepdims=True) + RMS_EPS)
    return (y * g.astype(jnp.float32)).astype(x.dtype)


def layer_norm(x, g, b):
    xf = x.astype(jnp.float32)
    xc = xf - jnp.mean(xf, axis=-1, keepdims=True)
    var = jnp.mean(xc * xc, axis=-1, keepdims=True)
    return (xc * lax.rsqrt(var + LN_EPS) * g.astype(jnp.float32) + b.astype(jnp.float32)).astype(x.dtype)


def rel_bucket(rel):
    nb = NUM_BUCKETS // 2
    max_exact = nb // 2
    n = jnp.abs(rel)
    sign_off = jnp.where(rel > 0, nb, 0)
    nf = jnp.maximum(n, 1).astype(jnp.float32)
    large = max_exact + (jnp.log(nf / max_exact) / math.log(REL_MAX_DIST / max_exact)
                         * (nb - max_exact)).astype(jnp.int32)
    large = jnp.minimum(large, nb - 1)
    return sign_off + jnp.where(n < max_exact, n, large)


def banded_attention(q, k, v, bias, half):
    n, L, h, hd = q.shape
    nb = -(-L // half)
    lp = nb * half
    pad = lp - L
    qp = jnp.pad(q, ((0, 0), (0, pad), (0, 0), (0, 0))).reshape(n, nb, half, h, hd)

    def windows(t):
        tp = jnp.pad(t, ((0, 0), (half, pad + half), (0, 0), (0, 0))).reshape(n, nb + 2, half, h, t.shape[-1])
        return jnp.concatenate([tp[:, :-2], tp[:, 1:-1], tp[:, 2:]], axis=2)

    kw = windows(k)
    vw = windows(v)
    s = jnp.einsum('nbqhd,nbkhd->nbhqk', qp, kw).astype(jnp.float32) + bias.astype(jnp.float32)[None, None]
    qpos = jnp.arange(nb)[:, None] * half + jnp.arange(half)[None, :]
    kpos = (jnp.arange(nb)[:, None] - 1) * half + jnp.arange(3 * half)[None, :]
    kp3 = kpos[:, None, :]
    valid = (kp3 >= 0) & (kp3 < L) & (jnp.abs(kp3 - qpos[:, :, None]) <= half)
    s = jnp.where(valid[None, :, None], s, NEG_INF)
    lse = jax.nn.logsumexp(s, axis=-1)
    p = jnp.exp(s - lse[..., None])
    o = jnp.einsum('nbhqk,nbkhd->nbqhd', p.astype(v.dtype), vw).reshape(n, lp, h, v.shape[-1])[:, :L]
    lse = jnp.swapaxes(lse, 2, 3).reshape(n, lp, h)[:, :L]
    return o, lse


def dilated_attention(q, k, v, table, window, dil):
    b, s, h, hd = q.shape
    half = window // (2 * dil)
    sub = s // dil

    def to_sub(t):
        return jnp.swapaxes(t.reshape(b, sub, dil, h, t.shape[-1]), 1, 2).reshape(b * dil, sub, h, t.shape[-1])

    rel = (jnp.arange(3 * half)[None, :] - half - jnp.arange(half)[:, None]) * dil
    bias = jnp.transpose(table[rel_bucket(rel)], (2, 0, 1))
    o, lse = banded_attention(to_sub(q), to_sub(k), to_sub(v), bias, half)
    o = jnp.swapaxes(o.reshape(b, dil, sub, h, hd), 1, 2).reshape(b, s, h, hd)
    lse = jnp.swapaxes(lse.reshape(b, dil, sub, h), 1, 2).reshape(b, s, h)
    return o, lse


def diff_attention(q, k, v, table, lam):
    b, s, _, h, hd = q.shape
    nq = s // QBLK
    q_blocks = jnp.moveaxis(q.reshape(b, nq, QBLK, 2, h, hd), 1, 0)
    starts = jnp.arange(nq, dtype=jnp.int32) * QBLK
    kpos = jnp.arange(s, dtype=jnp.int32)

    def one_block(args):
        qi, q0 = args
        logits = jnp.einsum('bqchd,bkchd->bchqk', qi, k).astype(jnp.float32)
        rel = kpos[None, :] - (q0 + jnp.arange(QBLK, dtype=jnp.int32))[:, None]
        bias = jnp.transpose(table[rel_bucket(rel)], (2, 0, 1)).astype(jnp.float32)
        probs = jax.nn.softmax(logits + bias[None, None], axis=-1)
        attn = probs[:, 0] - lam * probs[:, 1]
        return jnp.einsum('bhqk,bkhe->bqhe', attn.astype(v.dtype), v)

    o = lax.map(one_block, (q_blocks, starts))
    return jnp.moveaxis(o, 0, 1).reshape(b, s, h, v.shape[-1])


def encoder_layer(x, p, l):
    b, s, _ = x.shape
    lam_init = 0.8 - 0.6 * math.exp(-0.3 * l)
    h = rms_norm(x, p['norm1_g'][l])
    z = h @ p['w_in'][l]
    za, zb, zc, zg = jnp.split(z, SPLIT_POINTS, axis=-1)

    u, v = jnp.split(jax.nn.gelu(za), 2, axis=-1)
    vn = layer_norm(v, p['sgu_ln_g'][l], p['sgu_ln_b'][l]).reshape(b, s // CHUNK, CHUNK, A_GROUPS, HEAD_DIM)
    sv = jnp.einsum('gpq,bcqgd->bcpgd', p['sgu_w'][l], vn) + p['sgu_b'][l].T[None, None, :, :, None]
    a_out = u * sv.reshape(b, s, W_A)

    qb, kb, vb = (t.reshape(b, s, H_B, HEAD_DIM) for t in jnp.split(zb, 3, axis=-1))
    qb = rms_norm(qb, p['qn_b'][l]) * HEAD_DIM ** -0.5
    kb = rms_norm(kb, p['kn_b'][l])
    outs, lses = [], []
    for gi, (window, dil) in enumerate(DILATED_GROUPS):
        hs = slice(gi * HEADS_PER_GROUP, (gi + 1) * HEADS_PER_GROUP)
        o_g, lse_g = dilated_attention(qb[:, :, hs], kb[:, :, hs], vb[:, :, hs], p['rel_bias'][:, hs], window, dil)
        outs.append(o_g)
        lses.append(lse_g)
    o_b = jnp.stack(outs, axis=2)
    alpha = jax.nn.softmax(jnp.stack(lses, axis=2), axis=2)
    b_out = (o_b * alpha[..., None].astype(o_b.dtype)).reshape(b, s, W_B)

    qc, kc, vc = jnp.split(zc, (W_C_QK, 2 * W_C_QK), axis=-1)
    qc = rms_norm(qc.reshape(b, s, 2, H_C, HEAD_DIM), p['qn_c'][l]) * HEAD_DIM ** -0.5
    kc = rms_norm(kc.reshape(b, s, 2, H_C, HEAD_DIM), p['kn_c'][l])
    vc = vc.reshape(b, s, H_C, 2 * HEAD_DIM)
    lq1 = p['lam_q1'][l].astype(jnp.float32)
    lk1 = p['lam_k1'][l].astype(jnp.float32)
    lq2 = p['lam_q2'][l].astype(jnp.float32)
    lk2 = p['lam_k2'][l].astype(jnp.float32)
    lam = jnp.exp(jnp.sum(lq1 * lk1)) - jnp.exp(jnp.sum(lq2 * lk2)) + lam_init
    o_c = diff_attention(qc, kc, vc, p['rel_bias'][:, H_B:], lam)
    c_out = (rms_norm(o_c, p['subln_g'][l]) * (1.0 - lam_init)).reshape(b, s, W_C)

    gates = jax.nn.sigmoid(zg.astype(jnp.float32)).astype(x.dtype).reshape(b, s, N_BRANCH, D_MODEL)
    merged = (gates[:, :, 0] * (a_out @ p['w_pa'][l])
              + gates[:, :, 1] * (b_out @ p['w_pb'][l])
              + gates[:, :, 2] * (c_out @ p['w_pc'][l]))
    x = x + merged @ p['w_o'][l]

    h2 = rms_norm(x, p['norm2_g'][l])
    g_ff, u_ff = jnp.split(h2 @ p['w_gu'][l], 2, axis=-1)
    return x + (jax.nn.silu(g_ff) * u_ff) @ p['w_down'][l]


def encoder_trunk(x, p):
    for l in range(DEPTH):
        x = encoder_layer(x, p, l)
    return x


def setup_inputs(seed: int = 0) -> dict:
    key = jax.random.key(seed)
    ks = jax.random.split(key, 32)

    def nrm(k, shape, scale):
        return jax.random.normal(k, shape, jnp.float32) * scale

    def gain(k, shape):
        return 1.0 + 0.05 * jax.random.normal(k, shape, jnp.float32)

    return {
        'x_prompt': nrm(ks[0], (BATCH, SEQ, D_MODEL), 1.0),
        'x_sample': nrm(ks[1], (DEC_BATCH, DEC_SEQ, D_MODEL), 1.0),
        'rel_bias': nrm(ks[2], (NUM_BUCKETS, N_ATTN_HEADS), 0.5),
        'norm1_g': gain(ks[3], (DEPTH, D_MODEL)),
        'w_in': nrm(ks[4], (DEPTH, D_MODEL, IN_COLS), D_MODEL ** -0.5),
        'sgu_ln_g': gain(ks[5], (DEPTH, W_A)),
        'sgu_ln_b': nrm(ks[6], (DEPTH, W_A), 0.02),
        'sgu_w': nrm(ks[7], (DEPTH, A_GROUPS, CHUNK, CHUNK), CHUNK ** -0.5),
        'sgu_b': 1.0 + nrm(ks[8], (DEPTH, A_GROUPS, CHUNK), 0.1),
        'qn_b': gain(ks[9], (DEPTH, HEAD_DIM)),
        'kn_b': gain(ks[10], (DEPTH, HEAD_DIM)),
        'qn_c': gain(ks[11], (DEPTH, HEAD_DIM)),
        'kn_c': gain(ks[12], (DEPTH, HEAD_DIM)),
        'lam_q1': nrm(ks[13], (DEPTH, HEAD_DIM), 0.1),
        'lam_k1': nrm(ks[14], (DEPTH, HEAD_DIM), 0.1),
        'lam_q2': nrm(ks[15], (DEPTH, HEAD_DIM), 0.1),
        'lam_k2': nrm(ks[16], (DEPTH, HEAD_DIM), 0.1),
        'subln_g': gain(ks[17], (DEPTH, 2 * HEAD_DIM)),
        'w_pa': nrm(ks[18], (DEPTH, W_A, D_MODEL), W_A ** -0.5),
        'w_pb': nrm(ks[19], (DEPTH, W_B, D_MODEL), W_B ** -0.5),
        'w_pc': nrm(ks[20], (DEPTH, W_C, D_MODEL), W_C ** -0.5),
        'w_o': nrm(ks[21], (DEPTH, D_MODEL, D_MODEL), D_MODEL ** -0.5),
        'norm2_g': gain(ks[22], (DEPTH, D_MODEL)),
        'w_gu': nrm(ks[23], (DEPTH, D_MODEL, 2 * D_FF), D_MODEL ** -0.5),
        'w_down': nrm(ks[24], (DEPTH, D_FF, D_MODEL), D_FF ** -0.5),
    }


def reference(x_prompt, x_sample, rel_bias, norm1_g, w_in, sgu_ln_g, sgu_ln_b, sgu_w, sgu_b,
              qn_b, kn_b, qn_c, kn_c, lam_q1, lam_k1, lam_q2, lam_k2, subln_g,
              w_pa, w_pb, w_pc, w_o, norm2_g, w_gu, w_down):
    params = dict(rel_bias=rel_bias, norm1_g=norm1_g, w_in=w_in, sgu_ln_g=sgu_ln_g, sgu_ln_b=sgu_ln_b,
                  sgu_w=sgu_w, sgu_b=sgu_b, qn_b=qn_b, kn_b=kn_b, qn_c=qn_c, kn_c=kn_c,
                  lam_q1=lam_q1, lam_k1=lam_k1, lam_q2=lam_q2, lam_k2=lam_k2, subln_g=subln_g,
                  w_pa=w_pa, w_pb=w_pb, w_pc=w_pc, w_o=w_o, norm2_g=norm2_g, w_gu=w_gu, w_down=w_down)
    y_prompt = encoder_trunk(x_prompt, params)
    y_sample = encoder_trunk(x_sample, params)
    return (y_prompt, y_sample)
```

```python
import contextlib
import math
import numpy as np
import ml_dtypes
import concourse.bass as bass
import concourse.mybir as mybir
from concourse.bass_utils import run_bass_kernel_spmd

F32 = mybir.dt.float32
BF16 = mybir.dt.bfloat16
AF = mybir.ActivationFunctionType
ALU = mybir.AluOpType
AX = mybir.AxisListType

D = 1024
INC = 6784
DFF = 2816
NBUCK = 32
TB = 512
RMS_EPS = 1e-6
LN_EPS = 1e-5
GEO = {"c": (-768, 1152, 1, None), "b0": (-128, 512, 1, 64), "b1": (-256, 640, 4, 256), "b2": (-1024, 1408, 16, 1024)}
GEO_ORDER = ["c", "b0", "b1", "b2"]
GEO_NH = {"c": 4, "b0": 2, "b1": 2, "b2": 2}


def geo_W(g):
    return GEO[g][1] - GEO[g][0] + 512


def geo_L(g):
    return geo_W(g) + 127


STRIP_OFF = {}
_o = 0
for _g in GEO_ORDER:
    STRIP_OFF[_g] = _o
    _o += GEO_NH[_g] * geo_W(_g)
STRIP_COLS = _o
LMAX = max(geo_L(g) for g in GEO_ORDER)


def _rel_bucket_np(rel):
    nb = NBUCK // 2
    max_exact = nb // 2
    n = np.abs(rel)
    sign_off = np.where(rel > 0, nb, 0)
    nf = np.maximum(n, 1).astype(np.float32)
    large = max_exact + (np.log(nf / np.float32(max_exact)) / np.float32(math.log(1024 / max_exact))
                         * np.float32(nb - max_exact)).astype(np.int32)
    large = np.minimum(large, nb - 1)
    return sign_off + np.where(n < max_exact, n, large)


def host_constants():
    oh = np.zeros((4, NBUCK, LMAX), np.float32)
    mask = np.zeros((4, LMAX), np.float32)
    for gi, g in enumerate(GEO_ORDER):
        dmin, dmax, dil, hw = GEO[g]
        L = geo_L(g)
        rhi = dmax + 127
        rel = rhi - np.arange(L)
        b = _rel_bucket_np(rel)
        oh[gi, b, np.arange(L)] = 1.0
        if hw is None:
            mask[gi, :L] = 1.0
        else:
            mask[gi, :L] = ((rel % dil == 0) & (np.abs(rel) <= hw)).astype(np.float32)
    ident = np.eye(128, dtype=np.float32)
    return oh, mask, ident


class Buf:
    __slots__ = ("name", "w", "r")

    def __init__(self, name):
        self.name = name
        self.w = None
        self.r = {}


class _Rec:
    def __init__(self):
        self.call = None

    def __getattr__(self, name):
        def f(*a, **k):
            self.call = (name, a, k)
            return self
        return f


class Sched:
    COMPUTE = ("pe", "act", "dve", "pool")

    def __init__(self, nc, stack):
        self.nc = nc
        self.stack = stack
        self.q = {k: [] for k in ("pe", "act", "dve", "pool", "sp")}
        self.sems = {}
        self.tick = {}
        self.seen = {k: {} for k in self.q}
        for k in self.COMPUTE:
            self._sem(k)
        self.bufs = []

    def _sem(self, key):
        if key not in self.sems:
            self.sems[key] = self.stack.enter_context(self.nc.semaphore("s_" + key))
            self.tick[key] = 0
        return self.sems[key]

    def buf(self, name):
        b = Buf(name)
        self.bufs.append(b)
        return b

    def op(self, eng, fn, reads=(), writes=(), dma=None):
        deps = {}
        for b in reads:
            if b.w is not None:
                k, t = b.w
                if deps.get(k, 0) < t:
                    deps[k] = t
        for b in writes:
            if b.w is not None:
                k, t = b.w
                if deps.get(k, 0) < t:
                    deps[k] = t
            for k, t in b.r.items():
                if deps.get(k, 0) < t:
                    deps[k] = t
        waits = []
        seen = self.seen[eng]
        for k, t in deps.items():
            if dma is None and k == eng:
                if eng == "pe" or t < self.tick[eng] - 1:
                    continue
                if seen.get(k, 0) >= t:
                    continue
            elif seen.get(k, 0) >= t:
                continue
            seen[k] = t
            waits.append((k, t))
        if dma is not None:
            self._sem(dma)
            self.tick[dma] += 16
            ev = (dma, self.tick[dma])
            inc = (dma, 16)
        else:
            self.tick[eng] += 1
            ev = (eng, self.tick[eng])
            inc = (eng, 1)
        rec = _Rec()
        fn(rec)
        assert rec.call is not None
        self.q[eng].append((waits, rec.call, inc))
        for b in writes:
            b.w = ev
            b.r = {}
        for b in reads:
            if b in writes:
                continue
            k, t = ev
            if b.r.get(k, 0) < t:
                b.r[k] = t
        return ev

    def barrier(self):
        for eng in self.q:
            waits = []
            seen = self.seen[eng]
            for k, t in self.tick.items():
                if t > 0 and seen.get(k, 0) < t and k != eng:
                    seen[k] = t
                    waits.append((k, t))
            if waits:
                self.q[eng].append((waits, None, None))
        for b in self.bufs:
            b.w = None
            b.r = {}

    def emit(self):
        nc = self.nc
        sems = self.sems

        def run(e, items):
            for waits, fn, inc in items:
                for k, t in waits:
                    e.wait_ge(sems[k], t)
                if fn is not None:
                    name, a, k = fn
                    getattr(e, name)(*a, **k).then_inc(sems[inc[0]], inc[1])

        with nc.Block() as block:
            @block.tensor
            def _(e):
                run(e, self.q["pe"])

            @block.scalar
            def _(e):
                run(e, self.q["act"])

            @block.vector
            def _(e):
                run(e, self.q["dve"])

            @block.gpsimd
            def _(e):
                run(e, self.q["pool"])

            @block.sync
            def _(e):
                run(e, self.q["sp"])


def build_program(seq_lens, depth, lam_inits):
    nc = bass.Bass("TRN2", target_bir_lowering=False)
    n_p = sum(1 for g, _, _ in seq_lens if g == "p")
    n_s = sum(1 for g, _, _ in seq_lens if g == "s")
    S_p = max([s for g, _, s in seq_lens if g == "p"], default=128)
    S_s = max([s for g, _, s in seq_lens if g == "s"], default=128)
    SMAX = max(s for _, _, s in seq_lens)
    L = depth

    def din(name, shape, dt=F32):
        return nc.dram_tensor(name, list(shape), dt, kind="ExternalInput").ap()

    def dscr(name, shape, dt=BF16):
        return nc.dram_tensor(name, list(shape), dt, kind="Internal").ap()

    xin = {"p": din("xp", [max(n_p, 1), S_p, D]), "s": din("xs", [max(n_s, 1), S_s, D])}
    yout = {"p": nc.dram_tensor("yp", [max(n_p, 1), S_p, D], F32, kind="ExternalOutput").ap(),
            "s": nc.dram_tensor("ys", [max(n_s, 1), S_s, D], F32, kind="ExternalOutput").ap()}
    rel_bias = din("rel_bias", [NBUCK, 10])
    norm1_g = din("norm1_g", [L, D]); norm2_g = din("norm2_g", [L, D])
    w_in = din("w_in", [L, D, INC])
    sgu_ln_g = din("sgu_ln_g", [L, 512]); sgu_ln_b = din("sgu_ln_b", [L, 512])
    sgu_w = din("sgu_w", [L, 8, 128, 128]); sgu_b = din("sgu_b", [L, 8, 128])
    qn_b = din("qn_b", [L, 64]); kn_b = din("kn_b", [L, 64]); qn_c = din("qn_c", [L, 64]); kn_c = din("kn_c", [L, 64])
    lam_q1 = din("lam_q1", [L, 64]); lam_k1 = din("lam_k1", [L, 64]); lam_q2 = din("lam_q2", [L, 64]); lam_k2 = din("lam_k2", [L, 64])
    subln_g = din("subln_g", [L, 128])
    w_pa = din("w_pa", [L, 512, D]); w_pb = din("w_pb", [L, 384, D]); w_pc = din("w_pc", [L, 512, D])
    w_o = din("w_o", [L, D, D]); w_gu = din("w_gu", [L, D, 2 * DFF]); w_down = din("w_down", [L, DFF, D])
    c_oh = din("c_oh", [4, NBUCK, LMAX]); c_mask = din("c_mask", [4, LMAX]); c_ident = din("c_ident", [128, 128])

    wb_in = dscr("wb_in", [L, D, INC]); wb_pa = dscr("wb_pa", [L, 512, D]); wb_pb = dscr("wb_pb", [L, 384, D])
    wb_pc = dscr("wb_pc", [L, 512, D]); wb_o = dscr("wb_o", [L, D, D]); wb_gu = dscr("wb_gu", [L, D, 2 * DFF])
    wb_dn = dscr("wb_dn", [L, DFF, D])
    rrep = dscr("rrep", [10, 128, LMAX]); strd = dscr("strd", [128, STRIP_COLS])
    NSEQ = len(seq_lens)
    QT = [dscr(f"QT{i}", [7, 128, s]) for i, (_, _, s) in enumerate(seq_lens)]
    KT = [dscr(f"KT{i}", [7, 128, s]) for i, (_, _, s) in enumerate(seq_lens)]
    VB = [dscr(f"VB{i}", [s // 128, 128, 390]) for i, (_, _, s) in enumerate(seq_lens)]
    VC = [dscr(f"VC{i}", [s // 128, 128, 516]) for i, (_, _, s) in enumerate(seq_lens)]
    BOT = [dscr(f"BOT{i}", [3, 128, s]) for i, (_, _, s) in enumerate(seq_lens)]
    COT = [dscr(f"COT{i}", [4, 128, s]) for i, (_, _, s) in enumerate(seq_lens)]

    with contextlib.ExitStack() as st:
        S = Sched(nc, st)
        op = S.op

        def sbt(name, shape, dt=F32):
            return st.enter_context(nc.sbuf_tensor(name, list(shape), dt))

        ident = sbt("ident", [128, 128], BF16); b_ident = S.buf("ident")
        identf = sbt("identf", [128, 128], F32)
        epsr = sbt("epsr", [128, 2], F32); b_eps = S.buf("eps")
        small = sbt("small", [128, 64], F32)
        lamt = sbt("lamt", [128, 8], F32); b_lam = S.buf("lam")
        ksc = sbt("ksc", [128, 4], F32); b_ksc = S.buf("ksc")
        RBYTES = 204 * 1024
        R = sbt("R", [128, RBYTES // 2], BF16)
        psb = [st.enter_context(nc.psum_tensor(f"ps{i}", [128, 512], F32)) for i in range(8)]
        b_ps = [S.buf(f"ps{i}") for i in range(8)]

        class Alloc:
            def __init__(self):
                self.off = 0

            def __call__(self, shape, dt):
                n = int(np.prod(shape))
                size = 2 if dt == BF16 else 4
                nb = (n * size + 63) // 64 * 64
                off = self.off
                self.off += nb
                assert self.off <= RBYTES, ("region overflow", self.off)
                a = R[:, off // 2: off // 2 + n * size // 2]
                if dt == F32:
                    a = a.bitcast(F32)
                if len(shape) == 2:
                    a = a.rearrange("p (a b) -> p a b", b=shape[1])
                elif len(shape) == 3:
                    a = a.rearrange("p (a b c) -> p a b c", b=shape[1], c=shape[2])
                elif len(shape) == 4:
                    a = a.rearrange("p (a b c d) -> p a b c d", b=shape[1], c=shape[2], d=shape[3])
                return a

        def bf(psap):
            return psap.bitcast(BF16)

        op("sp", lambda e: e.dma_start(out=identf[:], in_=c_ident[:, :]), writes=[b_ident], dma="d_ident")
        op("dve", lambda e: e.tensor_copy(out=ident[:], in_=identf[:]), reads=[b_ident], writes=[b_ident])
        op("dve", lambda e: e.memset(epsr[:, 0:1], RMS_EPS), writes=[b_eps])
        op("dve", lambda e: e.memset(epsr[:, 1:2], LN_EPS), writes=[b_eps])

        b_wb = {}
        for l in range(L):
            for nm, src, dst, rows in (("in", w_in, wb_in, D), ("pa", w_pa, wb_pa, 512), ("pb", w_pb, wb_pb, 384),
                                       ("pc", w_pc, wb_pc, 512), ("o", w_o, wb_o, D), ("gu", w_gu, wb_gu, D),
                                       ("dn", w_down, wb_dn, DFF)):
                b = S.buf(f"wb_{nm}{l}")
                b_wb[(nm, l)] = b
                for r0 in range(0, rows, 128):
                    op("pool", (lambda e, src=src, dst=dst, l=l, r0=r0: e.dma_start(out=dst[l, r0:r0 + 128, :], in_=src[l, r0:r0 + 128, :])),
                       writes=[b], dma="d_cv")

        al = Alloc()
        tab = al([10], F32)
        tabb = al([128], F32)
        oht = al([LMAX], F32)
        mkt = al([LMAX], F32)
        rv = al([LMAX], BF16)
        evt = al([512], F32); b_evt = S.buf('evt')
        b_tab = S.buf("tab"); b_tabb = S.buf("tabb"); b_oht = S.buf("oht"); b_mkt = S.buf("mkt"); b_rv = S.buf("rv")
        b_rrep = S.buf("rrep"); b_strd = S.buf("strd")
        op("sp", lambda e: e.dma_start(out=tab[0:NBUCK, :], in_=rel_bias[:, :]), writes=[b_tab], dma="d_tab")
        strip_heads = {"c": [6, 7, 8, 9], "b0": [0, 1], "b1": [2, 3], "b2": [4, 5]}
        ri = 0
        for gi, g in enumerate(GEO_ORDER):
            Lg = geo_L(g); Wg = geo_W(g)
            op("sp", (lambda e, gi=gi, Lg=Lg: e.dma_start(out=oht[0:NBUCK, 0:Lg], in_=c_oh[gi, :, 0:Lg])), writes=[b_oht], dma="d_oht")
            op("sp", (lambda e, gi=gi, Lg=Lg: e.dma_start(out=mkt[:, 0:Lg], in_=c_mask[gi:gi + 1, 0:Lg].partition_broadcast(128))), writes=[b_mkt], dma="d_misc2")
            for hi, hcol in enumerate(strip_heads[g]):
                op("dve", (lambda e, hcol=hcol: e.tensor_copy(out=tabb[0:NBUCK, :], in_=tab[0:NBUCK, hcol:hcol + 1].to_broadcast([NBUCK, 128]))),
                   reads=[b_tab], writes=[b_tabb])
                for c0 in range(0, Lg, 512):
                    c1 = min(Lg, c0 + 512)
                    op("pe", (lambda e, c0=c0, c1=c1: e.matmul(psb[0][:, 0:c1 - c0], tabb[0:NBUCK, :], oht[0:NBUCK, c0:c1], start=True, stop=True)),
                       reads=[b_tabb, b_oht], writes=[b_ps[0]])
                    op("act", (lambda e, c0=c0, c1=c1: e.activation(out=evt[:, 0:c1 - c0], in_=psb[0][:, 0:c1 - c0], func=AF.Exp)),
                       reads=[b_ps[0]], writes=[b_evt])
                    op("dve", (lambda e, c0=c0, c1=c1: e.tensor_tensor(out=rv[:, c0:c1], in0=evt[:, 0:c1 - c0], in1=mkt[:, c0:c1], op=ALU.mult)),
                       reads=[b_evt, b_mkt], writes=[b_rv])
                op("sp", (lambda e, ri=ri, Lg=Lg: e.dma_start(out=rrep[ri, :, 0:Lg], in_=rv[:, 0:Lg])), reads=[b_rv], writes=[b_rrep], dma="d_rrep")
                soff = STRIP_OFF[g] + hi * Wg
                src = bass.AP(rrep.tensor, ri * 128 * LMAX + 127, [[LMAX - 1, 128], [1, Wg]])
                op("sp", (lambda e, soff=soff, Wg=Wg, src=src: e.dma_start(out=strd[:, soff:soff + Wg], in_=src)),
                   reads=[b_rrep], writes=[b_strd], dma="d_strd")
                ri += 1
        S.barrier()

        for si, (grp, gidx, SL) in enumerate(seq_lens):
            NT = SL // 128
            NB = SL // TB
            for l in range(L):
                xsrc = xin[grp] if l == 0 else yout[grp]
                ydst = yout[grp]

                al = Alloc()
                xt = [al([D], F32) for _ in range(4)]
                hb = [al([D], BF16) for _ in range(2)]
                hT = al([8, TB], BF16)
                g1t = al([D], F32)
                wqkv = al([8, 2688], BF16)
                junk = al([D], F32)
                sq = [al([512], F32) for _ in range(2)]
                qn = [al([512], BF16) for _ in range(2)]
                qst = al([7, TB], BF16)
                kst = al([7, TB], BF16)
                vbst = al([4, 6, 65], BF16)
                vcst = al([4, 4, 129], BF16)
                svs = [al([8], F32), al([8], F32)]
                b_xt = [S.buf(f"xt{t}") for t in range(4)]
                b_hb = [S.buf(f"hb{t}") for t in range(2)]
                b_hT = S.buf("hT"); b_g1 = S.buf("g1"); b_wqkv = S.buf("wqkv"); b_junk = S.buf("junk")
                b_sq = [S.buf("sq0"), S.buf("sq1")]; b_qn = [S.buf("qn0"), S.buf("qn1")]
                b_qst = S.buf("qst"); b_kst = S.buf("kst"); b_vbst = S.buf("vbst"); b_vcst = S.buf("vcst")
                b_ss = [S.buf(f"ss{t}") for t in range(4)]
                b_qs = [S.buf("qs0"), S.buf("qs1")]
                b_QT = S.buf("QTd"); b_KT = S.buf("KTd"); b_VB = S.buf("VBd"); b_VC = S.buf("VCd")
                b_BOT = S.buf("BOTd"); b_COT = S.buf("COTd")

                op("sp", (lambda e, l=l: e.dma_start(out=g1t[:, :], in_=norm1_g[l:l + 1, :].partition_broadcast(128))), writes=[b_g1], dma="d_g1")
                op("sp", (lambda e, l=l: e.dma_start(out=wqkv[:, :, :], in_=wb_in[l, :, 1024:3712].rearrange("(c p) n -> p c n", p=128))),
                   reads=[b_wb[("in", l)]], writes=[b_wqkv], dma="d_wqkv")
                for half in range(2):
                    for j, src in enumerate((qn_b, kn_b, qn_c, kn_c)):
                        op("sp", (lambda e, l=l, half=half, j=j, src=src: e.dma_start(
                            out=lamt[half * 64:(half + 1) * 64, j:j + 1], in_=src[l:l + 1, :].rearrange("o d -> d o"))),
                           writes=[b_lam], dma="d_lam")
                op("dve", lambda e: e.tensor_tensor(out=ksc[:, 0:1], in0=lamt[:, 0:1], in1=lamt[:, 1:2], op=ALU.mult), reads=[b_lam], writes=[b_ksc])
                op("dve", lambda e: e.tensor_tensor(out=ksc[:, 1:2], in0=lamt[:, 2:3], in1=lamt[:, 3:4], op=ALU.mult), reads=[b_lam], writes=[b_ksc])
                op("dve", lambda e: e.tensor_scalar(out=ksc[:, 0:2], in0=ksc[:, 0:2], scalar1=0.125, scalar2=None, op0=ALU.mult), reads=[b_ksc], writes=[b_ksc])
                op("dve", lambda e: e.memset(vbst[:, :, :, 64:65], 1.0), writes=[b_vbst])
                op("dve", lambda e: e.memset(vcst[:, :, :, 128:129], 1.0), writes=[b_vcst])

                rot = [0]

                def nxt(n=6):
                    i = rot[0] % n
                    rot[0] += 1
                    return i

                def norm_tile(t, tok0, src_ap, gt, b_g, trbank):
                    hbi = t % 2
                    op("pool", (lambda e: e.dma_start(out=xt[t][:, :], in_=src_ap[tok0:tok0 + 128, :])), writes=[b_xt[t]], dma=f"d_x{t}")
                    op("act", (lambda e: e.activation(out=junk[:, :], in_=xt[t][:, :], func=AF.Square, accum_out=small[:, t:t + 1])),
                       reads=[b_xt[t]], writes=[b_junk, b_ss[t]])
                    op("act", (lambda e: e.activation(out=small[:, t:t + 1], in_=small[:, t:t + 1], func=AF.Sqrt, scale=1.0 / D, bias=epsr[:, 0:1])),
                       reads=[b_ss[t], b_eps], writes=[b_ss[t]])
                    op("dve", (lambda e: e.reciprocal(out=small[:, t:t + 1], in_=small[:, t:t + 1])), reads=[b_ss[t]], writes=[b_ss[t]])
                    op("dve", (lambda e: e.scalar_tensor_tensor(out=hb[hbi][:, :], in0=xt[t][:, :], scalar=small[:, t:t + 1], in1=gt[:, :],
                                                                 op0=ALU.mult, op1=ALU.mult)),
                       reads=[b_xt[t], b_ss[t], b_g], writes=[b_hb[hbi]])
                    pv = bf(psb[trbank][:, :])
                    for c in range(8):
                        op("pe", (lambda e, c=c: e.transpose(pv[:, c * 128:(c + 1) * 128], hb[hbi][:, c * 128:(c + 1) * 128], ident[:, :])),
                           reads=[b_hb[hbi], b_ident], writes=[b_ps[trbank]])
                    op("act", (lambda e: e.activation(out=hT[:, :, t * 128:(t + 1) * 128], in_=pv.rearrange("p (c q) -> p c q", q=128), func=AF.Copy)),
                       reads=[b_ps[trbank]], writes=[b_hT])

                blocks = [("q", 0, 3, 0, 0), ("k", 384, 3, 0, 0), ("v", 768, 3, 0, 0),
                          ("q", 1152, 4, 3, 1), ("k", 1664, 4, 3, 1), ("v", 2176, 4, 3, 1)]
                for tb in range(NB):
                    for t in range(4):
                        norm_tile(t, tb * TB + t * 128, xsrc[gidx], g1t, b_g1, 6 + (t % 2))
                    for t in range(4):
                        for bi, (kind, c0, nch, ch0, isc) in enumerate(blocks):
                            ncol = nch * 128
                            pb_i = nxt()
                            pso = psb[pb_i]
                            for k in range(8):
                                op("pe", (lambda e, k=k, pso=pso, c0=c0, ncol=ncol, t=t: e.matmul(
                                    pso[:, 0:ncol], hT[:, k, t * 128:(t + 1) * 128], wqkv[:, k, c0:c0 + ncol], start=(k == 0), stop=(k == 7))),
                                   reads=[b_hT, b_wqkv], writes=[b_ps[pb_i]])
                            if kind == "v":
                                if isc == 0:
                                    op("dve", (lambda e, pso=pso, t=t: e.tensor_copy(out=vbst[:, t, :, 0:64], in_=pso[:, 0:384].rearrange("p (h d) -> p h d", d=64))),
                                       reads=[b_ps[pb_i]], writes=[b_vbst])
                                else:
                                    op("act", (lambda e, pso=pso, t=t: e.activation(out=vcst[:, t, :, 0:128], in_=pso[:, 0:512].rearrange("p (h d) -> p h d", d=128), func=AF.Copy)),
                                       reads=[b_ps[pb_i]], writes=[b_vcst])
                                continue
                            nh = ncol // 64
                            j = bi % 2
                            sv = svs[j]
                            op("act", (lambda e, pso=pso, ncol=ncol, j=j: e.activation(out=sq[j][:, 0:ncol], in_=pso[:, 0:ncol], func=AF.Square)),
                               reads=[b_ps[pb_i]], writes=[b_sq[j]])
                            op("dve", (lambda e, ncol=ncol, j=j, sv=sv, nh=nh: e.tensor_reduce(out=sv[:, 0:nh], in_=sq[j][:, 0:ncol].rearrange("p (h d) -> p h d", d=64), axis=AX.X, op=ALU.add)),
                               reads=[b_sq[j]], writes=[b_qs[j]])
                            op("act", (lambda e, sv=sv, nh=nh: e.activation(out=sv[:, 0:nh], in_=sv[:, 0:nh], func=AF.Sqrt, scale=1.0 / 64, bias=epsr[:, 0:1])),
                               reads=[b_qs[j], b_eps], writes=[b_qs[j]])
                            op("dve", (lambda e, sv=sv, nh=nh: e.reciprocal(out=sv[:, 0:nh], in_=sv[:, 0:nh])), reads=[b_qs[j]], writes=[b_qs[j]])
                            if isc == 0:
                                op("dve", (lambda e, pso=pso, ncol=ncol, j=j, sv=sv, nh=nh: e.tensor_tensor(
                                    out=qn[j][:, 0:ncol].rearrange("p (h d) -> p h d", d=64), in0=pso[:, 0:ncol].rearrange("p (h d) -> p h d", d=64),
                                    in1=sv[:, 0:nh].unsqueeze(2).to_broadcast([128, nh, 64]), op=ALU.mult)),
                                   reads=[b_ps[pb_i], b_qs[j]], writes=[b_qn[j]])
                            else:
                                op("dve", (lambda e, pso=pso, j=j, sv=sv: e.tensor_tensor(
                                    out=qn[j][:, 0:512].rearrange("p (h m d) -> p m h d", h=4, m=2),
                                    in0=pso[:, 0:512].rearrange("p (m h d) -> p m h d", m=2, h=4),
                                    in1=sv[:, 0:8].rearrange("p (m h) -> p m h", m=2).unsqueeze(3).to_broadcast([128, 2, 4, 64]), op=ALU.mult)),
                                   reads=[b_ps[pb_i], b_qs[j]], writes=[b_qn[j]])
                            trb = 6 + (bi % 2)
                            pv = bf(psb[trb][:, :])
                            for c in range(nch):
                                src = qn[j][:, c * 128:(c + 1) * 128]
                                op("pe", (lambda e, c=c, src=src, pv=pv: e.transpose(pv[:, c * 128:(c + 1) * 128], src, ident[:, :])),
                                   reads=[b_qn[j], b_ident], writes=[b_ps[trb]])
                            dstt = qst if kind == "q" else kst
                            b_dst = b_qst if kind == "q" else b_kst
                            if kind == "q":
                                op("dve", (lambda e, pv=pv, nch=nch, ch0=ch0, t=t, dstt=dstt: e.tensor_copy(
                                    out=dstt[:, ch0:ch0 + nch, t * 128:(t + 1) * 128], in_=pv[:, 0:nch * 128].rearrange("p (c q) -> p c q", q=128))),
                                   reads=[b_ps[trb]], writes=[b_dst])
                            else:
                                op("act", (lambda e, pv=pv, nch=nch, ch0=ch0, t=t, dstt=dstt, isc=isc: e.activation(
                                    out=dstt[:, ch0:ch0 + nch, t * 128:(t + 1) * 128], in_=pv[:, 0:nch * 128].rearrange("p (c q) -> p c q", q=128),
                                    func=AF.Copy, scale=ksc[:, isc:isc + 1])),
                                   reads=[b_ps[trb], b_ksc], writes=[b_dst])
                    s0 = tb * TB
                    op("pool", (lambda e, s0=s0: e.dma_start(out=QT[si][:, :, s0:s0 + TB].rearrange("c p s -> p c s"), in_=qst[:, :, :])), reads=[b_qst], writes=[b_QT], dma="d_qst")
                    op("pool", (lambda e, s0=s0: e.dma_start(out=KT[si][:, :, s0:s0 + TB].rearrange("c p s -> p c s"), in_=kst[:, :, :])), reads=[b_kst], writes=[b_KT], dma="d_kst")
                    op("pool", (lambda e, tb=tb: e.dma_start(out=VB[si][tb * 4:(tb + 1) * 4, :, :].rearrange("t p c -> p t c"), in_=vbst[:, :, :, :].rearrange("p t h d -> p t (h d)"))),
                       reads=[b_vbst], writes=[b_VB], dma="d_vbst")
                    op("pool", (lambda e, tb=tb: e.dma_start(out=VC[si][tb * 4:(tb + 1) * 4, :, :].rearrange("t p c -> p t c"), in_=vcst[:, :, :, :].rearrange("p t h d -> p t (h d)"))),
                       reads=[b_vcst], writes=[b_VC], dma="d_vcst")
                S.barrier()

                al = Alloc()
                strips = al([STRIP_COLS], BF16)
                vcall = al([NT, 516], BF16)
                slotA = al([3 * SL], BF16)
                slotB = al([3 * SL], BF16)
                slotC = al([NT * 390], BF16)
                pt = [al([512], BF16) for _ in range(4)]
                stage_b = al([4, 6, 65], F32)
                stage_c = al([4, 2, 129], F32)
                o_t = al([4, 128], F32); t_t = al([4, 128], F32); q_t = al([4, 128], F32)
                bo_tm = al([4, 384], BF16); co_tm = al([4, 128], BF16)
                boT_st = al([3, TB], BF16); coT_st = al([TB], BF16)
                sgt = al([128], F32)
                zz = al([64], F32)
                b_strips = S.buf("strips"); b_vcall = S.buf("vcall")
                b_slot = [S.buf("slotA"), S.buf("slotB"), S.buf("slotC")]
                b_pt = [S.buf(f"pt{i}") for i in range(4)]
                b_stb = S.buf("stage_b"); b_stc = S.buf("stage_c")
                b_ot = S.buf("o_t"); b_tt = S.buf("t_t"); b_qt = S.buf("q_t")
                b_botm = S.buf("bo_tm"); b_cotm = S.buf("co_tm"); b_boT = S.buf("boT_st"); b_coT = S.buf("coT_st")
                b_sgt = S.buf("sgt"); b_zz = S.buf("zz")
                slots = [slotA, slotB, slotC]

                op("sp", lambda e: e.dma_start(out=strips[:, :], in_=strd[:, :]), reads=[b_strd], writes=[b_strips], dma="d_strips")
                for t0 in range(0, NT, 8):
                    op("sp", (lambda e, t0=t0: e.dma_start(out=vcall[:, t0:min(NT, t0 + 8), :], in_=VC[si][t0:min(NT, t0 + 8), :, :].rearrange("t p c -> p t c"))),
                       reads=[b_VC], writes=[b_vcall], dma="d_vcall")
                for j, src in enumerate((lam_q1, lam_k1, lam_q2, lam_k2)):
                    op("sp", (lambda e, l=l, j=j, src=src: e.dma_start(out=o_t[:, j, 0:64], in_=src[l:l + 1, :].partition_broadcast(128))),
                       writes=[b_ot], dma="d_lamv")
                op("dve", lambda e: e.tensor_tensor(out=t_t[:, 0, 0:64], in0=o_t[:, 0, 0:64], in1=o_t[:, 1, 0:64], op=ALU.mult), reads=[b_ot], writes=[b_tt])
                op("dve", lambda e: e.tensor_tensor(out=t_t[:, 1, 0:64], in0=o_t[:, 2, 0:64], in1=o_t[:, 3, 0:64], op=ALU.mult), reads=[b_ot], writes=[b_tt])
                op("dve", lambda e: e.tensor_reduce(out=zz[:, 0:2], in_=t_t[:, 0:2, 0:64], axis=AX.X, op=ALU.add), reads=[b_tt], writes=[b_zz])
                op("act", lambda e: e.activation(out=zz[:, 2:4], in_=zz[:, 0:2], func=AF.Exp), reads=[b_zz], writes=[b_zz])
                li = float(lam_inits[l])
                op("dve", lambda e: e.tensor_tensor(out=lamt[:, 4:5], in0=zz[:, 3:4], in1=zz[:, 2:3], op=ALU.subtract), reads=[b_zz], writes=[b_lam])
                op("dve", lambda e: e.tensor_scalar(out=lamt[:, 4:5], in0=lamt[:, 4:5], scalar1=-li, scalar2=None, op0=ALU.add), reads=[b_lam], writes=[b_lam])
                op("sp", (lambda e, l=l: e.dma_start(out=sgt[:, :], in_=subln_g[l:l + 1, :].partition_broadcast(128))), writes=[b_sgt], dma="d_sgt")
                op("dve", lambda e: e.tensor_scalar(out=sgt[:, :], in0=sgt[:, :], scalar1=1.0 - li, scalar2=None, op0=ALU.mult), reads=[b_sgt], writes=[b_sgt])

                srot = [0]
                prot = [0]
                arot = [0]

                def score_tile(lhsT, rhs, strip_win, pvs, extra_reads):
                    sb_i = 4 + (srot[0] % 3); srot[0] += 1
                    pi = prot[0] % 4; prot[0] += 1
                    op("pe", (lambda e: e.matmul(psb[sb_i][:, :], lhsT, rhs, start=True, stop=True)), reads=extra_reads, writes=[b_ps[sb_i]])
                    op("act", (lambda e: e.activation(out=pt[pi][:, :], in_=psb[sb_i][:, :], func=AF.Exp)), reads=[b_ps[sb_i]], writes=[b_pt[pi]])
                    op("dve", (lambda e: e.tensor_tensor(out=pt[pi][:, :], in0=pt[pi][:, :], in1=strip_win, op=ALU.mult)), reads=[b_pt[pi], b_strips], writes=[b_pt[pi]])
                    for (oap, bi_, qs, rap, stt, stp) in pvs:
                        op("pe", (lambda e, oap=oap, qs=qs, rap=rap, stt=stt, stp=stp: e.matmul(oap, pt[pi][:, qs * 128:(qs + 1) * 128], rap, start=stt, stop=stp, skip_group_check=True)),
                           reads=[b_pt[pi]] + extra_reads, writes=[b_ps[bi_]])

                ktb = slotA.rearrange("p (c s) -> p c s", s=SL)
                qtb = slotB.rearrange("p (c s) -> p c s", s=SL)
                vbt = slotC.rearrange("p (t c) -> p t c", c=390)
                op("sp", lambda e: e.dma_start(out=ktb, in_=KT[si][0:3, :, :].rearrange("c p s -> p c s")), reads=[b_KT], writes=[b_slot[0]], dma="d_slot0")
                op("sp", lambda e: e.dma_start(out=qtb, in_=QT[si][0:3, :, :].rearrange("c p s -> p c s")), reads=[b_QT], writes=[b_slot[1]], dma="d_slot1")
                for t0 in range(0, NT, 8):
                    op("sp", (lambda e, t0=t0: e.dma_start(out=vbt[:, t0:min(NT, t0 + 8), :], in_=VB[si][t0:min(NT, t0 + 8), :, :].rearrange("t p c -> p t c"))),
                       reads=[b_VB], writes=[b_slot[2]], dma="d_slot2")
                for qb in range(NB):
                    for g in range(3):
                        gname = f"b{g}"
                        dmin, dmax, _, _ = GEO[gname]
                        Wg = geo_W(gname)
                        kts = [kt for kt in range(NT) if dmin <= kt * 128 - qb * TB <= dmax]
                        for hp in range(2):
                            m = g * 2 + hp
                            ab = arot[0] % 4; arot[0] += 1
                            soff = STRIP_OFF[gname] + hp * Wg
                            for i, kt in enumerate(kts):
                                c0 = dmax - (kt * 128 - qb * TB)
                                pvs = [(psb[ab][:, qs * 65:(qs + 1) * 65], ab, qs, vbt[:, kt, m * 65:(m + 1) * 65],
                                        (i == 0 and qs == 0), (i == len(kts) - 1)) for qs in range(4)]
                                score_tile(ktb[hp * 64:(hp + 1) * 64, g, kt * 128:(kt + 1) * 128], qtb[hp * 64:(hp + 1) * 64, g, qb * TB:(qb + 1) * TB],
                                           strips[:, soff + c0:soff + c0 + 512], pvs, [b_slot[0], b_slot[1], b_slot[2]])
                            op("act", (lambda e, ab=ab, m=m: e.activation(out=stage_b[:, :, m, :], in_=psb[ab][:, 0:260].rearrange("p (q c) -> p q c", c=65), func=AF.Copy)),
                               reads=[b_ps[ab]], writes=[b_stb])
                    zv = stage_b[:, :, :, 64]
                    op("dve", lambda e: e.tensor_tensor(out=zz[:, 0:8].rearrange("p (q h) -> p q h", h=2), in0=zv[:, :, 0:2], in1=zv[:, :, 2:4], op=ALU.add), reads=[b_stb], writes=[b_zz])
                    op("dve", lambda e: e.tensor_tensor(out=zz[:, 0:8].rearrange("p (q h) -> p q h", h=2), in0=zz[:, 0:8].rearrange("p (q h) -> p q h", h=2), in1=zv[:, :, 4:6], op=ALU.add), reads=[b_stb, b_zz], writes=[b_zz])
                    op("dve", lambda e: e.reciprocal(out=zz[:, 8:16], in_=zz[:, 0:8]), reads=[b_zz], writes=[b_zz])
                    for g in range(3):
                        op("dve", (lambda e, g=g: e.tensor_tensor(
                            out=bo_tm[:, :, g * 128:(g + 1) * 128].rearrange("p q (h d) -> p q h d", d=64),
                            in0=stage_b[:, :, 2 * g:2 * g + 2, 0:64],
                            in1=zz[:, 8:16].rearrange("p (q h) -> p q h", h=2).unsqueeze(3).to_broadcast([128, 4, 2, 64]), op=ALU.mult)),
                           reads=[b_stb, b_zz], writes=[b_botm])
                    pv = bf(psb[7][:, :])
                    for qs in range(4):
                        for c in range(3):
                            op("pe", (lambda e, qs=qs, c=c: e.transpose(pv[:, c * 128:(c + 1) * 128], bo_tm[:, qs, c * 128:(c + 1) * 128], ident[:, :])),
                               reads=[b_botm, b_ident], writes=[b_ps[7]])
                        op("act", (lambda e, qs=qs: e.activation(out=boT_st[:, :, qs * 128:(qs + 1) * 128], in_=pv[:, 0:384].rearrange("p (c q) -> p c q", q=128), func=AF.Copy)),
                           reads=[b_ps[7]], writes=[b_boT])
                    op("pool", (lambda e, qb=qb: e.dma_start(out=BOT[si][:, :, qb * TB:(qb + 1) * TB].rearrange("c p s -> p c s"), in_=boT_st[:, :, :])),
                       reads=[b_boT], writes=[b_BOT], dma="d_boT")

                dmin, dmax, _, _ = GEO["c"]
                Wc = geo_W("c")
                for h in range(4):
                    sl = slots[h % 3]
                    bsl = b_slot[h % 3]
                    ktc = sl[:, 0:SL]
                    qtc = sl[:, SL:2 * SL]
                    op("sp", (lambda e, h=h, ktc=ktc: e.dma_start(out=ktc, in_=KT[si][3 + h, :, :])), reads=[b_KT], writes=[bsl], dma=f"d_slot{h % 3}")
                    op("sp", (lambda e, h=h, qtc=qtc: e.dma_start(out=qtc, in_=QT[si][3 + h, :, :])), reads=[b_QT], writes=[bsl], dma=f"d_slot{h % 3}")
                    soff = STRIP_OFF["c"] + h * Wc
                    for qb in range(NB):
                        for mp in range(2):
                            a0 = (arot[0] % 2) * 2; arot[0] += 1
                            banks = (a0, a0 + 1)
                            for kt in range(NT):
                                dl = min(max(kt * 128 - qb * TB, dmin), dmax)
                                c0 = dmax - dl
                                pvs = []
                                for qs in range(4):
                                    bk = banks[qs // 2]
                                    col = (qs % 2) * 129
                                    pvs.append((psb[bk][:, col:col + 129], bk, qs, vcall[:, kt, h * 129:(h + 1) * 129], (kt == 0 and qs % 2 == 0), (kt == NT - 1)))
                                score_tile(ktc[mp * 64:(mp + 1) * 64, kt * 128:(kt + 1) * 128], qtc[mp * 64:(mp + 1) * 64, qb * TB:(qb + 1) * TB],
                                           strips[:, soff + c0:soff + c0 + 512], pvs, [bsl, b_vcall])
                            for hf in range(2):
                                bk = banks[hf]
                                op("act", (lambda e, bk=bk, hf=hf, mp=mp: e.activation(out=stage_c[:, 2 * hf:2 * hf + 2, mp, :], in_=psb[bk][:, 0:258].rearrange("p (q c) -> p q c", c=129), func=AF.Copy)),
                                   reads=[b_ps[bk]], writes=[b_stc])
                        rz = zz[:, 16:24].rearrange("p (q m) -> p q m", m=2)
                        op("dve", lambda e: e.reciprocal(out=rz, in_=stage_c[:, :, :, 128]), reads=[b_stc], writes=[b_zz])
                        op("dve", lambda e: e.tensor_scalar(out=zz[:, 24:28], in0=rz[:, :, 1], scalar1=lamt[:, 4:5], scalar2=None, op0=ALU.mult), reads=[b_zz, b_lam], writes=[b_zz])
                        op("dve", lambda e: e.tensor_tensor(out=o_t[:, :, :], in0=stage_c[:, :, 0, 0:128], in1=rz[:, :, 0].unsqueeze(2).to_broadcast([128, 4, 128]), op=ALU.mult),
                           reads=[b_stc, b_zz], writes=[b_ot])
                        op("dve", lambda e: e.tensor_tensor(out=t_t[:, :, :], in0=stage_c[:, :, 1, 0:128], in1=zz[:, 24:28].unsqueeze(2).to_broadcast([128, 4, 128]), op=ALU.mult),
                           reads=[b_stc, b_zz], writes=[b_tt])
                        op("dve", lambda e: e.tensor_tensor(out=o_t[:, :, :], in0=o_t[:, :, :], in1=t_t[:, :, :], op=ALU.add), reads=[b_ot, b_tt], writes=[b_ot])
                        op("act", lambda e: e.activation(out=q_t[:, :, :], in_=o_t[:, :, :], func=AF.Square), reads=[b_ot], writes=[b_qt])
                        op("dve", lambda e: e.tensor_reduce(out=zz[:, 28:32], in_=q_t[:, :, :], axis=AX.X, op=ALU.add), reads=[b_qt], writes=[b_zz])
                        op("act", lambda e: e.activation(out=zz[:, 28:32], in_=zz[:, 28:32], func=AF.Sqrt, scale=1.0 / 128, bias=epsr[:, 0:1]), reads=[b_zz, b_eps], writes=[b_zz])
                        op("dve", lambda e: e.reciprocal(out=zz[:, 28:32], in_=zz[:, 28:32]), reads=[b_zz], writes=[b_zz])
                        op("dve", lambda e: e.tensor_tensor(out=o_t[:, :, :], in0=o_t[:, :, :], in1=zz[:, 28:32].unsqueeze(2).to_broadcast([128, 4, 128]), op=ALU.mult),
                           reads=[b_ot, b_zz], writes=[b_ot])
                        op("dve", lambda e: e.tensor_tensor(out=co_tm[:, :, :], in0=o_t[:, :, :], in1=sgt[:, :].unsqueeze(1).to_broadcast([128, 4, 128]), op=ALU.mult),
                           reads=[b_ot, b_sgt], writes=[b_cotm])
                        pv = bf(psb[7][:, :])
                        for qs in range(4):
                            op("pe", (lambda e, qs=qs: e.transpose(pv[:, qs * 128:(qs + 1) * 128], co_tm[:, qs, :], ident[:, :])), reads=[b_cotm, b_ident], writes=[b_ps[7]])
                        op("act", lambda e: e.activation(out=coT_st[:, :], in_=pv[:, 0:512], func=AF.Copy), reads=[b_ps[7]], writes=[b_coT])
                        op("pool", (lambda e, h=h, qb=qb: e.dma_start(out=COT[si][h, :, qb * TB:(qb + 1) * TB], in_=coT_st[:, :])), reads=[b_coT], writes=[b_COT], dma="d_coT")
                S.barrier()

                al = Alloc()
                xt = [al([D], F32) for _ in range(4)]
                hb = [al([D], BF16) for _ in range(2)]
                hT = al([8, TB], BF16)
                g1t = al([D], F32)
                g2t = al([D], F32)
                junk = al([D], F32)
                lng = al([512], F32); lnb = al([512], F32)
                sgb = al([4, 128], F32)
                wgn = al([8, 128], F32)
                wgnb = al([8, 128], BF16)
                wgT = al([8, 128], BF16)
                gv = [al([512], F32) for _ in range(2)]
                vn = al([4, 512], BF16)
                uT = al([4, TB], BF16)
                tmpa = [al([TB], F32) for _ in range(2)]
                aT = al([4, TB], BF16)
                bcT = al([7, TB], BF16)
                sg = [al([TB], F32) for _ in range(3)]
                mm_ = [al([TB], F32) for _ in range(2)]
                mT = al([8, TB], BF16)
                sgf = [al([TB], BF16) for _ in range(3)]
                actT = al([22, TB], BF16)
                bst = al([8], F32)
                ring_n = 5
                SLOTB = 11 * 512
                ring = [al([SLOTB], BF16) for _ in range(ring_n)]
                b_xt = [S.buf(f"xt{t}") for t in range(4)]
                b_hb = [S.buf(f"hb{t}") for t in range(2)]
                b_hT = S.buf("hT"); b_g1 = S.buf("g1"); b_g2 = S.buf("g2"); b_junk = S.buf("junk")
                b_ss = [S.buf(f"ss{t}") for t in range(4)]
                b_ln = S.buf("ln"); b_sgb = S.buf("sgb"); b_wg = S.buf("wg"); b_wgT = S.buf("wgT")
                b_gv = [S.buf("gv0"), S.buf("gv1")]; b_vn = [S.buf(f"vn{t}") for t in range(4)]; b_uT = S.buf("uT")
                b_tmpa = [S.buf("tmpa0"), S.buf("tmpa1")]; b_aT = S.buf("aT"); b_bcT = S.buf("bcT")
                b_sg = [S.buf(f"sg{i}") for i in range(3)]; b_mm = [S.buf("mm0"), S.buf("mm1")]; b_mT = S.buf("mT")
                b_sgf = [S.buf(f"sgf{i}") for i in range(3)]; b_actT = S.buf("actT"); b_bst = [S.buf(f"bst{t}") for t in range(4)]
                b_ring = [S.buf(f"ring{i}") for i in range(ring_n)]
                b_BOT = S.buf("BOTd2"); b_COT = S.buf("COTd2")

                op("sp", (lambda e, l=l: e.dma_start(out=g1t[:, :], in_=norm1_g[l:l + 1, :].partition_broadcast(128))), writes=[b_g1], dma="d_g1")
                op("sp", (lambda e, l=l: e.dma_start(out=g2t[:, :], in_=norm2_g[l:l + 1, :].partition_broadcast(128))), writes=[b_g2], dma="d_g2")
                op("sp", (lambda e, l=l: e.dma_start(out=lng[:, :], in_=sgu_ln_g[l:l + 1, :].partition_broadcast(128))), writes=[b_ln], dma="d_ln")
                op("sp", (lambda e, l=l: e.dma_start(out=lnb[:, :], in_=sgu_ln_b[l:l + 1, :].partition_broadcast(128))), writes=[b_ln], dma="d_ln")
                for par in range(2):
                    src = bass.AP(sgu_b.tensor, l * 8 * 128 + par * 128, [[0, 64], [256, 4], [1, 128]])
                    op("sp", (lambda e, par=par, src=src: e.dma_start(out=sgb[par * 64:(par + 1) * 64, :, :], in_=src)), writes=[b_sgb], dma="d_sgb")
                op("sp", (lambda e, l=l: e.dma_start(out=wgn[:, :, :], in_=sgu_w[l, :, :, :].rearrange("g p q -> p g q"))), writes=[b_wg], dma="d_wgn")
                op("dve", lambda e: e.tensor_copy(out=wgnb[:, :, :], in_=wgn[:, :, :]), reads=[b_wg], writes=[b_wg])
                pv = bf(psb[7][:, :])
                for g in range(8):
                    op("pe", (lambda e, g=g: e.transpose(pv[:, g * 128:(g + 1) * 128], wgnb[:, g, :], ident[:, :])), reads=[b_wg, b_ident], writes=[b_ps[7]])
                op("dve", lambda e: e.tensor_copy(out=wgT[:, :, :], in_=pv.rearrange("p (g q) -> p g q", q=128)), reads=[b_ps[7]], writes=[b_wgT])

                pieces = []
                for tb in range(NB):
                    pieces.append([(("in", l), 8, 512, lambda l=l: wb_in[l, :, 512:1024], 0)])
                    pieces.append([(("in", l), 8, 512, lambda l=l: wb_in[l, :, 0:512], 0)])
                    for hf in range(2):
                        c0 = hf * 512
                        pieces.append([(("pa", l), 4, 512, lambda l=l, c0=c0: wb_pa[l, :, c0:c0 + 512], 0),
                                       (("pb", l), 3, 512, lambda l=l, c0=c0: wb_pb[l, :, c0:c0 + 512], 4),
                                       (("pc", l), 4, 512, lambda l=l, c0=c0: wb_pc[l, :, c0:c0 + 512], 7)])
                        for gi in range(3):
                            cc = 3712 + gi * 1024 + c0
                            pieces.append([(("in", l), 8, 512, lambda l=l, cc=cc: wb_in[l, :, cc:cc + 512], 0)])
                    for hf in range(2):
                        pieces.append([(("o", l), 8, 512, lambda l=l, hf=hf: wb_o[l, :, hf * 512:(hf + 1) * 512], 0)])
                    for f in range(11):
                        pieces.append([(("gu", l), 8, 256, lambda l=l, f=f: wb_gu[l, :, f * 256:(f + 1) * 256], 0),
                                       (("gu", l), 8, 256, lambda l=l, f=f: wb_gu[l, :, DFF + f * 256:DFF + (f + 1) * 256], 8)])
                    for hf in range(2):
                        for kh in range(2):
                            pieces.append([(("dn", l), 11, 512, lambda l=l, hf=hf, kh=kh: wb_dn[l, kh * 1408:(kh + 1) * 1408, hf * 512:(hf + 1) * 512], 0)])
                wstate = {"next_load": 0, "next_acq": 0, "held": []}

                def w_issue():
                    i = wstate["next_load"]
                    if i >= len(pieces):
                        return
                    slot = i % ring_n
                    for (wkey, nk, ncol, srcf, k0) in pieces[i]:
                        dst = ring[slot][:, k0 * ncol:(k0 + nk) * ncol].rearrange("p (k n) -> p k n", n=ncol)
                        src = srcf().rearrange("(k p) n -> p k n", p=128)
                        op("sp", (lambda e, dst=dst, src=src: e.dma_start(out=dst, in_=src)), reads=[b_wb[wkey]], writes=[b_ring[slot]], dma=f"d_ring{slot}")
                    wstate["next_load"] += 1

                def w_acquire():
                    i = wstate["next_acq"]
                    wstate["next_acq"] += 1
                    assert i < wstate["next_load"], "weight ring underflow"
                    slot = i % ring_n
                    return ring[slot], b_ring[slot]

                def w_release():
                    w_issue()

                for _ in range(ring_n):
                    w_issue()

                for tb in range(NB):
                    s0 = tb * TB
                    for t in range(4):
                        norm_tile(t, s0 + t * 128, xsrc[gidx], g1t, b_g1, 6 + (t % 2))
                    op("pool", (lambda e, s0=s0: e.dma_start(out=bcT[:, 0:3, :], in_=BOT[si][:, :, s0:s0 + TB].rearrange("c p s -> p c s"))), reads=[b_BOT], writes=[b_bcT], dma="d_bcT")
                    op("pool", (lambda e, s0=s0: e.dma_start(out=bcT[:, 3:7, :], in_=COT[si][:, :, s0:s0 + TB].rearrange("c p s -> p c s"))), reads=[b_COT], writes=[b_bcT], dma="d_bcT")
                    wv, bwv = w_acquire()
                    wv3 = wv[:, 0:8 * 512].rearrange("p (k n) -> p k n", n=512)
                    for t in range(4):
                        pb_i = nxt()
                        for k in range(8):
                            op("pe", (lambda e, k=k, t=t, pb_i=pb_i: e.matmul(psb[pb_i][:, :], hT[:, k, t * 128:(t + 1) * 128], wv3[:, k, :], start=(k == 0), stop=(k == 7))),
                               reads=[b_hT, bwv], writes=[b_ps[pb_i]])
                        j = t % 2
                        op("act", (lambda e, pb_i=pb_i, j=j: e.activation(out=gv[j][:, :], in_=psb[pb_i][:, :], func=AF.Gelu_apprx_tanh)), reads=[b_ps[pb_i]], writes=[b_gv[j]])
                        op("dve", (lambda e, j=j, t=t: e.bn_stats(out=bst[:, 0:6], in_=gv[j][:, :])), reads=[b_gv[j]], writes=[b_bst[0]])
                        op("dve", (lambda e, t=t: e.bn_aggr(out=small[:, 8 + 2 * t:10 + 2 * t], in_=bst[:, 0:6])), reads=[b_bst[0]], writes=[b_bst[1]])
                        op("act", (lambda e, t=t: e.activation(out=small[:, 9 + 2 * t:10 + 2 * t], in_=small[:, 9 + 2 * t:10 + 2 * t], func=AF.Sqrt, scale=1.0, bias=epsr[:, 1:2])),
                           reads=[b_bst[1], b_eps], writes=[b_bst[1]])
                        op("dve", (lambda e, t=t: e.reciprocal(out=small[:, 9 + 2 * t:10 + 2 * t], in_=small[:, 9 + 2 * t:10 + 2 * t])), reads=[b_bst[1]], writes=[b_bst[1]])
                        op("dve", (lambda e, j=j, t=t: e.tensor_scalar(out=gv[j][:, :], in0=gv[j][:, :], scalar1=small[:, 8 + 2 * t:9 + 2 * t], scalar2=small[:, 9 + 2 * t:10 + 2 * t],
                                                                    op0=ALU.subtract, op1=ALU.mult)), reads=[b_gv[j], b_bst[1]], writes=[b_gv[j]])
                        op("dve", (lambda e, j=j: e.tensor_tensor(out=gv[j][:, :], in0=gv[j][:, :], in1=lng[:, :], op=ALU.mult)), reads=[b_gv[j], b_ln], writes=[b_gv[j]])
                        op("dve", (lambda e, j=j, t=t: e.tensor_tensor(out=vn[:, t, :], in0=gv[j][:, :], in1=lnb[:, :], op=ALU.add)), reads=[b_gv[j], b_ln], writes=[b_vn[t]])
                    w_release()
                    wu, bwu = w_acquire()
                    wu3 = wu[:, 0:8 * 512].rearrange("p (k n) -> p k n", n=512)
                    for c in range(4):
                        pb_i = nxt()
                        for k in range(8):
                            op("pe", (lambda e, k=k, c=c, pb_i=pb_i: e.matmul(psb[pb_i][:, :], wu3[:, k, c * 128:(c + 1) * 128], hT[:, k, :], start=(k == 0), stop=(k == 7))),
                               reads=[b_hT, bwu], writes=[b_ps[pb_i]])
                        op("act", (lambda e, c=c, pb_i=pb_i: e.activation(out=uT[:, c, :], in_=psb[pb_i][:, :], func=AF.Gelu_apprx_tanh)), reads=[b_ps[pb_i]], writes=[b_uT])
                    w_release()
                    for j in range(4):
                        pa_i = nxt(); pb_i = nxt()
                        for t in range(4):
                            op("pe", (lambda e, j=j, t=t, pa_i=pa_i: e.matmul(psb[pa_i][:, t * 128:(t + 1) * 128], vn[:, t, j * 128:(j + 1) * 128], wgT[:, 2 * j, :], start=True, stop=True)),
                               reads=[b_vn[t], b_wgT], writes=[b_ps[pa_i]])
                        for t in range(4):
                            op("pe", (lambda e, j=j, t=t, pb_i=pb_i: e.matmul(psb[pb_i][:, t * 128:(t + 1) * 128], vn[:, t, j * 128:(j + 1) * 128], wgT[:, 2 * j + 1, :], start=True, stop=True)),
                               reads=[b_vn[t], b_wgT], writes=[b_ps[pb_i]])
                        jj = j % 2
                        op("dve", (lambda e, j=j, jj=jj, pa_i=pa_i: e.tensor_tensor(out=tmpa[jj][0:64, :].rearrange("p (t q) -> p t q", q=128),
                                                                              in0=psb[pa_i][0:64, :].rearrange("p (t q) -> p t q", q=128),
                                                                              in1=sgb[0:64, j, :].unsqueeze(1).to_broadcast([64, 4, 128]), op=ALU.add)),
                           reads=[b_ps[pa_i], b_sgb], writes=[b_tmpa[jj]])
                        op("dve", (lambda e, j=j, jj=jj, pb_i=pb_i: e.tensor_tensor(out=tmpa[jj][64:128, :].rearrange("p (t q) -> p t q", q=128),
                                                                              in0=psb[pb_i][64:128, :].rearrange("p (t q) -> p t q", q=128),
                                                                              in1=sgb[64:128, j, :].unsqueeze(1).to_broadcast([64, 4, 128]), op=ALU.add)),
                           reads=[b_ps[pb_i], b_sgb], writes=[b_tmpa[jj]])
                        op("dve", (lambda e, j=j, jj=jj: e.tensor_tensor(out=aT[:, j, :], in0=tmpa[jj][:, :], in1=uT[:, j, :], op=ALU.mult)), reads=[b_tmpa[jj], b_uT], writes=[b_aT])
                    for hf in range(2):
                        wp, bwp = w_acquire()
                        wp3 = wp[:, 0:11 * 512].rearrange("p (k n) -> p k n", n=512)
                        wg_ = []
                        for gi in range(3):
                            w_, bw_ = w_acquire()
                            wg_.append((w_[:, 0:8 * 512].rearrange("p (k n) -> p k n", n=512), bw_))
                        for oc in range(4):
                            cs = slice(oc * 128, (oc + 1) * 128)
                            gbanks = []
                            for gi in range(3):
                                pg = nxt()
                                gbanks.append(pg)
                                w3, bw3 = wg_[gi]
                                for k in range(8):
                                    op("pe", (lambda e, k=k, pg=pg, w3=w3, cs=cs: e.matmul(psb[pg][:, :], w3[:, k, cs], hT[:, k, :], start=(k == 0), stop=(k == 7))),
                                       reads=[b_hT, bw3], writes=[b_ps[pg]])
                                op("act", (lambda e, gi=gi, pg=pg: e.activation(out=sg[gi][:, :], in_=psb[pg][:, :], func=AF.Sigmoid)), reads=[b_ps[pg]], writes=[b_sg[gi]])
                            pbanks = []
                            for bi_, (k0, nk, srcT, koff, bsrc) in enumerate(((0, 4, aT, 0, b_aT), (4, 3, bcT, 0, b_bcT), (7, 4, bcT, 3, b_bcT))):
                                pp = nxt()
                                pbanks.append(pp)
                                for k in range(nk):
                                    op("pe", (lambda e, k=k, pp=pp, k0=k0, nk=nk, srcT=srcT, koff=koff, cs=cs: e.matmul(
                                        psb[pp][:, :], wp3[:, k0 + k, cs], srcT[:, koff + k, :], start=(k == 0), stop=(k == nk - 1))),
                                       reads=[bsrc, bwp], writes=[b_ps[pp]])
                            op("dve", (lambda e, pbanks=pbanks: e.tensor_tensor(out=mm_[0][:, :], in0=psb[pbanks[0]][:, :], in1=sg[0][:, :], op=ALU.mult)),
                               reads=[b_ps[pbanks[0]], b_sg[0]], writes=[b_mm[0]])
                            op("dve", (lambda e, pbanks=pbanks: e.tensor_tensor(out=mm_[1][:, :], in0=psb[pbanks[1]][:, :], in1=sg[1][:, :], op=ALU.mult)),
                               reads=[b_ps[pbanks[1]], b_sg[1]], writes=[b_mm[1]])
                            op("dve", lambda e: e.tensor_tensor(out=mm_[0][:, :], in0=mm_[0][:, :], in1=mm_[1][:, :], op=ALU.add), reads=[b_mm[0], b_mm[1]], writes=[b_mm[0]])
                            op("dve", (lambda e, pbanks=pbanks: e.tensor_tensor(out=mm_[1][:, :], in0=psb[pbanks[2]][:, :], in1=sg[2][:, :], op=ALU.mult)),
                               reads=[b_ps[pbanks[2]], b_sg[2]], writes=[b_mm[1]])
                            op("dve", (lambda e, hf=hf, oc=oc: e.tensor_tensor(out=mT[:, hf * 4 + oc, :], in0=mm_[0][:, :], in1=mm_[1][:, :], op=ALU.add)),
                               reads=[b_mm[0], b_mm[1]], writes=[b_mT])
                        for _ in range(4):
                            w_release()
                    for hf in range(2):
                        wo, bwo = w_acquire()
                        wo3 = wo[:, 0:8 * 512].rearrange("p (k n) -> p k n", n=512)
                        for t in range(4):
                            pb_i = nxt()
                            for k in range(8):
                                op("pe", (lambda e, k=k, t=t, pb_i=pb_i, wo3=wo3: e.matmul(psb[pb_i][:, :], mT[:, k, t * 128:(t + 1) * 128], wo3[:, k, :], start=(k == 0), stop=(k == 7))),
                                   reads=[b_mT, bwo], writes=[b_ps[pb_i]])
                            op("dve", (lambda e, t=t, hf=hf, pb_i=pb_i: e.tensor_tensor(out=xt[t][:, hf * 512:(hf + 1) * 512], in0=xt[t][:, hf * 512:(hf + 1) * 512], in1=psb[pb_i][:, :], op=ALU.add)),
                               reads=[b_ps[pb_i], b_xt[t]], writes=[b_xt[t]])
                        w_release()
                    for t in range(4):
                        hbi = t % 2
                        trbank = 6 + (t % 2)
                        op("act", (lambda e, t=t: e.activation(out=junk[:, :], in_=xt[t][:, :], func=AF.Square, accum_out=small[:, t:t + 1])), reads=[b_xt[t]], writes=[b_junk, b_ss[t]])
                        op("act", (lambda e, t=t: e.activation(out=small[:, t:t + 1], in_=small[:, t:t + 1], func=AF.Sqrt, scale=1.0 / D, bias=epsr[:, 0:1])), reads=[b_ss[t], b_eps], writes=[b_ss[t]])
                        op("dve", (lambda e, t=t: e.reciprocal(out=small[:, t:t + 1], in_=small[:, t:t + 1])), reads=[b_ss[t]], writes=[b_ss[t]])
                        op("dve", (lambda e, t=t, hbi=hbi: e.scalar_tensor_tensor(out=hb[hbi][:, :], in0=xt[t][:, :], scalar=small[:, t:t + 1], in1=g2t[:, :], op0=ALU.mult, op1=ALU.mult)),
                           reads=[b_xt[t], b_ss[t], b_g2], writes=[b_hb[hbi]])
                        pv = bf(psb[trbank][:, :])
                        for c in range(8):
                            op("pe", (lambda e, c=c, pv=pv, hbi=hbi: e.transpose(pv[:, c * 128:(c + 1) * 128], hb[hbi][:, c * 128:(c + 1) * 128], ident[:, :])),
                               reads=[b_hb[hbi], b_ident], writes=[b_ps[trbank]])
                        op("act", (lambda e, t=t, pv=pv: e.activation(out=hT[:, :, t * 128:(t + 1) * 128], in_=pv.rearrange("p (c q) -> p c q", q=128), func=AF.Copy)),
                           reads=[b_ps[trbank]], writes=[b_hT])
                    for f in range(11):
                        wgu, bwgu = w_acquire()
                        wgu3 = wgu[:, 0:16 * 256].rearrange("p (k n) -> p k n", n=256)
                        for ch in range(2):
                            pg = nxt(); pu = nxt()
                            for k in range(8):
                                op("pe", (lambda e, k=k, ch=ch, pg=pg: e.matmul(psb[pg][:, :], wgu3[:, k, ch * 128:(ch + 1) * 128], hT[:, k, :], start=(k == 0), stop=(k == 7))),
                                   reads=[b_hT, bwgu], writes=[b_ps[pg]])
                            for k in range(8):
                                op("pe", (lambda e, k=k, ch=ch, pu=pu: e.matmul(psb[pu][:, :], wgu3[:, 8 + k, ch * 128:(ch + 1) * 128], hT[:, k, :], start=(k == 0), stop=(k == 7))),
                                   reads=[b_hT, bwgu], writes=[b_ps[pu]])
                            sj = (f * 2 + ch) % 3
                            op("act", (lambda e, pg=pg, sj=sj: e.activation(out=sgf[sj][:, :], in_=psb[pg][:, :], func=AF.Silu)), reads=[b_ps[pg]], writes=[b_sgf[sj]])
                            op("dve", (lambda e, pu=pu, sj=sj, f=f, ch=ch: e.tensor_tensor(out=actT[:, f * 2 + ch, :], in0=psb[pu][:, :], in1=sgf[sj][:, :], op=ALU.mult)),
                               reads=[b_ps[pu], b_sgf[sj]], writes=[b_actT])
                        w_release()
                    for hf in range(2):
                        wda, bwda = w_acquire()
                        wdb, bwdb = w_acquire()
                        wd3 = [wda[:, 0:11 * 512].rearrange("p (k n) -> p k n", n=512), wdb[:, 0:11 * 512].rearrange("p (k n) -> p k n", n=512)]
                        for t in range(4):
                            pb_i = nxt()
                            for k in range(22):
                                op("pe", (lambda e, k=k, t=t, pb_i=pb_i, wd3=wd3: e.matmul(psb[pb_i][:, :], actT[:, k, t * 128:(t + 1) * 128], wd3[k // 11][:, k % 11, :], start=(k == 0), stop=(k == 21))),
                                   reads=[b_actT, bwda, bwdb], writes=[b_ps[pb_i]])
                            op("dve", (lambda e, t=t, hf=hf, pb_i=pb_i: e.tensor_tensor(out=xt[t][:, hf * 512:(hf + 1) * 512], in0=xt[t][:, hf * 512:(hf + 1) * 512], in1=psb[pb_i][:, :], op=ALU.add)),
                               reads=[b_ps[pb_i], b_xt[t]], writes=[b_xt[t]])
                        w_release()
                        w_release()
                    for t in range(4):
                        tok0 = s0 + t * 128
                        op("pool", (lambda e, t=t, tok0=tok0: e.dma_start(out=ydst[gidx, tok0:tok0 + 128, :], in_=xt[t][:, :])), reads=[b_xt[t]], dma=f"d_x{t}")
                S.barrier()
        S.barrier()
        S.emit()
    return nc


N_CORES = 8
_PROGRAM_CACHE = {}


def _lam_inits(depth):
    return [0.8 - 0.6 * math.exp(-0.3 * l) for l in range(depth)]


def kernel(x_prompt, x_sample, rel_bias, norm1_g, w_in, sgu_ln_g, sgu_ln_b, sgu_w, sgu_b,
           qn_b, kn_b, qn_c, kn_c, lam_q1, lam_k1, lam_q2, lam_k2, subln_g,
           w_pa, w_pb, w_pc, w_o, norm2_g, w_gu, w_down):
    x_prompt = np.asarray(x_prompt, dtype=np.float32)
    x_sample = np.asarray(x_sample, dtype=np.float32)
    depth = int(np.asarray(w_in).shape[0])
    nb_p, s_p = x_prompt.shape[0], x_prompt.shape[1]
    nb_s, s_s = x_sample.shape[0], x_sample.shape[1]
    ncores = N_CORES
    assert nb_p % ncores == 0 and nb_s % ncores == 0
    pp, ps_ = nb_p // ncores, nb_s // ncores
    seq_lens = [("p", i, s_p) for i in range(pp)] + [("s", i, s_s) for i in range(ps_)]
    key = (tuple(seq_lens), depth)
    if key not in _PROGRAM_CACHE:
        _PROGRAM_CACHE[key] = build_program(seq_lens, depth, _lam_inits(depth))
    nc = _PROGRAM_CACHE[key]
    oh, mask, ident = host_constants()
    shared = dict(rel_bias=rel_bias, norm1_g=norm1_g, norm2_g=norm2_g, w_in=w_in, sgu_ln_g=sgu_ln_g, sgu_ln_b=sgu_ln_b,
                  sgu_w=sgu_w, sgu_b=sgu_b, qn_b=qn_b, kn_b=kn_b, qn_c=qn_c, kn_c=kn_c, lam_q1=lam_q1, lam_k1=lam_k1,
                  lam_q2=lam_q2, lam_k2=lam_k2, subln_g=subln_g, w_pa=w_pa, w_pb=w_pb, w_pc=w_pc, w_o=w_o, w_gu=w_gu,
                  w_down=w_down, c_oh=oh, c_mask=mask, c_ident=ident)
    shared = {k: np.ascontiguousarray(np.asarray(v, dtype=np.float32)) for k, v in shared.items()}
    in_maps = []
    for c in range(ncores):
        m = dict(shared)
        m["xp"] = np.ascontiguousarray(x_prompt[c * pp:(c + 1) * pp])
        m["xs"] = np.ascontiguousarray(x_sample[c * ps_:(c + 1) * ps_])
        in_maps.append(m)
    res = run_bass_kernel_spmd(nc, in_maps, core_ids=list(range(ncores)))
    yp = np.concatenate([np.asarray(r["yp"]) for r in res.results], axis=0).astype(np.float32)
    ys = np.concatenate([np.asarray(r["ys"]) for r in res.results], axis=0).astype(np.float32)
    return (yp, ys)
```

```python
import contextlib
import math
import numpy as np
import ml_dtypes
import concourse.bass as bass
import concourse.mybir as mybir
from concourse.bass_utils import run_bass_kernel_spmd

F32 = mybir.dt.float32
BF16 = mybir.dt.bfloat16
AF = mybir.ActivationFunctionType
ALU = mybir.AluOpType
AX = mybir.AxisListType

D = 1024
INC = 6784
DFF = 2816
NBUCK = 32
TB = 512
RMS_EPS = 1e-6
LN_EPS = 1e-5
GEO = {"c": (-768, 1152, 1, None), "b0": (-128, 512, 1, 64), "b1": (-256, 640, 4, 256), "b2": (-1024, 1408, 16, 1024)}
GEO_ORDER = ["c", "b0", "b1", "b2"]
GEO_NH = {"c": 4, "b0": 2, "b1": 2, "b2": 2}


def geo_W(g):
    return GEO[g][1] - GEO[g][0] + 512


def geo_L(g):
    return geo_W(g) + 127


STRIP_OFF = {}
_o = 0
for _g in GEO_ORDER:
    STRIP_OFF[_g] = _o
    _o += GEO_NH[_g] * geo_W(_g)
STRIP_COLS = _o
LMAX = max(geo_L(g) for g in GEO_ORDER)


def _rel_bucket_np(rel):
    nb = NBUCK // 2
    max_exact = nb // 2
    n = np.abs(rel)
    sign_off = np.where(rel > 0, nb, 0)
    nf = np.maximum(n, 1).astype(np.float32)
    large = max_exact + (np.log(nf / np.float32(max_exact)) / np.float32(math.log(1024 / max_exact))
                         * np.float32(nb - max_exact)).astype(np.int32)
    large = np.minimum(large, nb - 1)
    return sign_off + np.where(n < max_exact, n, large)


def host_constants():
    oh = np.zeros((4, NBUCK, LMAX), np.float32)
    mask = np.zeros((4, LMAX), np.float32)
    for gi, g in enumerate(GEO_ORDER):
        dmin, dmax, dil, hw = GEO[g]
        L = geo_L(g)
        rhi = dmax + 127
        rel = rhi - np.arange(L)
        b = _rel_bucket_np(rel)
        oh[gi, b, np.arange(L)] = 1.0
        if hw is None:
            mask[gi, :L] = 1.0
        else:
            mask[gi, :L] = ((rel % dil == 0) & (np.abs(rel) <= hw)).astype(np.float32)
    ident = np.eye(128, dtype=np.float32)
    return oh, mask, ident


class Buf:
    __slots__ = ("name", "w", "r")

    def __init__(self, name):
        self.name = name
        self.w = None
        self.r = {}


class _Rec:
    def __init__(self):
        self.call = None

    def __getattr__(self, name):
        def f(*a, **k):
            self.call = (name, a, k)
            return self
        return f


class Sched:
    COMPUTE = ("pe", "act", "dve", "pool")

    def __init__(self, nc, stack):
        self.nc = nc
        self.stack = stack
        self.q = {k: [] for k in ("pe", "act", "dve", "pool", "sp")}
        self.sems = {}
        self.tick = {}
        self.seen = {k: {} for k in self.q}
        for k in self.COMPUTE:
            self._sem(k)
        self.bufs = []

    def _sem(self, key):
        if key not in self.sems:
            self.sems[key] = self.stack.enter_context(self.nc.semaphore("s_" + key))
            self.tick[key] = 0
        return self.sems[key]

    def buf(self, name):
        b = Buf(name)
        self.bufs.append(b)
        return b

    def op(self, eng, fn, reads=(), writes=(), dma=None):
        deps = {}
        for b in reads:
            if b.w is not None:
                k, t = b.w
                if deps.get(k, 0) < t:
                    deps[k] = t
        for b in writes:
            if b.w is not None:
                k, t = b.w
                if deps.get(k, 0) < t:
                    deps[k] = t
            for k, t in b.r.items():
                if deps.get(k, 0) < t:
                    deps[k] = t
        waits = []
        seen = self.seen[eng]
        for k, t in deps.items():
            if dma is None and k == eng:
                if eng == "pe" or t < self.tick[eng] - 1:
                    continue
                if seen.get(k, 0) >= t:
                    continue
            elif seen.get(k, 0) >= t:
                continue
            seen[k] = t
            waits.append((k, t))
        if dma is not None:
            self._sem(dma)
            self.tick[dma] += 16
            ev = (dma, self.tick[dma])
            inc = (dma, 16)
        else:
            self.tick[eng] += 1
            ev = (eng, self.tick[eng])
            inc = (eng, 1)
        rec = _Rec()
        fn(rec)
        assert rec.call is not None
        self.q[eng].append((waits, rec.call, inc))
        for b in writes:
            b.w = ev
            b.r = {}
        for b in reads:
            if b in writes:
                continue
            k, t = ev
            if b.r.get(k, 0) < t:
                b.r[k] = t
        return ev

    def barrier(self):
        for eng in self.q:
            waits = []
            seen = self.seen[eng]
            for k, t in self.tick.items():
                if t > 0 and seen.get(k, 0) < t and k != eng:
                    seen[k] = t
                    waits.append((k, t))
            if waits:
                self.q[eng].append((waits, None, None))
        for b in self.bufs:
            b.w = None
            b.r = {}

    def emit(self):
        nc = self.nc
        sems = self.sems

        def run(e, items):
            for waits, fn, inc in items:
                for k, t in waits:
                    e.wait_ge(sems[k], t)
                if fn is not None:
                    name, a, k = fn
                    getattr(e, name)(*a, **k).then_inc(sems[inc[0]], inc[1])

        with nc.Block() as block:
            @block.tensor
            def _(e):
                run(e, self.q["pe"])

            @block.scalar
            def _(e):
                run(e, self.q["act"])

            @block.vector
            def _(e):
                run(e, self.q["dve"])

            @block.gpsimd
            def _(e):
                run(e, self.q["pool"])

            @block.sync
            def _(e):
                run(e, self.q["sp"])


class Lag:
    def __init__(self, la):
        self.la = la
        self.q = []

    def push(self, fn, main=True):
        self.q.append((fn, main))
        if main:
            while sum(1 for _, m in self.q if m) > self.la:
                f, _ = self.q.pop(0)
                f()
            while self.q and not self.q[0][1]:
                f, _ = self.q.pop(0)
                f()

    def flush(self):
        while self.q:
            f, _ = self.q.pop(0)
            f()


def build_program(seq_lens, depth, lam_inits):
    nc = bass.Bass("TRN2", target_bir_lowering=False)
    n_p = sum(1 for g, _, _ in seq_lens if g == "p")
    n_s = sum(1 for g, _, _ in seq_lens if g == "s")
    S_p = max([s for g, _, s in seq_lens if g == "p"], default=128)
    S_s = max([s for g, _, s in seq_lens if g == "s"], default=128)
    SMAX = max(s for _, _, s in seq_lens)
    L = depth

    def din(name, shape, dt=F32):
        return nc.dram_tensor(name, list(shape), dt, kind="ExternalInput").ap()

    def dscr(name, shape, dt=BF16):
        return nc.dram_tensor(name, list(shape), dt, kind="Internal").ap()

    xin = {"p": din("xp", [max(n_p, 1), S_p, D]), "s": din("xs", [max(n_s, 1), S_s, D])}
    yout = {"p": nc.dram_tensor("yp", [max(n_p, 1), S_p, D], F32, kind="ExternalOutput").ap(),
            "s": nc.dram_tensor("ys", [max(n_s, 1), S_s, D], F32, kind="ExternalOutput").ap()}
    rel_bias = din("rel_bias", [NBUCK, 10])
    norm1_g = din("norm1_g", [L, D]); norm2_g = din("norm2_g", [L, D])
    w_in = din("w_in", [L, D, INC])
    sgu_ln_g = din("sgu_ln_g", [L, 512]); sgu_ln_b = din("sgu_ln_b", [L, 512])
    sgu_w = din("sgu_w", [L, 8, 128, 128]); sgu_b = din("sgu_b", [L, 8, 128])
    qn_b = din("qn_b", [L, 64]); kn_b = din("kn_b", [L, 64]); qn_c = din("qn_c", [L, 64]); kn_c = din("kn_c", [L, 64])
    lam_q1 = din("lam_q1", [L, 64]); lam_k1 = din("lam_k1", [L, 64]); lam_q2 = din("lam_q2", [L, 64]); lam_k2 = din("lam_k2", [L, 64])
    subln_g = din("subln_g", [L, 128])
    w_pa = din("w_pa", [L, 512, D]); w_pb = din("w_pb", [L, 384, D]); w_pc = din("w_pc", [L, 512, D])
    w_o = din("w_o", [L, D, D]); w_gu = din("w_gu", [L, D, 2 * DFF]); w_down = din("w_down", [L, DFF, D])
    c_oh = din("c_oh", [4, NBUCK, LMAX]); c_mask = din("c_mask", [4, LMAX]); c_ident = din("c_ident", [128, 128])

    wb_in = dscr("wb_in", [L, D, INC]); wb_pa = dscr("wb_pa", [L, 512, D]); wb_pb = dscr("wb_pb", [L, 384, D])
    wb_pc = dscr("wb_pc", [L, 512, D]); wb_o = dscr("wb_o", [L, D, D]); wb_gu = dscr("wb_gu", [L, D, 2 * DFF])
    wb_dn = dscr("wb_dn", [L, DFF, D])
    rrep = dscr("rrep", [10, 128, LMAX]); strd = dscr("strd", [128, STRIP_COLS])
    NSEQ = len(seq_lens)
    QT = [dscr(f"QT{i}", [7, 128, s]) for i, (_, _, s) in enumerate(seq_lens)]
    KT = [dscr(f"KT{i}", [7, 128, s]) for i, (_, _, s) in enumerate(seq_lens)]
    VB = [dscr(f"VB{i}", [s // 128, 128, 390]) for i, (_, _, s) in enumerate(seq_lens)]
    VC = [dscr(f"VC{i}", [s // 128, 128, 516]) for i, (_, _, s) in enumerate(seq_lens)]
    BOT = [dscr(f"BOT{i}", [3, 128, s]) for i, (_, _, s) in enumerate(seq_lens)]
    COT = [dscr(f"COT{i}", [4, 128, s]) for i, (_, _, s) in enumerate(seq_lens)]

    with contextlib.ExitStack() as st:
        S = Sched(nc, st)
        op = S.op

        def sbt(name, shape, dt=F32):
            return st.enter_context(nc.sbuf_tensor(name, list(shape), dt))

        ident = sbt("ident", [128, 128], BF16); b_ident = S.buf("ident")
        identf = sbt("identf", [128, 128], F32)
        epsr = sbt("epsr", [128, 2], F32); b_eps = S.buf("eps")
        small = sbt("small", [128, 64], F32)
        lamt = sbt("lamt", [128, 8], F32); b_lam = S.buf("lam")
        ksc = sbt("ksc", [128, 4], F32); b_ksc = S.buf("ksc")
        RBYTES = 204 * 1024
        R = sbt("R", [128, RBYTES // 2], BF16)
        psb = [st.enter_context(nc.psum_tensor(f"ps{i}", [128, 512], F32)) for i in range(8)]
        b_ps = [S.buf(f"ps{i}") for i in range(8)]

        class Alloc:
            def __init__(self):
                self.off = 0

            def __call__(self, shape, dt):
                n = int(np.prod(shape))
                size = 2 if dt == BF16 else 4
                nb = (n * size + 63) // 64 * 64
                off = self.off
                self.off += nb
                assert self.off <= RBYTES, ("region overflow", self.off)
                a = R[:, off // 2: off // 2 + n * size // 2]
                if dt == F32:
                    a = a.bitcast(F32)
                if len(shape) == 2:
                    a = a.rearrange("p (a b) -> p a b", b=shape[1])
                elif len(shape) == 3:
                    a = a.rearrange("p (a b c) -> p a b c", b=shape[1], c=shape[2])
                elif len(shape) == 4:
                    a = a.rearrange("p (a b c d) -> p a b c d", b=shape[1], c=shape[2], d=shape[3])
                return a

        def bf(psap):
            return psap.bitcast(BF16)

        op("sp", lambda e: e.dma_start(out=identf[:], in_=c_ident[:, :]), writes=[b_ident], dma="d_ident")
        op("dve", lambda e: e.tensor_copy(out=ident[:], in_=identf[:]), reads=[b_ident], writes=[b_ident])
        op("dve", lambda e: e.memset(epsr[:, 0:1], RMS_EPS), writes=[b_eps])
        op("dve", lambda e: e.memset(epsr[:, 1:2], LN_EPS), writes=[b_eps])

        b_wb = {}
        for l in range(L):
            for nm, src, dst, rows in (("in", w_in, wb_in, D), ("pa", w_pa, wb_pa, 512), ("pb", w_pb, wb_pb, 384),
                                       ("pc", w_pc, wb_pc, 512), ("o", w_o, wb_o, D), ("gu", w_gu, wb_gu, D),
                                       ("dn", w_down, wb_dn, DFF)):
                b = S.buf(f"wb_{nm}{l}")
                b_wb[(nm, l)] = b
                for r0 in range(0, rows, 128):
                    op("pool", (lambda e, src=src, dst=dst, l=l, r0=r0: e.dma_start(out=dst[l, r0:r0 + 128, :], in_=src[l, r0:r0 + 128, :])),
                       writes=[b], dma="d_cv")

        al = Alloc()
        tab = al([10], F32)
        tabb = al([128], F32)
        oht = al([LMAX], F32)
        mkt = al([LMAX], F32)
        rv = al([LMAX], BF16)
        evt = al([512], F32); b_evt = S.buf('evt')
        b_tab = S.buf("tab"); b_tabb = S.buf("tabb"); b_oht = S.buf("oht"); b_mkt = S.buf("mkt"); b_rv = S.buf("rv")
        b_rrep = S.buf("rrep"); b_strd = S.buf("strd")
        op("sp", lambda e: e.dma_start(out=tab[0:NBUCK, :], in_=rel_bias[:, :]), writes=[b_tab], dma="d_tab")
        strip_heads = {"c": [6, 7, 8, 9], "b0": [0, 1], "b1": [2, 3], "b2": [4, 5]}
        ri = 0
        for gi, g in enumerate(GEO_ORDER):
            Lg = geo_L(g); Wg = geo_W(g)
            op("sp", (lambda e, gi=gi, Lg=Lg: e.dma_start(out=oht[0:NBUCK, 0:Lg], in_=c_oh[gi, :, 0:Lg])), writes=[b_oht], dma="d_oht")
            op("sp", (lambda e, gi=gi, Lg=Lg: e.dma_start(out=mkt[:, 0:Lg], in_=c_mask[gi:gi + 1, 0:Lg].partition_broadcast(128))), writes=[b_mkt], dma="d_misc2")
            for hi, hcol in enumerate(strip_heads[g]):
                op("dve", (lambda e, hcol=hcol: e.tensor_copy(out=tabb[0:NBUCK, :], in_=tab[0:NBUCK, hcol:hcol + 1].to_broadcast([NBUCK, 128]))),
                   reads=[b_tab], writes=[b_tabb])
                for c0 in range(0, Lg, 512):
                    c1 = min(Lg, c0 + 512)
                    op("pe", (lambda e, c0=c0, c1=c1: e.matmul(psb[0][:, 0:c1 - c0], tabb[0:NBUCK, :], oht[0:NBUCK, c0:c1], start=True, stop=True)),
                       reads=[b_tabb, b_oht], writes=[b_ps[0]])
                    op("act", (lambda e, c0=c0, c1=c1: e.activation(out=evt[:, 0:c1 - c0], in_=psb[0][:, 0:c1 - c0], func=AF.Exp)),
                       reads=[b_ps[0]], writes=[b_evt])
                    op("dve", (lambda e, c0=c0, c1=c1: e.tensor_tensor(out=rv[:, c0:c1], in0=evt[:, 0:c1 - c0], in1=mkt[:, c0:c1], op=ALU.mult)),
                       reads=[b_evt, b_mkt], writes=[b_rv])
                op("sp", (lambda e, ri=ri, Lg=Lg: e.dma_start(out=rrep[ri, :, 0:Lg], in_=rv[:, 0:Lg])), reads=[b_rv], writes=[b_rrep], dma="d_rrep")
                soff = STRIP_OFF[g] + hi * Wg
                src = bass.AP(rrep.tensor, ri * 128 * LMAX + 127, [[LMAX - 1, 128], [1, Wg]])
                op("sp", (lambda e, soff=soff, Wg=Wg, src=src: e.dma_start(out=strd[:, soff:soff + Wg], in_=src)),
                   reads=[b_rrep], writes=[b_strd], dma="d_strd")
                ri += 1
        S.barrier()

        for si, (grp, gidx, SL) in enumerate(seq_lens):
            NT = SL // 128
            NB = SL // TB
            for l in range(L):
                xsrc = xin[grp] if l == 0 else yout[grp]
                ydst = yout[grp]

                al = Alloc()
                xt = [al([D], F32) for _ in range(4)]
                hb = [al([D], BF16) for _ in range(2)]
                hT = al([8, TB], BF16)
                g1t = al([D], F32)
                wqkv = al([8, 2688], BF16)
                junk = al([D], F32)
                sq = [al([512], F32) for _ in range(4)]
                qn = [al([512], BF16) for _ in range(4)]
                qst = al([7, TB], BF16)
                kst = al([7, TB], BF16)
                vbst = al([4, 6, 65], BF16)
                vcst = al([4, 4, 129], BF16)
                svs = [al([8], F32) for _ in range(4)]
                b_xt = [S.buf(f"xt{t}") for t in range(4)]
                b_hb = [S.buf(f"hb{t}") for t in range(2)]
                b_hT = S.buf("hT"); b_g1 = S.buf("g1"); b_wqkv = S.buf("wqkv"); b_junk = S.buf("junk")
                b_sq = [S.buf(f"sq{i}") for i in range(4)]; b_qn = [S.buf(f"qn{i}") for i in range(4)]
                b_qst = S.buf("qst"); b_kst = S.buf("kst"); b_vbst = S.buf("vbst"); b_vcst = S.buf("vcst")
                b_ss = [S.buf(f"ss{t}") for t in range(4)]
                b_qs = [S.buf(f"qs{i}") for i in range(4)]
                lag1 = Lag(2)
                qkc = [0]
                b_QT = S.buf("QTd"); b_KT = S.buf("KTd"); b_VB = S.buf("VBd"); b_VC = S.buf("VCd")
                b_BOT = S.buf("BOTd"); b_COT = S.buf("COTd")

                op("sp", (lambda e, l=l: e.dma_start(out=g1t[:, :], in_=norm1_g[l:l + 1, :].partition_broadcast(128))), writes=[b_g1], dma="d_g1")
                op("sp", (lambda e, l=l: e.dma_start(out=wqkv[:, :, :], in_=wb_in[l, :, 1024:3712].rearrange("(c p) n -> p c n", p=128))),
                   reads=[b_wb[("in", l)]], writes=[b_wqkv], dma="d_wqkv")
                for half in range(2):
                    for j, src in enumerate((qn_b, kn_b, qn_c, kn_c)):
                        op("sp", (lambda e, l=l, half=half, j=j, src=src: e.dma_start(
                            out=lamt[half * 64:(half + 1) * 64, j:j + 1], in_=src[l:l + 1, :].rearrange("o d -> d o"))),
                           writes=[b_lam], dma="d_lam")
                op("dve", lambda e: e.tensor_tensor(out=ksc[:, 0:1], in0=lamt[:, 0:1], in1=lamt[:, 1:2], op=ALU.mult), reads=[b_lam], writes=[b_ksc])
                op("dve", lambda e: e.tensor_tensor(out=ksc[:, 1:2], in0=lamt[:, 2:3], in1=lamt[:, 3:4], op=ALU.mult), reads=[b_lam], writes=[b_ksc])
                op("dve", lambda e: e.tensor_scalar(out=ksc[:, 0:2], in0=ksc[:, 0:2], scalar1=0.125, scalar2=None, op0=ALU.mult), reads=[b_ksc], writes=[b_ksc])
                op("dve", lambda e: e.memset(vbst[:, :, :, 64:65], 1.0), writes=[b_vbst])
                op("dve", lambda e: e.memset(vcst[:, :, :, 128:129], 1.0), writes=[b_vcst])

                rot = [0]

                def nxt(n=6):
                    i = rot[0] % n
                    rot[0] += 1
                    return i

                def norm_tile(t, tok0, src_ap, gt, b_g, trbank, ldq="pool"):
                    hbi = t % 2
                    op(ldq, (lambda e: e.dma_start(out=xt[t][:, :], in_=src_ap[tok0:tok0 + 128, :])), writes=[b_xt[t]], dma=f"d_x{t}")
                    op("act", (lambda e: e.activation(out=junk[:, :], in_=xt[t][:, :], func=AF.Square, accum_out=small[:, t:t + 1])),
                       reads=[b_xt[t]], writes=[b_junk, b_ss[t]])
                    op("act", (lambda e: e.activation(out=small[:, t:t + 1], in_=small[:, t:t + 1], func=AF.Sqrt, scale=1.0 / D, bias=epsr[:, 0:1])),
                       reads=[b_ss[t], b_eps], writes=[b_ss[t]])
                    op("dve", (lambda e: e.reciprocal(out=small[:, t:t + 1], in_=small[:, t:t + 1])), reads=[b_ss[t]], writes=[b_ss[t]])
                    op("dve", (lambda e: e.scalar_tensor_tensor(out=hb[hbi][:, :], in0=xt[t][:, :], scalar=small[:, t:t + 1], in1=gt[:, :],
                                                                 op0=ALU.mult, op1=ALU.mult)),
                       reads=[b_xt[t], b_ss[t], b_g], writes=[b_hb[hbi]])
                    pv = bf(psb[trbank][:, :])
                    for c in range(8):
                        op("pe", (lambda e, c=c: e.transpose(pv[:, c * 128:(c + 1) * 128], hb[hbi][:, c * 128:(c + 1) * 128], ident[:, :])),
                           reads=[b_hb[hbi], b_ident], writes=[b_ps[trbank]])
                    op("act", (lambda e: e.activation(out=hT[:, :, t * 128:(t + 1) * 128], in_=pv.rearrange("p (c q) -> p c q", q=128), func=AF.Copy)),
                       reads=[b_ps[trbank]], writes=[b_hT])

                blocks = [("q", 0, 3, 0, 0), ("k", 384, 3, 0, 0), ("v", 768, 3, 0, 0),
                          ("q", 1152, 4, 3, 1), ("k", 1664, 4, 3, 1), ("v", 2176, 4, 3, 1)]
                for tb in range(NB):
                    for t in range(4):
                        norm_tile(t, tb * TB + t * 128, xsrc[gidx], g1t, b_g1, 6 + (t % 2), ldq="sp")
                    for t in range(4):
                        for bi, (kind, c0, nch, ch0, isc) in enumerate(blocks):
                            ncol = nch * 128
                            pb_i = nxt()
                            pso = psb[pb_i]
                            for k in range(8):
                                op("pe", (lambda e, k=k, pso=pso, c0=c0, ncol=ncol, t=t: e.matmul(
                                    pso[:, 0:ncol], hT[:, k, t * 128:(t + 1) * 128], wqkv[:, k, c0:c0 + ncol], start=(k == 0), stop=(k == 7))),
                                   reads=[b_hT, b_wqkv], writes=[b_ps[pb_i]])
                            if kind == "v":
                                if isc == 0:
                                    op("dve", (lambda e, pso=pso, t=t: e.tensor_copy(out=vbst[:, t, :, 0:64], in_=pso[:, 0:384].rearrange("p (h d) -> p h d", d=64))),
                                       reads=[b_ps[pb_i]], writes=[b_vbst])
                                else:
                                    op("act", (lambda e, pso=pso, t=t: e.activation(out=vcst[:, t, :, 0:128], in_=pso[:, 0:512].rearrange("p (h d) -> p h d", d=128), func=AF.Copy)),
                                       reads=[b_ps[pb_i]], writes=[b_vcst])
                                continue
                            nh = ncol // 64
                            j = qkc[0] % 4
                            qkc[0] += 1
                            sv = svs[j]
                            op("act", (lambda e, pso=pso, ncol=ncol, j=j: e.activation(out=sq[j][:, 0:ncol], in_=pso[:, 0:ncol], func=AF.Square)),
                               reads=[b_ps[pb_i]], writes=[b_sq[j]])
                            op("dve", (lambda e, ncol=ncol, j=j, sv=sv, nh=nh: e.tensor_reduce(out=sv[:, 0:nh], in_=sq[j][:, 0:ncol].rearrange("p (h d) -> p h d", d=64), axis=AX.X, op=ALU.add)),
                               reads=[b_sq[j]], writes=[b_qs[j]])
                            op("act", (lambda e, sv=sv, nh=nh: e.activation(out=sv[:, 0:nh], in_=sv[:, 0:nh], func=AF.Sqrt, scale=1.0 / 64, bias=epsr[:, 0:1])),
                               reads=[b_qs[j], b_eps], writes=[b_qs[j]])
                            op("dve", (lambda e, sv=sv, nh=nh: e.reciprocal(out=sv[:, 0:nh], in_=sv[:, 0:nh])), reads=[b_qs[j]], writes=[b_qs[j]])
                            if isc == 0:
                                op("dve", (lambda e, pso=pso, ncol=ncol, j=j, sv=sv, nh=nh: e.tensor_tensor(
                                    out=qn[j][:, 0:ncol].rearrange("p (h d) -> p h d", d=64), in0=pso[:, 0:ncol].rearrange("p (h d) -> p h d", d=64),
                                    in1=sv[:, 0:nh].unsqueeze(2).to_broadcast([128, nh, 64]), op=ALU.mult)),
                                   reads=[b_ps[pb_i], b_qs[j]], writes=[b_qn[j]])
                            else:
                                op("dve", (lambda e, pso=pso, j=j, sv=sv: e.tensor_tensor(
                                    out=qn[j][:, 0:512].rearrange("p (h m d) -> p m h d", h=4, m=2),
                                    in0=pso[:, 0:512].rearrange("p (m h d) -> p m h d", m=2, h=4),
                                    in1=sv[:, 0:8].rearrange("p (m h) -> p m h", m=2).unsqueeze(3).to_broadcast([128, 2, 4, 64]), op=ALU.mult)),
                                   reads=[b_ps[pb_i], b_qs[j]], writes=[b_qn[j]])
                            def tr_part(kind=kind, nch=nch, ch0=ch0, isc=isc, t=t, j=j, bi=bi):
                                trb = 6 + (bi % 2)
                                pv = bf(psb[trb][:, :])
                                for c in range(nch):
                                    src = qn[j][:, c * 128:(c + 1) * 128]
                                    op("pe", (lambda e, c=c, src=src, pv=pv: e.transpose(pv[:, c * 128:(c + 1) * 128], src, ident[:, :])),
                                       reads=[b_qn[j], b_ident], writes=[b_ps[trb]])
                                dstt = qst if kind == "q" else kst
                                b_dst = b_qst if kind == "q" else b_kst
                                if kind == "q":
                                    op("dve", (lambda e: e.tensor_copy(
                                        out=dstt[:, ch0:ch0 + nch, t * 128:(t + 1) * 128], in_=pv[:, 0:nch * 128].rearrange("p (c q) -> p c q", q=128))),
                                       reads=[b_ps[trb]], writes=[b_dst])
                                else:
                                    op("act", (lambda e: e.activation(
                                        out=dstt[:, ch0:ch0 + nch, t * 128:(t + 1) * 128], in_=pv[:, 0:nch * 128].rearrange("p (c q) -> p c q", q=128),
                                        func=AF.Copy, scale=ksc[:, isc:isc + 1])),
                                       reads=[b_ps[trb], b_ksc], writes=[b_dst])
                            lag1.push(tr_part)

                    def tb_stores(tb=tb):
                        s0 = tb * TB
                        op("pool", (lambda e: e.dma_start(out=QT[si][:, :, s0:s0 + TB].rearrange("c p s -> p c s"), in_=qst[:, :, :])), reads=[b_qst], writes=[b_QT], dma="d_qst")
                        op("pool", (lambda e: e.dma_start(out=KT[si][:, :, s0:s0 + TB].rearrange("c p s -> p c s"), in_=kst[:, :, :])), reads=[b_kst], writes=[b_KT], dma="d_kst")
                        op("pool", (lambda e: e.dma_start(out=VB[si][tb * 4:(tb + 1) * 4, :, :].rearrange("t p c -> p t c"), in_=vbst[:, :, :, :].rearrange("p t h d -> p t (h d)"))),
                           reads=[b_vbst], writes=[b_VB], dma="d_vbst")
                        op("pool", (lambda e: e.dma_start(out=VC[si][tb * 4:(tb + 1) * 4, :, :].rearrange("t p c -> p t c"), in_=vcst[:, :, :, :].rearrange("p t h d -> p t (h d)"))),
                           reads=[b_vcst], writes=[b_VC], dma="d_vcst")
                    lag1.push(tb_stores, main=False)
                lag1.flush()
                S.barrier()

                al = Alloc()
                strips = al([STRIP_COLS], BF16)
                vcall = al([NT, 516], BF16)
                slotA = al([3 * SL], BF16)
                slotB = al([3 * SL], BF16)
                slotC = al([NT * 390], BF16)
                pt = [al([512], BF16) for _ in range(4)]
                stage_b = al([4, 6, 65], F32)
                stage_c = al([4, 2, 129], F32)
                o_t = al([4, 128], F32); t_t = al([4, 128], F32); q_t = al([4, 128], F32)
                bo_tm = al([4, 384], BF16); co_tm = al([4, 128], BF16)
                boT_st = al([3, TB], BF16); coT_st = al([TB], BF16)
                sgt = al([128], F32)
                zz = al([64], F32)
                b_strips = S.buf("strips"); b_vcall = S.buf("vcall")
                b_slot = [S.buf("slotA"), S.buf("slotB"), S.buf("slotC")]
                b_pt = [S.buf(f"pt{i}") for i in range(4)]
                b_stb = S.buf("stage_b"); b_stc = S.buf("stage_c")
                b_ot = S.buf("o_t"); b_tt = S.buf("t_t"); b_qt = S.buf("q_t")
                b_botm = S.buf("bo_tm"); b_cotm = S.buf("co_tm"); b_boT = S.buf("boT_st"); b_coT = S.buf("coT_st")
                b_sgt = S.buf("sgt"); b_zz = S.buf("zz")
                slots = [slotA, slotB, slotC]

                op("sp", lambda e: e.dma_start(out=strips[:, :], in_=strd[:, :]), reads=[b_strd], writes=[b_strips], dma="d_strips")
                for t0 in range(0, NT, 8):
                    op("sp", (lambda e, t0=t0: e.dma_start(out=vcall[:, t0:min(NT, t0 + 8), :], in_=VC[si][t0:min(NT, t0 + 8), :, :].rearrange("t p c -> p t c"))),
                       reads=[b_VC], writes=[b_vcall], dma="d_vcall")
                for j, src in enumerate((lam_q1, lam_k1, lam_q2, lam_k2)):
                    op("sp", (lambda e, l=l, j=j, src=src: e.dma_start(out=o_t[:, j, 0:64], in_=src[l:l + 1, :].partition_broadcast(128))),
                       writes=[b_ot], dma="d_lamv")
                op("dve", lambda e: e.tensor_tensor(out=t_t[:, 0, 0:64], in0=o_t[:, 0, 0:64], in1=o_t[:, 1, 0:64], op=ALU.mult), reads=[b_ot], writes=[b_tt])
                op("dve", lambda e: e.tensor_tensor(out=t_t[:, 1, 0:64], in0=o_t[:, 2, 0:64], in1=o_t[:, 3, 0:64], op=ALU.mult), reads=[b_ot], writes=[b_tt])
                op("dve", lambda e: e.tensor_reduce(out=zz[:, 0:2], in_=t_t[:, 0:2, 0:64], axis=AX.X, op=ALU.add), reads=[b_tt], writes=[b_zz])
                op("act", lambda e: e.activation(out=zz[:, 2:4], in_=zz[:, 0:2], func=AF.Exp), reads=[b_zz], writes=[b_zz])
                li = float(lam_inits[l])
                op("dve", lambda e: e.tensor_tensor(out=lamt[:, 4:5], in0=zz[:, 3:4], in1=zz[:, 2:3], op=ALU.subtract), reads=[b_zz], writes=[b_lam])
                op("dve", lambda e: e.tensor_scalar(out=lamt[:, 4:5], in0=lamt[:, 4:5], scalar1=-li, scalar2=None, op0=ALU.add), reads=[b_lam], writes=[b_lam])
                op("sp", (lambda e, l=l: e.dma_start(out=sgt[:, :], in_=subln_g[l:l + 1, :].partition_broadcast(128))), writes=[b_sgt], dma="d_sgt")
                op("dve", lambda e: e.tensor_scalar(out=sgt[:, :], in0=sgt[:, :], scalar1=1.0 - li, scalar2=None, op0=ALU.mult), reads=[b_sgt], writes=[b_sgt])

                srot = [0]
                prot = [0]
                arot = [0]
                lag15 = Lag(2)

                def score_tile(lhsT, rhs, strip_win, pvs, extra_reads):
                    sb_i = 4 + (srot[0] % 3); srot[0] += 1
                    pi = prot[0] % 4; prot[0] += 1
                    op("pe", (lambda e: e.matmul(psb[sb_i][:, :], lhsT, rhs, start=True, stop=True)), reads=extra_reads, writes=[b_ps[sb_i]])
                    op("act", (lambda e: e.activation(out=pt[pi][:, :], in_=psb[sb_i][:, :], func=AF.Exp)), reads=[b_ps[sb_i]], writes=[b_pt[pi]])
                    op("dve", (lambda e: e.tensor_tensor(out=pt[pi][:, :], in0=pt[pi][:, :], in1=strip_win, op=ALU.mult)), reads=[b_pt[pi], b_strips], writes=[b_pt[pi]])
                    def pv_part():
                        for (oap, bi_, qs, rap, stt, stp) in pvs:
                            op("pe", (lambda e, oap=oap, qs=qs, rap=rap, stt=stt, stp=stp: e.matmul(oap, pt[pi][:, qs * 128:(qs + 1) * 128], rap, start=stt, stop=stp, skip_group_check=True)),
                               reads=[b_pt[pi]] + extra_reads, writes=[b_ps[bi_]])
                    lag15.push(pv_part)

                ktb = slotA.rearrange("p (c s) -> p c s", s=SL)
                qtb = slotB.rearrange("p (c s) -> p c s", s=SL)
                vbt = slotC.rearrange("p (t c) -> p t c", c=390)
                op("sp", lambda e: e.dma_start(out=ktb, in_=KT[si][0:3, :, :].rearrange("c p s -> p c s")), reads=[b_KT], writes=[b_slot[0]], dma="d_slot0")
                op("sp", lambda e: e.dma_start(out=qtb, in_=QT[si][0:3, :, :].rearrange("c p s -> p c s")), reads=[b_QT], writes=[b_slot[1]], dma="d_slot1")
                for t0 in range(0, NT, 8):
                    op("sp", (lambda e, t0=t0: e.dma_start(out=vbt[:, t0:min(NT, t0 + 8), :], in_=VB[si][t0:min(NT, t0 + 8), :, :].rearrange("t p c -> p t c"))),
                       reads=[b_VB], writes=[b_slot[2]], dma="d_slot2")
                for qb in range(NB):
                    for g in range(3):
                        gname = f"b{g}"
                        dmin, dmax, _, _ = GEO[gname]
                        Wg = geo_W(gname)
                        kts = [kt for kt in range(NT) if dmin <= kt * 128 - qb * TB <= dmax]
                        for hp in range(2):
                            m = g * 2 + hp
                            ab = arot[0] % 4; arot[0] += 1
                            soff = STRIP_OFF[gname] + hp * Wg
                            for i, kt in enumerate(kts):
                                c0 = dmax - (kt * 128 - qb * TB)
                                pvs = [(psb[ab][:, qs * 65:(qs + 1) * 65], ab, qs, vbt[:, kt, m * 65:(m + 1) * 65],
                                        (i == 0 and qs == 0), (i == len(kts) - 1)) for qs in range(4)]
                                score_tile(ktb[hp * 64:(hp + 1) * 64, g, kt * 128:(kt + 1) * 128], qtb[hp * 64:(hp + 1) * 64, g, qb * TB:(qb + 1) * TB],
                                           strips[:, soff + c0:soff + c0 + 512], pvs, [b_slot[0], b_slot[1], b_slot[2]])
                            def evac_b(ab=ab, m=m):
                                op("act", (lambda e, ab=ab, m=m: e.activation(out=stage_b[:, :, m, :], in_=psb[ab][:, 0:260].rearrange("p (q c) -> p q c", c=65), func=AF.Copy)),
                                   reads=[b_ps[ab]], writes=[b_stb])
                            lag15.push(evac_b, main=False)
                    def norm_b(qb=qb):
                        zv = stage_b[:, :, :, 64]
                        op("dve", lambda e: e.tensor_tensor(out=zz[:, 0:8].rearrange("p (q h) -> p q h", h=2), in0=zv[:, :, 0:2], in1=zv[:, :, 2:4], op=ALU.add), reads=[b_stb], writes=[b_zz])
                        op("dve", lambda e: e.tensor_tensor(out=zz[:, 0:8].rearrange("p (q h) -> p q h", h=2), in0=zz[:, 0:8].rearrange("p (q h) -> p q h", h=2), in1=zv[:, :, 4:6], op=ALU.add), reads=[b_stb, b_zz], writes=[b_zz])
                        op("dve", lambda e: e.reciprocal(out=zz[:, 8:16], in_=zz[:, 0:8]), reads=[b_zz], writes=[b_zz])
                        for g in range(3):
                            op("dve", (lambda e, g=g: e.tensor_tensor(
                                out=bo_tm[:, :, g * 128:(g + 1) * 128].rearrange("p q (h d) -> p q h d", d=64),
                                in0=stage_b[:, :, 2 * g:2 * g + 2, 0:64],
                                in1=zz[:, 8:16].rearrange("p (q h) -> p q h", h=2).unsqueeze(3).to_broadcast([128, 4, 2, 64]), op=ALU.mult)),
                               reads=[b_stb, b_zz], writes=[b_botm])
                        pv = bf(psb[7][:, :])
                        for qs in range(4):
                            for c in range(3):
                                op("pe", (lambda e, qs=qs, c=c: e.transpose(pv[:, c * 128:(c + 1) * 128], bo_tm[:, qs, c * 128:(c + 1) * 128], ident[:, :])),
                                   reads=[b_botm, b_ident], writes=[b_ps[7]])
                            op("act", (lambda e, qs=qs: e.activation(out=boT_st[:, :, qs * 128:(qs + 1) * 128], in_=pv[:, 0:384].rearrange("p (c q) -> p c q", q=128), func=AF.Copy)),
                               reads=[b_ps[7]], writes=[b_boT])
                        op("pool", (lambda e, qb=qb: e.dma_start(out=BOT[si][:, :, qb * TB:(qb + 1) * TB].rearrange("c p s -> p c s"), in_=boT_st[:, :, :])),
                           reads=[b_boT], writes=[b_BOT], dma="d_boT")
                    lag15.push(norm_b, main=False)

                dmin, dmax, _, _ = GEO["c"]
                Wc = geo_W("c")
                for h in range(4):
                    lag15.flush()
                    sl = slots[h % 3]
                    bsl = b_slot[h % 3]
                    ktc = sl[:, 0:SL]
                    qtc = sl[:, SL:2 * SL]
                    op("sp", (lambda e, h=h, ktc=ktc: e.dma_start(out=ktc, in_=KT[si][3 + h, :, :])), reads=[b_KT], writes=[bsl], dma=f"d_slot{h % 3}")
                    op("sp", (lambda e, h=h, qtc=qtc: e.dma_start(out=qtc, in_=QT[si][3 + h, :, :])), reads=[b_QT], writes=[bsl], dma=f"d_slot{h % 3}")
                    soff = STRIP_OFF["c"] + h * Wc
                    for qb in range(NB):
                        for mp in range(2):
                            a0 = (arot[0] % 2) * 2; arot[0] += 1
                            banks = (a0, a0 + 1)
                            for kt in range(NT):
                                dl = min(max(kt * 128 - qb * TB, dmin), dmax)
                                c0 = dmax - dl
                                pvs = []
                                for qs in range(4):
                                    bk = banks[qs // 2]
                                    col = (qs % 2) * 129
                                    pvs.append((psb[bk][:, col:col + 129], bk, qs, vcall[:, kt, h * 129:(h + 1) * 129], (kt == 0 and qs % 2 == 0), (kt == NT - 1)))
                                score_tile(ktc[mp * 64:(mp + 1) * 64, kt * 128:(kt + 1) * 128], qtc[mp * 64:(mp + 1) * 64, qb * TB:(qb + 1) * TB],
                                           strips[:, soff + c0:soff + c0 + 512], pvs, [bsl, b_vcall])
                            def evac_c(banks=banks, mp=mp):
                                for hf in range(2):
                                    bk = banks[hf]
                                    op("act", (lambda e, bk=bk, hf=hf, mp=mp: e.activation(out=stage_c[:, 2 * hf:2 * hf + 2, mp, :], in_=psb[bk][:, 0:258].rearrange("p (q c) -> p q c", c=129), func=AF.Copy)),
                                       reads=[b_ps[bk]], writes=[b_stc])
                            lag15.push(evac_c, main=False)
                        def norm_c(h=h, qb=qb):
                            rz = zz[:, 16:24].rearrange("p (q m) -> p q m", m=2)
                            op("dve", lambda e: e.reciprocal(out=rz, in_=stage_c[:, :, :, 128]), reads=[b_stc], writes=[b_zz])
                            op("dve", lambda e: e.tensor_scalar(out=zz[:, 24:28], in0=rz[:, :, 1], scalar1=lamt[:, 4:5], scalar2=None, op0=ALU.mult), reads=[b_zz, b_lam], writes=[b_zz])
                            op("dve", lambda e: e.tensor_tensor(out=o_t[:, :, :], in0=stage_c[:, :, 0, 0:128], in1=rz[:, :, 0].unsqueeze(2).to_broadcast([128, 4, 128]), op=ALU.mult),
                               reads=[b_stc, b_zz], writes=[b_ot])
                            op("dve", lambda e: e.tensor_tensor(out=t_t[:, :, :], in0=stage_c[:, :, 1, 0:128], in1=zz[:, 24:28].unsqueeze(2).to_broadcast([128, 4, 128]), op=ALU.mult),
                               reads=[b_stc, b_zz], writes=[b_tt])
                            op("dve", lambda e: e.tensor_tensor(out=o_t[:, :, :], in0=o_t[:, :, :], in1=t_t[:, :, :], op=ALU.add), reads=[b_ot, b_tt], writes=[b_ot])
                            op("act", lambda e: e.activation(out=q_t[:, :, :], in_=o_t[:, :, :], func=AF.Square), reads=[b_ot], writes=[b_qt])
                            op("dve", lambda e: e.tensor_reduce(out=zz[:, 28:32], in_=q_t[:, :, :], axis=AX.X, op=ALU.add), reads=[b_qt], writes=[b_zz])
                            op("act", lambda e: e.activation(out=zz[:, 28:32], in_=zz[:, 28:32], func=AF.Sqrt, scale=1.0 / 128, bias=epsr[:, 0:1]), reads=[b_zz, b_eps], writes=[b_zz])
                            op("dve", lambda e: e.reciprocal(out=zz[:, 28:32], in_=zz[:, 28:32]), reads=[b_zz], writes=[b_zz])
                            op("dve", lambda e: e.tensor_tensor(out=o_t[:, :, :], in0=o_t[:, :, :], in1=zz[:, 28:32].unsqueeze(2).to_broadcast([128, 4, 128]), op=ALU.mult),
                               reads=[b_ot, b_zz], writes=[b_ot])
                            op("dve", lambda e: e.tensor_tensor(out=co_tm[:, :, :], in0=o_t[:, :, :], in1=sgt[:, :].unsqueeze(1).to_broadcast([128, 4, 128]), op=ALU.mult),
                               reads=[b_ot, b_sgt], writes=[b_cotm])
                            pv = bf(psb[7][:, :])
                            for qs in range(4):
                                op("pe", (lambda e, qs=qs: e.transpose(pv[:, qs * 128:(qs + 1) * 128], co_tm[:, qs, :], ident[:, :])), reads=[b_cotm, b_ident], writes=[b_ps[7]])
                            op("act", lambda e: e.activation(out=coT_st[:, :], in_=pv[:, 0:512], func=AF.Copy), reads=[b_ps[7]], writes=[b_coT])
                            op("pool", (lambda e, h=h, qb=qb: e.dma_start(out=COT[si][h, :, qb * TB:(qb + 1) * TB], in_=coT_st[:, :])), reads=[b_coT], writes=[b_COT], dma="d_coT")
                        lag15.push(norm_c, main=False)
                lag15.flush()
                S.barrier()

                al = Alloc()
                xt = [al([D], F32) for _ in range(4)]
                hb = [al([D], BF16) for _ in range(2)]
                hT = al([8, TB], BF16)
                g1t = al([D], F32)
                g2t = al([D], F32)
                junk = al([D], F32)
                lng = al([512], F32); lnb = al([512], F32)
                sgb = al([4, 128], F32)
                wgn = al([8, 128], F32)
                wgnb = al([8, 128], BF16)
                wgT = al([8, 128], BF16)
                gv = [al([512], F32) for _ in range(2)]
                vn = al([4, 512], BF16)
                uT = al([4, TB], BF16)
                tmpa = [al([TB], F32) for _ in range(2)]
                aT = al([4, TB], BF16)
                bcT = al([7, TB], BF16)
                sg = [al([TB], F32) for _ in range(3)]
                mm_ = [al([TB], F32) for _ in range(2)]
                mT = al([8, TB], BF16)
                sgf = [al([TB], BF16) for _ in range(3)]
                actT = al([22, TB], BF16)
                bst = al([8], F32)
                ring_n = 5
                SLOTB = 11 * 512
                ring = [al([SLOTB], BF16) for _ in range(ring_n)]
                b_xt = [S.buf(f"xt{t}") for t in range(4)]
                b_hb = [S.buf(f"hb{t}") for t in range(2)]
                b_hT = S.buf("hT"); b_g1 = S.buf("g1"); b_g2 = S.buf("g2"); b_junk = S.buf("junk")
                b_ss = [S.buf(f"ss{t}") for t in range(4)]
                b_ln = S.buf("ln"); b_sgb = S.buf("sgb"); b_wg = S.buf("wg"); b_wgT = S.buf("wgT")
                b_gv = [S.buf("gv0"), S.buf("gv1")]; b_vn = [S.buf(f"vn{t}") for t in range(4)]; b_uT = S.buf("uT")
                b_tmpa = [S.buf("tmpa0"), S.buf("tmpa1")]; b_aT = S.buf("aT"); b_bcT = S.buf("bcT")
                b_sg = [S.buf(f"sg{i}") for i in range(3)]; b_mm = [S.buf("mm0"), S.buf("mm1")]; b_mT = S.buf("mT")
                b_sgf = [S.buf(f"sgf{i}") for i in range(3)]; b_actT = S.buf("actT"); b_bst = [S.buf(f"bst{t}") for t in range(4)]
                b_ring = [S.buf(f"ring{i}") for i in range(ring_n)]
                b_BOT = S.buf("BOTd2"); b_COT = S.buf("COTd2")

                op("sp", (lambda e, l=l: e.dma_start(out=g1t[:, :], in_=norm1_g[l:l + 1, :].partition_broadcast(128))), writes=[b_g1], dma="d_g1")
                op("sp", (lambda e, l=l: e.dma_start(out=g2t[:, :], in_=norm2_g[l:l + 1, :].partition_broadcast(128))), writes=[b_g2], dma="d_g2")
                op("sp", (lambda e, l=l: e.dma_start(out=lng[:, :], in_=sgu_ln_g[l:l + 1, :].partition_broadcast(128))), writes=[b_ln], dma="d_ln")
                op("sp", (lambda e, l=l: e.dma_start(out=lnb[:, :], in_=sgu_ln_b[l:l + 1, :].partition_broadcast(128))), writes=[b_ln], dma="d_ln")
                for par in range(2):
                    src = bass.AP(sgu_b.tensor, l * 8 * 128 + par * 128, [[0, 64], [256, 4], [1, 128]])
                    op("sp", (lambda e, par=par, src=src: e.dma_start(out=sgb[par * 64:(par + 1) * 64, :, :], in_=src)), writes=[b_sgb], dma="d_sgb")
                op("sp", (lambda e, l=l: e.dma_start(out=wgn[:, :, :], in_=sgu_w[l, :, :, :].rearrange("g p q -> p g q"))), writes=[b_wg], dma="d_wgn")
                op("dve", lambda e: e.tensor_copy(out=wgnb[:, :, :], in_=wgn[:, :, :]), reads=[b_wg], writes=[b_wg])
                pv = bf(psb[7][:, :])
                for g in range(8):
                    op("pe", (lambda e, g=g: e.transpose(pv[:, g * 128:(g + 1) * 128], wgnb[:, g, :], ident[:, :])), reads=[b_wg, b_ident], writes=[b_ps[7]])
                op("dve", lambda e: e.tensor_copy(out=wgT[:, :, :], in_=pv.rearrange("p (g q) -> p g q", q=128)), reads=[b_ps[7]], writes=[b_wgT])

                pieces = []
                for tb in range(NB):
                    pieces.append([(("in", l), 8, 512, lambda l=l: wb_in[l, :, 512:1024], 0)])
                    pieces.append([(("in", l), 8, 512, lambda l=l: wb_in[l, :, 0:512], 0)])
                    for hf in range(2):
                        c0 = hf * 512
                        pieces.append([(("pa", l), 4, 512, lambda l=l, c0=c0: wb_pa[l, :, c0:c0 + 512], 0),
                                       (("pb", l), 3, 512, lambda l=l, c0=c0: wb_pb[l, :, c0:c0 + 512], 4),
                                       (("pc", l), 4, 512, lambda l=l, c0=c0: wb_pc[l, :, c0:c0 + 512], 7)])
                        for gi in range(3):
                            cc = 3712 + gi * 1024 + c0
                            pieces.append([(("in", l), 8, 512, lambda l=l, cc=cc: wb_in[l, :, cc:cc + 512], 0)])
                    for hf in range(2):
                        pieces.append([(("o", l), 8, 512, lambda l=l, hf=hf: wb_o[l, :, hf * 512:(hf + 1) * 512], 0)])
                    for f in range(11):
                        pieces.append([(("gu", l), 8, 256, lambda l=l, f=f: wb_gu[l, :, f * 256:(f + 1) * 256], 0),
                                       (("gu", l), 8, 256, lambda l=l, f=f: wb_gu[l, :, DFF + f * 256:DFF + (f + 1) * 256], 8)])
                    for hf in range(2):
                        for kh in range(2):
                            pieces.append([(("dn", l), 11, 512, lambda l=l, hf=hf, kh=kh: wb_dn[l, kh * 1408:(kh + 1) * 1408, hf * 512:(hf + 1) * 512], 0)])
                wstate = {"next_load": 0, "next_acq": 0, "held": []}

                def w_issue():
                    i = wstate["next_load"]
                    if i >= len(pieces):
                        return
                    slot = i % ring_n
                    for (wkey, nk, ncol, srcf, k0) in pieces[i]:
                        dst = ring[slot][:, k0 * ncol:(k0 + nk) * ncol].rearrange("p (k n) -> p k n", n=ncol)
                        src = srcf().rearrange("(k p) n -> p k n", p=128)
                        op("sp", (lambda e, dst=dst, src=src: e.dma_start(out=dst, in_=src)), reads=[b_wb[wkey]], writes=[b_ring[slot]], dma=f"d_ring{slot}")
                    wstate["next_load"] += 1

                def w_acquire():
                    i = wstate["next_acq"]
                    wstate["next_acq"] += 1
                    assert i < wstate["next_load"], "weight ring underflow"
                    slot = i % ring_n
                    return ring[slot], b_ring[slot]

                def w_release():
                    w_issue()

                for _ in range(ring_n):
                    w_issue()

                for tb in range(NB):
                    s0 = tb * TB
                    for t in range(4):
                        norm_tile(t, s0 + t * 128, xsrc[gidx], g1t, b_g1, 6 + (t % 2))
                    op("pool", (lambda e, s0=s0: e.dma_start(out=bcT[:, 0:3, :], in_=BOT[si][:, :, s0:s0 + TB].rearrange("c p s -> p c s"))), reads=[b_BOT], writes=[b_bcT], dma="d_bcT")
                    op("pool", (lambda e, s0=s0: e.dma_start(out=bcT[:, 3:7, :], in_=COT[si][:, :, s0:s0 + TB].rearrange("c p s -> p c s"))), reads=[b_COT], writes=[b_bcT], dma="d_bcT")
                    wv, bwv = w_acquire()
                    wv3 = wv[:, 0:8 * 512].rearrange("p (k n) -> p k n", n=512)
                    for t in range(4):
                        pb_i = nxt()
                        for k in range(8):
                            op("pe", (lambda e, k=k, t=t, pb_i=pb_i: e.matmul(psb[pb_i][:, :], hT[:, k, t * 128:(t + 1) * 128], wv3[:, k, :], start=(k == 0), stop=(k == 7))),
                               reads=[b_hT, bwv], writes=[b_ps[pb_i]])
                        j = t % 2
                        op("act", (lambda e, pb_i=pb_i, j=j: e.activation(out=gv[j][:, :], in_=psb[pb_i][:, :], func=AF.Gelu_apprx_tanh)), reads=[b_ps[pb_i]], writes=[b_gv[j]])
                        op("dve", (lambda e, j=j, t=t: e.bn_stats(out=bst[:, 0:6], in_=gv[j][:, :])), reads=[b_gv[j]], writes=[b_bst[0]])
                        op("dve", (lambda e, t=t: e.bn_aggr(out=small[:, 8 + 2 * t:10 + 2 * t], in_=bst[:, 0:6])), reads=[b_bst[0]], writes=[b_bst[1]])
                        op("act", (lambda e, t=t: e.activation(out=small[:, 9 + 2 * t:10 + 2 * t], in_=small[:, 9 + 2 * t:10 + 2 * t], func=AF.Sqrt, scale=1.0, bias=epsr[:, 1:2])),
                           reads=[b_bst[1], b_eps], writes=[b_bst[1]])
                        op("dve", (lambda e, t=t: e.reciprocal(out=small[:, 9 + 2 * t:10 + 2 * t], in_=small[:, 9 + 2 * t:10 + 2 * t])), reads=[b_bst[1]], writes=[b_bst[1]])
                        op("dve", (lambda e, j=j, t=t: e.tensor_scalar(out=gv[j][:, :], in0=gv[j][:, :], scalar1=small[:, 8 + 2 * t:9 + 2 * t], scalar2=small[:, 9 + 2 * t:10 + 2 * t],
                                                                    op0=ALU.subtract, op1=ALU.mult)), reads=[b_gv[j], b_bst[1]], writes=[b_gv[j]])
                        op("dve", (lambda e, j=j: e.tensor_tensor(out=gv[j][:, :], in0=gv[j][:, :], in1=lng[:, :], op=ALU.mult)), reads=[b_gv[j], b_ln], writes=[b_gv[j]])
                        op("dve", (lambda e, j=j, t=t: e.tensor_tensor(out=vn[:, t, :], in0=gv[j][:, :], in1=lnb[:, :], op=ALU.add)), reads=[b_gv[j], b_ln], writes=[b_vn[t]])
                    w_release()
                    wu, bwu = w_acquire()
                    wu3 = wu[:, 0:8 * 512].rearrange("p (k n) -> p k n", n=512)
                    for c in range(4):
                        pb_i = nxt()
                        for k in range(8):
                            op("pe", (lambda e, k=k, c=c, pb_i=pb_i: e.matmul(psb[pb_i][:, :], wu3[:, k, c * 128:(c + 1) * 128], hT[:, k, :], start=(k == 0), stop=(k == 7))),
                               reads=[b_hT, bwu], writes=[b_ps[pb_i]])
                        op("act", (lambda e, c=c, pb_i=pb_i: e.activation(out=uT[:, c, :], in_=psb[pb_i][:, :], func=AF.Gelu_apprx_tanh)), reads=[b_ps[pb_i]], writes=[b_uT])
                    w_release()
                    for j in range(4):
                        pa_i = nxt(); pb_i = nxt()
                        for t in range(4):
                            op("pe", (lambda e, j=j, t=t, pa_i=pa_i: e.matmul(psb[pa_i][:, t * 128:(t + 1) * 128], vn[:, t, j * 128:(j + 1) * 128], wgT[:, 2 * j, :], start=True, stop=True)),
                               reads=[b_vn[t], b_wgT], writes=[b_ps[pa_i]])
                        for t in range(4):
                            op("pe", (lambda e, j=j, t=t, pb_i=pb_i: e.matmul(psb[pb_i][:, t * 128:(t + 1) * 128], vn[:, t, j * 128:(j + 1) * 128], wgT[:, 2 * j + 1, :], start=True, stop=True)),
                               reads=[b_vn[t], b_wgT], writes=[b_ps[pb_i]])
                        jj = j % 2
                        op("dve", (lambda e, j=j, jj=jj, pa_i=pa_i: e.tensor_tensor(out=tmpa[jj][0:64, :].rearrange("p (t q) -> p t q", q=128),
                                                                              in0=psb[pa_i][0:64, :].rearrange("p (t q) -> p t q", q=128),
                                                                              in1=sgb[0:64, j, :].unsqueeze(1).to_broadcast([64, 4, 128]), op=ALU.add)),
                           reads=[b_ps[pa_i], b_sgb], writes=[b_tmpa[jj]])
                        op("dve", (lambda e, j=j, jj=jj, pb_i=pb_i: e.tensor_tensor(out=tmpa[jj][64:128, :].rearrange("p (t q) -> p t q", q=128),
                                                                              in0=psb[pb_i][64:128, :].rearrange("p (t q) -> p t q", q=128),
                                                                              in1=sgb[64:128, j, :].unsqueeze(1).to_broadcast([64, 4, 128]), op=ALU.add)),
                           reads=[b_ps[pb_i], b_sgb], writes=[b_tmpa[jj]])
                        op("dve", (lambda e, j=j, jj=jj: e.tensor_tensor(out=aT[:, j, :], in0=tmpa[jj][:, :], in1=uT[:, j, :], op=ALU.mult)), reads=[b_tmpa[jj], b_uT], writes=[b_aT])
                    for hf in range(2):
                        wp, bwp = w_acquire()
                        wp3 = wp[:, 0:11 * 512].rearrange("p (k n) -> p k n", n=512)
                        wg_ = []
                        for gi in range(3):
                            w_, bw_ = w_acquire()
                            wg_.append((w_[:, 0:8 * 512].rearrange("p (k n) -> p k n", n=512), bw_))
                        for oc in range(4):
                            cs = slice(oc * 128, (oc + 1) * 128)
                            gbanks = []
                            for gi in range(3):
                                pg = nxt()
                                gbanks.append(pg)
                                w3, bw3 = wg_[gi]
                                for k in range(8):
                                    op("pe", (lambda e, k=k, pg=pg, w3=w3, cs=cs: e.matmul(psb[pg][:, :], w3[:, k, cs], hT[:, k, :], start=(k == 0), stop=(k == 7))),
                                       reads=[b_hT, bw3], writes=[b_ps[pg]])
                                op("act", (lambda e, gi=gi, pg=pg: e.activation(out=sg[gi][:, :], in_=psb[pg][:, :], func=AF.Sigmoid)), reads=[b_ps[pg]], writes=[b_sg[gi]])
                            pbanks = []
                            for bi_, (k0, nk, srcT, koff, bsrc) in enumerate(((0, 4, aT, 0, b_aT), (4, 3, bcT, 0, b_bcT), (7, 4, bcT, 3, b_bcT))):
                                pp = nxt()
                                pbanks.append(pp)
                                for k in range(nk):
                                    op("pe", (lambda e, k=k, pp=pp, k0=k0, nk=nk, srcT=srcT, koff=koff, cs=cs: e.matmul(
                                        psb[pp][:, :], wp3[:, k0 + k, cs], srcT[:, koff + k, :], start=(k == 0), stop=(k == nk - 1))),
                                       reads=[bsrc, bwp], writes=[b_ps[pp]])
                            op("dve", (lambda e, pbanks=pbanks: e.tensor_tensor(out=mm_[0][:, :], in0=psb[pbanks[0]][:, :], in1=sg[0][:, :], op=ALU.mult)),
                               reads=[b_ps[pbanks[0]], b_sg[0]], writes=[b_mm[0]])
                            op("dve", (lambda e, pbanks=pbanks: e.tensor_tensor(out=mm_[1][:, :], in0=psb[pbanks[1]][:, :], in1=sg[1][:, :], op=ALU.mult)),
                               reads=[b_ps[pbanks[1]], b_sg[1]], writes=[b_mm[1]])
                            op("dve", lambda e: e.tensor_tensor(out=mm_[0][:, :], in0=mm_[0][:, :], in1=mm_[1][:, :], op=ALU.add), reads=[b_mm[0], b_mm[1]], writes=[b_mm[0]])
                            op("dve", (lambda e, pbanks=pbanks: e.tensor_tensor(out=mm_[1][:, :], in0=psb[pbanks[2]][:, :], in1=sg[2][:, :], op=ALU.mult)),
                               reads=[b_ps[pbanks[2]], b_sg[2]], writes=[b_mm[1]])
                            op("dve", (lambda e, hf=hf, oc=oc: e.tensor_tensor(out=mT[:, hf * 4 + oc, :], in0=mm_[0][:, :], in1=mm_[1][:, :], op=ALU.add)),
                               reads=[b_mm[0], b_mm[1]], writes=[b_mT])
                        for _ in range(4):
                            w_release()
                    for hf in range(2):
                        wo, bwo = w_acquire()
                        wo3 = wo[:, 0:8 * 512].rearrange("p (k n) -> p k n", n=512)
                        for t in range(4):
                            pb_i = nxt()
                            for k in range(8):
                                op("pe", (lambda e, k=k, t=t, pb_i=pb_i, wo3=wo3: e.matmul(psb[pb_i][:, :], mT[:, k, t * 128:(t + 1) * 128], wo3[:, k, :], start=(k == 0), stop=(k == 7))),
                                   reads=[b_mT, bwo], writes=[b_ps[pb_i]])
                            op("dve", (lambda e, t=t, hf=hf, pb_i=pb_i: e.tensor_tensor(out=xt[t][:, hf * 512:(hf + 1) * 512], in0=xt[t][:, hf * 512:(hf + 1) * 512], in1=psb[pb_i][:, :], op=ALU.add)),
                               reads=[b_ps[pb_i], b_xt[t]], writes=[b_xt[t]])
                        w_release()
                    for t in range(4):
                        hbi = t % 2
                        trbank = 6 + (t % 2)
                        op("act", (lambda e, t=t: e.activation(out=junk[:, :], in_=xt[t][:, :], func=AF.Square, accum_out=small[:, t:t + 1])), reads=[b_xt[t]], writes=[b_junk, b_ss[t]])
                        op("act", (lambda e, t=t: e.activation(out=small[:, t:t + 1], in_=small[:, t:t + 1], func=AF.Sqrt, scale=1.0 / D, bias=epsr[:, 0:1])), reads=[b_ss[t], b_eps], writes=[b_ss[t]])
                        op("dve", (lambda e, t=t: e.reciprocal(out=small[:, t:t + 1], in_=small[:, t:t + 1])), reads=[b_ss[t]], writes=[b_ss[t]])
                        op("dve", (lambda e, t=t, hbi=hbi: e.scalar_tensor_tensor(out=hb[hbi][:, :], in0=xt[t][:, :], scalar=small[:, t:t + 1], in1=g2t[:, :], op0=ALU.mult, op1=ALU.mult)),
                           reads=[b_xt[t], b_ss[t], b_g2], writes=[b_hb[hbi]])
                        pv = bf(psb[trbank][:, :])
                        for c in range(8):
                            op("pe", (lambda e, c=c, pv=pv, hbi=hbi: e.transpose(pv[:, c * 128:(c + 1) * 128], hb[hbi][:, c * 128:(c + 1) * 128], ident[:, :])),
                               reads=[b_hb[hbi], b_ident], writes=[b_ps[trbank]])
                        op("act", (lambda e, t=t, pv=pv: e.activation(out=hT[:, :, t * 128:(t + 1) * 128], in_=pv.rearrange("p (c q) -> p c q", q=128), func=AF.Copy)),
                           reads=[b_ps[trbank]], writes=[b_hT])
                    for f in range(11):
                        wgu, bwgu = w_acquire()
                        wgu3 = wgu[:, 0:16 * 256].rearrange("p (k n) -> p k n", n=256)
                        for ch in range(2):
                            pg = nxt(); pu = nxt()
                            for k in range(8):
                                op("pe", (lambda e, k=k, ch=ch, pg=pg: e.matmul(psb[pg][:, :], wgu3[:, k, ch * 128:(ch + 1) * 128], hT[:, k, :], start=(k == 0), stop=(k == 7))),
                                   reads=[b_hT, bwgu], writes=[b_ps[pg]])
                            for k in range(8):
                                op("pe", (lambda e, k=k, ch=ch, pu=pu: e.matmul(psb[pu][:, :], wgu3[:, 8 + k, ch * 128:(ch + 1) * 128], hT[:, k, :], start=(k == 0), stop=(k == 7))),
                                   reads=[b_hT, bwgu], writes=[b_ps[pu]])
                            sj = (f * 2 + ch) % 3
                            op("act", (lambda e, pg=pg, sj=sj: e.activation(out=sgf[sj][:, :], in_=psb[pg][:, :], func=AF.Silu)), reads=[b_ps[pg]], writes=[b_sgf[sj]])
                            op("dve", (lambda e, pu=pu, sj=sj, f=f, ch=ch: e.tensor_tensor(out=actT[:, f * 2 + ch, :], in0=psb[pu][:, :], in1=sgf[sj][:, :], op=ALU.mult)),
                               reads=[b_ps[pu], b_sgf[sj]], writes=[b_actT])
                        w_release()
                    for hf in range(2):
                        wda, bwda = w_acquire()
                        wdb, bwdb = w_acquire()
                        wd3 = [wda[:, 0:11 * 512].rearrange("p (k n) -> p k n", n=512), wdb[:, 0:11 * 512].rearrange("p (k n) -> p k n", n=512)]
                        for t in range(4):
                            pb_i = nxt()
                            for k in range(22):
                                op("pe", (lambda e, k=k, t=t, pb_i=pb_i, wd3=wd3: e.matmul(psb[pb_i][:, :], actT[:, k, t * 128:(t + 1) * 128], wd3[k // 11][:, k % 11, :], start=(k == 0), stop=(k == 21))),
                                   reads=[b_actT, bwda, bwdb], writes=[b_ps[pb_i]])
                            op("dve", (lambda e, t=t, hf=hf, pb_i=pb_i: e.tensor_tensor(out=xt[t][:, hf * 512:(hf + 1) * 512], in0=xt[t][:, hf * 512:(hf + 1) * 512], in1=psb[pb_i][:, :], op=ALU.add)),
                               reads=[b_ps[pb_i], b_xt[t]], writes=[b_xt[t]])
                        w_release()
                        w_release()
                    for t in range(4):
                        tok0 = s0 + t * 128
                        op("pool", (lambda e, t=t, tok0=tok0: e.dma_start(out=ydst[gidx, tok0:tok0 + 128, :], in_=xt[t][:, :])), reads=[b_xt[t]], dma=f"d_x{t}")
                S.barrier()
        S.barrier()
        S.emit()
    return nc


N_CORES = 8
_PROGRAM_CACHE = {}


def _lam_inits(depth):
    return [0.8 - 0.6 * math.exp(-0.3 * l) for l in range(depth)]


def kernel(x_prompt, x_sample, rel_bias, norm1_g, w_in, sgu_ln_g, sgu_ln_b, sgu_w, sgu_b,
           qn_b, kn_b, qn_c, kn_c, lam_q1, lam_k1, lam_q2, lam_k2, subln_g,
           w_pa, w_pb, w_pc, w_o, norm2_g, w_gu, w_down):
    x_prompt = np.asarray(x_prompt, dtype=np.float32)
    x_sample = np.asarray(x_sample, dtype=np.float32)
    depth = int(np.asarray(w_in).shape[0])
    nb_p, s_p = x_prompt.shape[0], x_prompt.shape[1]
    nb_s, s_s = x_sample.shape[0], x_sample.shape[1]
    ncores = N_CORES
    assert nb_p % ncores == 0 and nb_s % ncores == 0
    pp, ps_ = nb_p // ncores, nb_s // ncores
    seq_lens = [("p", i, s_p) for i in range(pp)] + [("s", i, s_s) for i in range(ps_)]
    key = (tuple(seq_lens), depth)
    if key not in _PROGRAM_CACHE:
        _PROGRAM_CACHE[key] = build_program(seq_lens, depth, _lam_inits(depth))
    nc = _PROGRAM_CACHE[key]
    oh, mask, ident = host_constants()
    shared = dict(rel_bias=rel_bias, norm1_g=norm1_g, norm2_g=norm2_g, w_in=w_in, sgu_ln_g=sgu_ln_g, sgu_ln_b=sgu_ln_b,
                  sgu_w=sgu_w, sgu_b=sgu_b, qn_b=qn_b, kn_b=kn_b, qn_c=qn_c, kn_c=kn_c, lam_q1=lam_q1, lam_k1=lam_k1,
                  lam_q2=lam_q2, lam_k2=lam_k2, subln_g=subln_g, w_pa=w_pa, w_pb=w_pb, w_pc=w_pc, w_o=w_o, w_gu=w_gu,
                  w_down=w_down, c_oh=oh, c_mask=mask, c_ident=ident)
    shared = {k: np.ascontiguousarray(np.asarray(v, dtype=np.float32)) for k, v in shared.items()}
    in_maps = []
    for c in range(ncores):
        m = dict(shared)
        m["xp"] = np.ascontiguousarray(x_prompt[c * pp:(c + 1) * pp])
        m["xs"] = np.ascontiguousarray(x_sample[c * ps_:(c + 1) * ps_])
        in_maps.append(m)
    res = run_bass_kernel_spmd(nc, in_maps, core_ids=list(range(ncores)))
    yp = np.concatenate([np.asarray(r["yp"]) for r in res.results], axis=0).astype(np.float32)
    ys = np.concatenate([np.asarray(r["ys"]) for r in res.results], axis=0).astype(np.float32)
    return (yp, ys)
```

```python
import contextlib
import math
import numpy as np
import ml_dtypes
import concourse.bass as bass
import concourse.mybir as mybir
from concourse.bass_utils import run_bass_kernel_spmd

F32 = mybir.dt.float32
BF16 = mybir.dt.bfloat16
AF = mybir.ActivationFunctionType
ALU = mybir.AluOpType
AX = mybir.AxisListType

D = 1024
INC = 6784
DFF = 2816
NBUCK = 32
TB = 512
RMS_EPS = 1e-6
LN_EPS = 1e-5
GEO = {"c": (-768, 1152, 1, None), "b0": (-128, 512, 1, 64), "b1": (-256, 640, 4, 256), "b2": (-1024, 1408, 16, 1024)}
GEO_ORDER = ["c", "b0", "b1", "b2"]
GEO_NH = {"c": 4, "b0": 2, "b1": 2, "b2": 2}


def geo_W(g):
    return GEO[g][1] - GEO[g][0] + 512


def geo_L(g):
    return geo_W(g) + 127


STRIP_OFF = {}
_o = 0
for _g in GEO_ORDER:
    STRIP_OFF[_g] = _o
    _o += GEO_NH[_g] * geo_W(_g)
STRIP_COLS = _o
LMAX = max(geo_L(g) for g in GEO_ORDER)


def _rel_bucket_np(rel):
    nb = NBUCK // 2
    max_exact = nb // 2
    n = np.abs(rel)
    sign_off = np.where(rel > 0, nb, 0)
    nf = np.maximum(n, 1).astype(np.float32)
    large = max_exact + (np.log(nf / np.float32(max_exact)) / np.float32(math.log(1024 / max_exact))
                         * np.float32(nb - max_exact)).astype(np.int32)
    large = np.minimum(large, nb - 1)
    return sign_off + np.where(n < max_exact, n, large)


def host_constants():
    oh = np.zeros((4, NBUCK, LMAX), np.float32)
    mask = np.zeros((4, LMAX), np.float32)
    for gi, g in enumerate(GEO_ORDER):
        dmin, dmax, dil, hw = GEO[g]
        L = geo_L(g)
        rhi = dmax + 127
        rel = rhi - np.arange(L)
        b = _rel_bucket_np(rel)
        oh[gi, b, np.arange(L)] = 1.0
        if hw is None:
            mask[gi, :L] = 1.0
        else:
            mask[gi, :L] = ((rel % dil == 0) & (np.abs(rel) <= hw)).astype(np.float32)
    ident = np.eye(128, dtype=np.float32)
    return oh, mask, ident


class Buf:
    __slots__ = ("name", "w", "r")

    def __init__(self, name):
        self.name = name
        self.w = None
        self.r = {}


class _Rec:
    def __init__(self):
        self.call = None

    def __getattr__(self, name):
        def f(*a, **k):
            self.call = (name, a, k)
            return self
        return f


class Sched:
    COMPUTE = ("pe", "act", "dve", "pool")

    def __init__(self, nc, stack):
        self.nc = nc
        self.stack = stack
        self.q = {k: [] for k in ("pe", "act", "dve", "pool", "sp")}
        self.sems = {}
        self.tick = {}
        self.seen = {k: {} for k in self.q}
        for k in self.COMPUTE:
            self._sem(k)
        self.bufs = []

    def _sem(self, key):
        if key not in self.sems:
            self.sems[key] = self.stack.enter_context(self.nc.semaphore("s_" + key))
            self.tick[key] = 0
        return self.sems[key]

    def buf(self, name):
        b = Buf(name)
        self.bufs.append(b)
        return b

    def op(self, eng, fn, reads=(), writes=(), dma=None):
        deps = {}
        for b in reads:
            if b.w is not None:
                k, t = b.w
                if deps.get(k, 0) < t:
                    deps[k] = t
        for b in writes:
            if b.w is not None:
                k, t = b.w
                if deps.get(k, 0) < t:
                    deps[k] = t
            for k, t in b.r.items():
                if deps.get(k, 0) < t:
                    deps[k] = t
        waits = []
        seen = self.seen[eng]
        for k, t in deps.items():
            if dma is None and k == eng:
                if eng == "pe" or t < self.tick[eng] - 1:
                    continue
                if seen.get(k, 0) >= t:
                    continue
            elif seen.get(k, 0) >= t:
                continue
            seen[k] = t
            waits.append((k, t))
        if dma is not None:
            self._sem(dma)
            self.tick[dma] += 16
            ev = (dma, self.tick[dma])
            inc = (dma, 16)
        else:
            self.tick[eng] += 1
            ev = (eng, self.tick[eng])
            inc = (eng, 1)
        rec = _Rec()
        fn(rec)
        assert rec.call is not None
        self.q[eng].append((waits, rec.call, inc))
        for b in writes:
            b.w = ev
            b.r = {}
        for b in reads:
            if b in writes:
                continue
            k, t = ev
            if b.r.get(k, 0) < t:
                b.r[k] = t
        return ev

    def barrier(self):
        for eng in self.q:
            waits = []
            seen = self.seen[eng]
            for k, t in self.tick.items():
                if t > 0 and seen.get(k, 0) < t and k != eng:
                    seen[k] = t
                    waits.append((k, t))
            if waits:
                self.q[eng].append((waits, None, None))
        for b in self.bufs:
            b.w = None
            b.r = {}

    def emit(self):
        nc = self.nc
        sems = self.sems

        def run(e, items):
            for waits, fn, inc in items:
                for k, t in waits:
                    e.wait_ge(sems[k], t)
                if fn is not None:
                    name, a, k = fn
                    getattr(e, name)(*a, **k).then_inc(sems[inc[0]], inc[1])

        with nc.Block() as block:
            @block.tensor
            def _(e):
                run(e, self.q["pe"])

            @block.scalar
            def _(e):
                run(e, self.q["act"])

            @block.vector
            def _(e):
                run(e, self.q["dve"])

            @block.gpsimd
            def _(e):
                run(e, self.q["pool"])

            @block.sync
            def _(e):
                run(e, self.q["sp"])


class Lag:
    def __init__(self, la):
        self.la = la
        self.q = []

    def push(self, fn, main=True):
        self.q.append((fn, main))
        if main:
            while sum(1 for _, m in self.q if m) > self.la:
                f, _ = self.q.pop(0)
                f()
            while self.q and not self.q[0][1]:
                f, _ = self.q.pop(0)
                f()

    def flush(self):
        while self.q:
            f, _ = self.q.pop(0)
            f()


def build_program(seq_lens, depth, lam_inits):
    nc = bass.Bass("TRN2", target_bir_lowering=False)
    n_p = sum(1 for g, _, _ in seq_lens if g == "p")
    n_s = sum(1 for g, _, _ in seq_lens if g == "s")
    S_p = max([s for g, _, s in seq_lens if g == "p"], default=128)
    S_s = max([s for g, _, s in seq_lens if g == "s"], default=128)
    SMAX = max(s for _, _, s in seq_lens)
    L = depth

    def din(name, shape, dt=F32):
        return nc.dram_tensor(name, list(shape), dt, kind="ExternalInput").ap()

    def dscr(name, shape, dt=BF16):
        return nc.dram_tensor(name, list(shape), dt, kind="Internal").ap()

    xin = {"p": din("xp", [max(n_p, 1), S_p, D]), "s": din("xs", [max(n_s, 1), S_s, D])}
    yout = {"p": nc.dram_tensor("yp", [max(n_p, 1), S_p, D], F32, kind="ExternalOutput").ap(),
            "s": nc.dram_tensor("ys", [max(n_s, 1), S_s, D], F32, kind="ExternalOutput").ap()}
    rel_bias = din("rel_bias", [NBUCK, 10])
    norm1_g = din("norm1_g", [L, D]); norm2_g = din("norm2_g", [L, D])
    w_in = din("w_in", [L, D, INC])
    sgu_ln_g = din("sgu_ln_g", [L, 512]); sgu_ln_b = din("sgu_ln_b", [L, 512])
    sgu_w = din("sgu_w", [L, 8, 128, 128]); sgu_b = din("sgu_b", [L, 8, 128])
    qn_b = din("qn_b", [L, 64]); kn_b = din("kn_b", [L, 64]); qn_c = din("qn_c", [L, 64]); kn_c = din("kn_c", [L, 64])
    lam_q1 = din("lam_q1", [L, 64]); lam_k1 = din("lam_k1", [L, 64]); lam_q2 = din("lam_q2", [L, 64]); lam_k2 = din("lam_k2", [L, 64])
    subln_g = din("subln_g", [L, 128])
    w_pa = din("w_pa", [L, 512, D]); w_pb = din("w_pb", [L, 384, D]); w_pc = din("w_pc", [L, 512, D])
    w_o = din("w_o", [L, D, D]); w_gu = din("w_gu", [L, D, 2 * DFF]); w_down = din("w_down", [L, DFF, D])
    c_oh = din("c_oh", [4, NBUCK, LMAX]); c_mask = din("c_mask", [4, LMAX]); c_ident = din("c_ident", [128, 128])

    wb_in = dscr("wb_in", [L, D, INC]); wb_pa = dscr("wb_pa", [L, 512, D]); wb_pb = dscr("wb_pb", [L, 384, D])
    wb_pc = dscr("wb_pc", [L, 512, D]); wb_o = dscr("wb_o", [L, D, D]); wb_gu = dscr("wb_gu", [L, D, 2 * DFF])
    wb_dn = dscr("wb_dn", [L, DFF, D])
    rrep = dscr("rrep", [10, 128, LMAX]); strd = dscr("strd", [128, STRIP_COLS])
    NSEQ = len(seq_lens)
    QT = [dscr(f"QT{i}", [7, 128, s]) for i, (_, _, s) in enumerate(seq_lens)]
    KT = [dscr(f"KT{i}", [7, 128, s]) for i, (_, _, s) in enumerate(seq_lens)]
    VB = [dscr(f"VB{i}", [s // 128, 128, 390]) for i, (_, _, s) in enumerate(seq_lens)]
    VC = [dscr(f"VC{i}", [s // 128, 128, 516]) for i, (_, _, s) in enumerate(seq_lens)]
    BOT = [dscr(f"BOT{i}", [3, 128, s]) for i, (_, _, s) in enumerate(seq_lens)]
    COT = [dscr(f"COT{i}", [4, 128, s]) for i, (_, _, s) in enumerate(seq_lens)]

    with contextlib.ExitStack() as st:
        S = Sched(nc, st)
        op = S.op

        def sbt(name, shape, dt=F32):
            return st.enter_context(nc.sbuf_tensor(name, list(shape), dt))

        ident = sbt("ident", [128, 128], BF16); b_ident = S.buf("ident")
        identf = sbt("identf", [128, 128], F32)
        epsr = sbt("epsr", [128, 2], F32); b_eps = S.buf("eps")
        small = sbt("small", [128, 64], F32)
        lamt = sbt("lamt", [128, 8], F32); b_lam = S.buf("lam")
        ksc = sbt("ksc", [128, 4], F32); b_ksc = S.buf("ksc")
        RBYTES = 204 * 1024
        R = sbt("R", [128, RBYTES // 2], BF16)
        psb = [st.enter_context(nc.psum_tensor(f"ps{i}", [128, 512], F32)) for i in range(8)]
        b_ps = [S.buf(f"ps{i}") for i in range(8)]

        class Alloc:
            def __init__(self):
                self.off = 0

            def __call__(self, shape, dt):
                n = int(np.prod(shape))
                size = 2 if dt == BF16 else 4
                nb = (n * size + 63) // 64 * 64
                off = self.off
                self.off += nb
                assert self.off <= RBYTES, ("region overflow", self.off)
                a = R[:, off // 2: off // 2 + n * size // 2]
                if dt == F32:
                    a = a.bitcast(F32)
                if len(shape) == 2:
                    a = a.rearrange("p (a b) -> p a b", b=shape[1])
                elif len(shape) == 3:
                    a = a.rearrange("p (a b c) -> p a b c", b=shape[1], c=shape[2])
                elif len(shape) == 4:
                    a = a.rearrange("p (a b c d) -> p a b c d", b=shape[1], c=shape[2], d=shape[3])
                return a

        def bf(psap):
            return psap.bitcast(BF16)

        op("sp", lambda e: e.dma_start(out=identf[:], in_=c_ident[:, :]), writes=[b_ident], dma="d_ident")
        op("dve", lambda e: e.tensor_copy(out=ident[:], in_=identf[:]), reads=[b_ident], writes=[b_ident])
        op("dve", lambda e: e.memset(epsr[:, 0:1], RMS_EPS), writes=[b_eps])
        op("dve", lambda e: e.memset(epsr[:, 1:2], LN_EPS), writes=[b_eps])

        b_wb = {}
        for l in range(L):
            for nm, src, dst, rows in (("in", w_in, wb_in, D), ("pa", w_pa, wb_pa, 512), ("pb", w_pb, wb_pb, 384),
                                       ("pc", w_pc, wb_pc, 512), ("o", w_o, wb_o, D), ("gu", w_gu, wb_gu, D),
                                       ("dn", w_down, wb_dn, DFF)):
                b = S.buf(f"wb_{nm}{l}")
                b_wb[(nm, l)] = b
                for r0 in range(0, rows, 128):
                    op("pool", (lambda e, src=src, dst=dst, l=l, r0=r0: e.dma_start(out=dst[l, r0:r0 + 128, :], in_=src[l, r0:r0 + 128, :])),
                       writes=[b], dma="d_cv")

        al = Alloc()
        tab = al([10], F32)
        tabb = al([128], F32)
        oht = al([LMAX], F32)
        mkt = al([LMAX], F32)
        rv = al([LMAX], BF16)
        evt = al([512], F32); b_evt = S.buf('evt')
        b_tab = S.buf("tab"); b_tabb = S.buf("tabb"); b_oht = S.buf("oht"); b_mkt = S.buf("mkt"); b_rv = S.buf("rv")
        b_rrep = S.buf("rrep"); b_strd = S.buf("strd")
        op("sp", lambda e: e.dma_start(out=tab[0:NBUCK, :], in_=rel_bias[:, :]), writes=[b_tab], dma="d_tab")
        strip_heads = {"c": [6, 7, 8, 9], "b0": [0, 1], "b1": [2, 3], "b2": [4, 5]}
        ri = 0
        for gi, g in enumerate(GEO_ORDER):
            Lg = geo_L(g); Wg = geo_W(g)
            op("sp", (lambda e, gi=gi, Lg=Lg: e.dma_start(out=oht[0:NBUCK, 0:Lg], in_=c_oh[gi, :, 0:Lg])), writes=[b_oht], dma="d_oht")
            op("sp", (lambda e, gi=gi, Lg=Lg: e.dma_start(out=mkt[:, 0:Lg], in_=c_mask[gi:gi + 1, 0:Lg].partition_broadcast(128))), writes=[b_mkt], dma="d_misc2")
            for hi, hcol in enumerate(strip_heads[g]):
                op("dve", (lambda e, hcol=hcol: e.tensor_copy(out=tabb[0:NBUCK, :], in_=tab[0:NBUCK, hcol:hcol + 1].to_broadcast([NBUCK, 128]))),
                   reads=[b_tab], writes=[b_tabb])
                for c0 in range(0, Lg, 512):
                    c1 = min(Lg, c0 + 512)
                    op("pe", (lambda e, c0=c0, c1=c1: e.matmul(psb[0][:, 0:c1 - c0], tabb[0:NBUCK, :], oht[0:NBUCK, c0:c1], start=True, stop=True)),
                       reads=[b_tabb, b_oht], writes=[b_ps[0]])
                    op("act", (lambda e, c0=c0, c1=c1: e.activation(out=evt[:, 0:c1 - c0], in_=psb[0][:, 0:c1 - c0], func=AF.Exp)),
                       reads=[b_ps[0]], writes=[b_evt])
                    op("dve", (lambda e, c0=c0, c1=c1: e.tensor_tensor(out=rv[:, c0:c1], in0=evt[:, 0:c1 - c0], in1=mkt[:, c0:c1], op=ALU.mult)),
                       reads=[b_evt, b_mkt], writes=[b_rv])
                op("sp", (lambda e, ri=ri, Lg=Lg: e.dma_start(out=rrep[ri, :, 0:Lg], in_=rv[:, 0:Lg])), reads=[b_rv], writes=[b_rrep], dma="d_rrep")
                soff = STRIP_OFF[g] + hi * Wg
                src = bass.AP(rrep.tensor, ri * 128 * LMAX + 127, [[LMAX - 1, 128], [1, Wg]])
                op("sp", (lambda e, soff=soff, Wg=Wg, src=src: e.dma_start(out=strd[:, soff:soff + Wg], in_=src)),
                   reads=[b_rrep], writes=[b_strd], dma="d_strd")
                ri += 1
        S.barrier()

        for si, (grp, gidx, SL) in enumerate(seq_lens):
            NT = SL // 128
            NB = SL // TB
            for l in range(L):
                xsrc = xin[grp] if l == 0 else yout[grp]
                ydst = yout[grp]

                al = Alloc()
                xt = [al([D], F32) for _ in range(4)]
                hb = [al([D], BF16) for _ in range(4)]
                hT = al([8, TB], BF16)
                g1t = al([D], F32)
                wqkv = al([8, 2688], BF16)
                junk = al([D], F32)
                sq = [al([512], F32) for _ in range(4)]
                qn = [al([512], BF16) for _ in range(4)]
                qst = al([7, TB], BF16)
                kst = al([7, TB], BF16)
                vbst = al([4, 6, 65], BF16)
                vcst = al([4, 4, 129], BF16)
                svs = [al([8], F32) for _ in range(4)]
                b_xt = [S.buf(f"xt{t}") for t in range(4)]
                b_hb = [S.buf(f"hb{t}") for t in range(4)]
                b_hT = S.buf("hT"); b_g1 = S.buf("g1"); b_wqkv = S.buf("wqkv"); b_junk = S.buf("junk")
                b_sq = [S.buf(f"sq{i}") for i in range(4)]; b_qn = [S.buf(f"qn{i}") for i in range(4)]
                b_qst = S.buf("qst"); b_kst = S.buf("kst"); b_vbst = S.buf("vbst"); b_vcst = S.buf("vcst")
                b_ss = [S.buf(f"ss{t}") for t in range(4)]
                b_qs = [S.buf(f"qs{i}") for i in range(4)]
                lag1 = Lag(2)
                qkc = [0]
                b_QT = S.buf("QTd"); b_KT = S.buf("KTd"); b_VB = S.buf("VBd"); b_VC = S.buf("VCd")
                b_BOT = S.buf("BOTd"); b_COT = S.buf("COTd")

                op("sp", (lambda e, l=l: e.dma_start(out=g1t[:, :], in_=norm1_g[l:l + 1, :].partition_broadcast(128))), writes=[b_g1], dma="d_g1")
                op("sp", (lambda e, l=l: e.dma_start(out=wqkv[:, :, :], in_=wb_in[l, :, 1024:3712].rearrange("(c p) n -> p c n", p=128))),
                   reads=[b_wb[("in", l)]], writes=[b_wqkv], dma="d_wqkv")
                for half in range(2):
                    for j, src in enumerate((qn_b, kn_b, qn_c, kn_c)):
                        op("sp", (lambda e, l=l, half=half, j=j, src=src: e.dma_start(
                            out=lamt[half * 64:(half + 1) * 64, j:j + 1], in_=src[l:l + 1, :].rearrange("o d -> d o"))),
                           writes=[b_lam], dma="d_lam")
                op("dve", lambda e: e.tensor_tensor(out=ksc[:, 0:1], in0=lamt[:, 0:1], in1=lamt[:, 1:2], op=ALU.mult), reads=[b_lam], writes=[b_ksc])
                op("dve", lambda e: e.tensor_tensor(out=ksc[:, 1:2], in0=lamt[:, 2:3], in1=lamt[:, 3:4], op=ALU.mult), reads=[b_lam], writes=[b_ksc])
                op("dve", lambda e: e.tensor_scalar(out=ksc[:, 0:2], in0=ksc[:, 0:2], scalar1=0.125, scalar2=None, op0=ALU.mult), reads=[b_ksc], writes=[b_ksc])
                op("dve", lambda e: e.memset(vbst[:, :, :, 64:65], 1.0), writes=[b_vbst])
                op("dve", lambda e: e.memset(vcst[:, :, :, 128:129], 1.0), writes=[b_vcst])

                rot = [0]

                def nxt(n=6):
                    i = rot[0] % n
                    rot[0] += 1
                    return i

                def norm_tile(t, tok0, src_ap, gt, b_g, trbank, ldq="pool", part="ab"):
                    hbi = t % 4
                    if "a" in part:
                        norm_a(t, tok0, src_ap, gt, b_g, ldq, hbi)
                    if "b" in part:
                        norm_b_(t, trbank, hbi)

                def norm_a(t, tok0, src_ap, gt, b_g, ldq, hbi):
                    op(ldq, (lambda e: e.dma_start(out=xt[t][:, :], in_=src_ap[tok0:tok0 + 128, :])), writes=[b_xt[t]], dma=f"d_x{t}")
                    op("act", (lambda e: e.activation(out=junk[:, :], in_=xt[t][:, :], func=AF.Square, accum_out=small[:, t:t + 1])),
                       reads=[b_xt[t]], writes=[b_junk, b_ss[t]])
                    op("act", (lambda e: e.activation(out=small[:, t:t + 1], in_=small[:, t:t + 1], func=AF.Sqrt, scale=1.0 / D, bias=epsr[:, 0:1])),
                       reads=[b_ss[t], b_eps], writes=[b_ss[t]])
                    op("dve", (lambda e: e.reciprocal(out=small[:, t:t + 1], in_=small[:, t:t + 1])), reads=[b_ss[t]], writes=[b_ss[t]])
                    op("dve", (lambda e: e.scalar_tensor_tensor(out=hb[hbi][:, :], in0=xt[t][:, :], scalar=small[:, t:t + 1], in1=gt[:, :],
                                                                 op0=ALU.mult, op1=ALU.mult)),
                       reads=[b_xt[t], b_ss[t], b_g], writes=[b_hb[hbi]])

                def norm_b_(t, trbank, hbi):
                    pv = bf(psb[trbank][:, :])
                    for c in range(8):
                        op("pe", (lambda e, c=c: e.transpose(pv[:, c * 128:(c + 1) * 128], hb[hbi][:, c * 128:(c + 1) * 128], ident[:, :])),
                           reads=[b_hb[hbi], b_ident], writes=[b_ps[trbank]])
                    op("act", (lambda e: e.activation(out=hT[:, :, t * 128:(t + 1) * 128], in_=pv.rearrange("p (c q) -> p c q", q=128), func=AF.Copy)),
                       reads=[b_ps[trbank]], writes=[b_hT])

                blocks = [("q", 0, 3, 0, 0), ("k", 384, 3, 0, 0), ("v", 768, 3, 0, 0),
                          ("q", 1152, 4, 3, 1), ("k", 1664, 4, 3, 1), ("v", 2176, 4, 3, 1)]
                for tb in range(NB):
                    for t in range(4):
                        norm_tile(t, tb * TB + t * 128, xsrc[gidx], g1t, b_g1, 6 + (t % 2), ldq="sp", part="a")
                    for t in range(4):
                        norm_tile(t, tb * TB + t * 128, xsrc[gidx], g1t, b_g1, 6 + (t % 2), ldq="sp", part="b")
                    for t in range(4):
                        for bi, (kind, c0, nch, ch0, isc) in enumerate(blocks):
                            ncol = nch * 128
                            pb_i = nxt()
                            pso = psb[pb_i]
                            for k in range(8):
                                op("pe", (lambda e, k=k, pso=pso, c0=c0, ncol=ncol, t=t: e.matmul(
                                    pso[:, 0:ncol], hT[:, k, t * 128:(t + 1) * 128], wqkv[:, k, c0:c0 + ncol], start=(k == 0), stop=(k == 7))),
                                   reads=[b_hT, b_wqkv], writes=[b_ps[pb_i]])
                            if kind == "v":
                                if isc == 0:
                                    op("dve", (lambda e, pso=pso, t=t: e.tensor_copy(out=vbst[:, t, :, 0:64], in_=pso[:, 0:384].rearrange("p (h d) -> p h d", d=64))),
                                       reads=[b_ps[pb_i]], writes=[b_vbst])
                                else:
                                    op("act", (lambda e, pso=pso, t=t: e.activation(out=vcst[:, t, :, 0:128], in_=pso[:, 0:512].rearrange("p (h d) -> p h d", d=128), func=AF.Copy)),
                                       reads=[b_ps[pb_i]], writes=[b_vcst])
                                continue
                            nh = ncol // 64
                            j = qkc[0] % 4
                            qkc[0] += 1
                            sv = svs[j]
                            op("act", (lambda e, pso=pso, ncol=ncol, j=j: e.activation(out=sq[j][:, 0:ncol], in_=pso[:, 0:ncol], func=AF.Square)),
                               reads=[b_ps[pb_i]], writes=[b_sq[j]])
                            op("dve", (lambda e, ncol=ncol, j=j, sv=sv, nh=nh: e.tensor_reduce(out=sv[:, 0:nh], in_=sq[j][:, 0:ncol].rearrange("p (h d) -> p h d", d=64), axis=AX.X, op=ALU.add)),
                               reads=[b_sq[j]], writes=[b_qs[j]])
                            op("act", (lambda e, sv=sv, nh=nh: e.activation(out=sv[:, 0:nh], in_=sv[:, 0:nh], func=AF.Sqrt, scale=1.0 / 64, bias=epsr[:, 0:1])),
                               reads=[b_qs[j], b_eps], writes=[b_qs[j]])
                            op("dve", (lambda e, sv=sv, nh=nh: e.reciprocal(out=sv[:, 0:nh], in_=sv[:, 0:nh])), reads=[b_qs[j]], writes=[b_qs[j]])
                            if isc == 0:
                                op("dve", (lambda e, pso=pso, ncol=ncol, j=j, sv=sv, nh=nh: e.tensor_tensor(
                                    out=qn[j][:, 0:ncol].rearrange("p (h d) -> p h d", d=64), in0=pso[:, 0:ncol].rearrange("p (h d) -> p h d", d=64),
                                    in1=sv[:, 0:nh].unsqueeze(2).to_broadcast([128, nh, 64]), op=ALU.mult)),
                                   reads=[b_ps[pb_i], b_qs[j]], writes=[b_qn[j]])
                            else:
                                op("dve", (lambda e, pso=pso, j=j, sv=sv: e.tensor_tensor(
                                    out=qn[j][:, 0:512].rearrange("p (h m d) -> p m h d", h=4, m=2),
                                    in0=pso[:, 0:512].rearrange("p (m h d) -> p m h d", m=2, h=4),
                                    in1=sv[:, 0:8].rearrange("p (m h) -> p m h", m=2).unsqueeze(3).to_broadcast([128, 2, 4, 64]), op=ALU.mult)),
                                   reads=[b_ps[pb_i], b_qs[j]], writes=[b_qn[j]])
                            def tr_part(kind=kind, nch=nch, ch0=ch0, isc=isc, t=t, j=j, bi=bi):
                                trb = 6 + (bi % 2)
                                pv = bf(psb[trb][:, :])
                                for c in range(nch):
                                    src = qn[j][:, c * 128:(c + 1) * 128]
                                    op("pe", (lambda e, c=c, src=src, pv=pv: e.transpose(pv[:, c * 128:(c + 1) * 128], src, ident[:, :])),
                                       reads=[b_qn[j], b_ident], writes=[b_ps[trb]])
                                dstt = qst if kind == "q" else kst
                                b_dst = b_qst if kind == "q" else b_kst
                                if kind == "q":
                                    op("dve", (lambda e: e.tensor_copy(
                                        out=dstt[:, ch0:ch0 + nch, t * 128:(t + 1) * 128], in_=pv[:, 0:nch * 128].rearrange("p (c q) -> p c q", q=128))),
                                       reads=[b_ps[trb]], writes=[b_dst])
                                else:
                                    op("act", (lambda e: e.activation(
                                        out=dstt[:, ch0:ch0 + nch, t * 128:(t + 1) * 128], in_=pv[:, 0:nch * 128].rearrange("p (c q) -> p c q", q=128),
                                        func=AF.Copy, scale=ksc[:, isc:isc + 1])),
                                       reads=[b_ps[trb], b_ksc], writes=[b_dst])
                            lag1.push(tr_part)

                    def tb_stores(tb=tb):
                        s0 = tb * TB
                        op("pool", (lambda e: e.dma_start(out=QT[si][:, :, s0:s0 + TB].rearrange("c p s -> p c s"), in_=qst[:, :, :])), reads=[b_qst], writes=[b_QT], dma="d_qst")
                        op("pool", (lambda e: e.dma_start(out=KT[si][:, :, s0:s0 + TB].rearrange("c p s -> p c s"), in_=kst[:, :, :])), reads=[b_kst], writes=[b_KT], dma="d_kst")
                        op("pool", (lambda e: e.dma_start(out=VB[si][tb * 4:(tb + 1) * 4, :, :].rearrange("t p c -> p t c"), in_=vbst[:, :, :, :].rearrange("p t h d -> p t (h d)"))),
                           reads=[b_vbst], writes=[b_VB], dma="d_vbst")
                        op("pool", (lambda e: e.dma_start(out=VC[si][tb * 4:(tb + 1) * 4, :, :].rearrange("t p c -> p t c"), in_=vcst[:, :, :, :].rearrange("p t h d -> p t (h d)"))),
                           reads=[b_vcst], writes=[b_VC], dma="d_vcst")
                    lag1.push(tb_stores, main=False)
                lag1.flush()
                S.barrier()

                al = Alloc()
                strips = al([STRIP_COLS], BF16)
                vcall = al([NT, 516], BF16)
                slotA = al([3 * SL], BF16)
                slotB = al([3 * SL], BF16)
                slotC = al([NT * 390], BF16)
                pt = [al([512], BF16) for _ in range(6)]
                stage_b = al([4, 6, 65], F32)
                stage_c = al([4, 2, 129], F32)
                o_t = al([4, 128], F32); t_t = al([4, 128], F32); q_t = al([4, 128], F32)
                bo_tm = al([4, 384], BF16); co_tm = al([4, 128], BF16)
                boT_st = al([3, TB], BF16); coT_st = al([TB], BF16)
                sgt = al([128], F32)
                zz = al([64], F32)
                b_strips = S.buf("strips"); b_vcall = S.buf("vcall")
                b_slot = [S.buf("slotA"), S.buf("slotB"), S.buf("slotC")]
                b_pt = [S.buf(f"pt{i}") for i in range(6)]
                b_stb = S.buf("stage_b"); b_stc = S.buf("stage_c")
                b_ot = S.buf("o_t"); b_tt = S.buf("t_t"); b_qt = S.buf("q_t")
                b_botm = S.buf("bo_tm"); b_cotm = S.buf("co_tm"); b_boT = S.buf("boT_st"); b_coT = S.buf("coT_st")
                b_sgt = S.buf("sgt"); b_zz = S.buf("zz")
                slots = [slotA, slotB, slotC]

                op("sp", lambda e: e.dma_start(out=strips[:, :], in_=strd[:, :]), reads=[b_strd], writes=[b_strips], dma="d_strips")
                for t0 in range(0, NT, 8):
                    op("sp", (lambda e, t0=t0: e.dma_start(out=vcall[:, t0:min(NT, t0 + 8), :], in_=VC[si][t0:min(NT, t0 + 8), :, :].rearrange("t p c -> p t c"))),
                       reads=[b_VC], writes=[b_vcall], dma="d_vcall")
                for j, src in enumerate((lam_q1, lam_k1, lam_q2, lam_k2)):
                    op("sp", (lambda e, l=l, j=j, src=src: e.dma_start(out=o_t[:, j, 0:64], in_=src[l:l + 1, :].partition_broadcast(128))),
                       writes=[b_ot], dma="d_lamv")
                op("dve", lambda e: e.tensor_tensor(out=t_t[:, 0, 0:64], in0=o_t[:, 0, 0:64], in1=o_t[:, 1, 0:64], op=ALU.mult), reads=[b_ot], writes=[b_tt])
                op("dve", lambda e: e.tensor_tensor(out=t_t[:, 1, 0:64], in0=o_t[:, 2, 0:64], in1=o_t[:, 3, 0:64], op=ALU.mult), reads=[b_ot], writes=[b_tt])
                op("dve", lambda e: e.tensor_reduce(out=zz[:, 0:2], in_=t_t[:, 0:2, 0:64], axis=AX.X, op=ALU.add), reads=[b_tt], writes=[b_zz])
                op("act", lambda e: e.activation(out=zz[:, 2:4], in_=zz[:, 0:2], func=AF.Exp), reads=[b_zz], writes=[b_zz])
                li = float(lam_inits[l])
                op("dve", lambda e: e.tensor_tensor(out=lamt[:, 4:5], in0=zz[:, 3:4], in1=zz[:, 2:3], op=ALU.subtract), reads=[b_zz], writes=[b_lam])
                op("dve", lambda e: e.tensor_scalar(out=lamt[:, 4:5], in0=lamt[:, 4:5], scalar1=-li, scalar2=None, op0=ALU.add), reads=[b_lam], writes=[b_lam])
                op("sp", (lambda e, l=l: e.dma_start(out=sgt[:, :], in_=subln_g[l:l + 1, :].partition_broadcast(128))), writes=[b_sgt], dma="d_sgt")
                op("dve", lambda e: e.tensor_scalar(out=sgt[:, :], in0=sgt[:, :], scalar1=1.0 - li, scalar2=None, op0=ALU.mult), reads=[b_sgt], writes=[b_sgt])

                srot = [0]
                prot = [0]
                arot = [0]
                lag15 = Lag(2)

                def score_pair(tiles):
                    pr = srot[0] % 2
                    pq = srot[0] % 3
                    srot[0] += 1
                    info = []
                    for j, (lhsT, rhs, strip_win, pvs, extra_reads) in enumerate(tiles):
                        sb_i = 3 + 2 * pr + j
                        pi = 2 * pq + j
                        op("pe", (lambda e, sb_i=sb_i, lhsT=lhsT, rhs=rhs: e.matmul(psb[sb_i][:, :], lhsT, rhs, start=True, stop=True)), reads=extra_reads, writes=[b_ps[sb_i]])
                        info.append((sb_i, pi, strip_win, pvs, extra_reads))
                    for (sb_i, pi, strip_win, pvs, extra_reads) in info:
                        op("act", (lambda e, sb_i=sb_i, pi=pi: e.activation(out=pt[pi][:, :], in_=psb[sb_i][:, :], func=AF.Exp)), reads=[b_ps[sb_i]], writes=[b_pt[pi]])
                    for (sb_i, pi, strip_win, pvs, extra_reads) in info:
                        op("dve", (lambda e, pi=pi, strip_win=strip_win: e.tensor_tensor(out=pt[pi][:, :], in0=pt[pi][:, :], in1=strip_win, op=ALU.mult)), reads=[b_pt[pi], b_strips], writes=[b_pt[pi]])

                    def pv_part():
                        for (sb_i, pi, strip_win, pvs, extra_reads) in info:
                            for (oap, bi_, qs, rap, stt, stp) in pvs:
                                op("pe", (lambda e, oap=oap, qs=qs, rap=rap, stt=stt, stp=stp, pi=pi: e.matmul(oap, pt[pi][:, qs * 128:(qs + 1) * 128], rap, start=stt, stop=stp, skip_group_check=True)),
                                   reads=[b_pt[pi]] + extra_reads, writes=[b_ps[bi_]])
                    lag15.push(pv_part)

                ktb = slotA.rearrange("p (c s) -> p c s", s=SL)
                qtb = slotB.rearrange("p (c s) -> p c s", s=SL)
                vbt = slotC.rearrange("p (t c) -> p t c", c=390)
                op("sp", lambda e: e.dma_start(out=ktb, in_=KT[si][0:3, :, :].rearrange("c p s -> p c s")), reads=[b_KT], writes=[b_slot[0]], dma="d_slot0")
                op("sp", lambda e: e.dma_start(out=qtb, in_=QT[si][0:3, :, :].rearrange("c p s -> p c s")), reads=[b_QT], writes=[b_slot[1]], dma="d_slot1")
                for t0 in range(0, NT, 8):
                    op("sp", (lambda e, t0=t0: e.dma_start(out=vbt[:, t0:min(NT, t0 + 8), :], in_=VB[si][t0:min(NT, t0 + 8), :, :].rearrange("t p c -> p t c"))),
                       reads=[b_VB], writes=[b_slot[2]], dma="d_slot2")
                for qb in range(NB):
                    for g in range(3):
                        gname = f"b{g}"
                        dmin, dmax, _, _ = GEO[gname]
                        Wg = geo_W(gname)
                        kts = [kt for kt in range(NT) if dmin <= kt * 128 - qb * TB <= dmax]
                        for i, kt in enumerate(kts):
                            c0 = dmax - (kt * 128 - qb * TB)
                            tiles = []
                            for hp in range(2):
                                m = g * 2 + hp
                                soff = STRIP_OFF[gname] + hp * Wg
                                pvs = [(psb[hp][:, qs * 65:(qs + 1) * 65], hp, qs, vbt[:, kt, m * 65:(m + 1) * 65],
                                        (i == 0 and qs == 0), (i == len(kts) - 1)) for qs in range(4)]
                                tiles.append((ktb[hp * 64:(hp + 1) * 64, g, kt * 128:(kt + 1) * 128], qtb[hp * 64:(hp + 1) * 64, g, qb * TB:(qb + 1) * TB],
                                              strips[:, soff + c0:soff + c0 + 512], pvs, [b_slot[0], b_slot[1], b_slot[2]]))
                            score_pair(tiles)

                        def evac_b(g=g):
                            for hp in range(2):
                                m = g * 2 + hp
                                op("dve", (lambda e, hp=hp, m=m: e.tensor_copy(out=stage_b[:, :, m, :], in_=psb[hp][:, 0:260].rearrange("p (q c) -> p q c", c=65))),
                                   reads=[b_ps[hp]], writes=[b_stb])
                        lag15.push(evac_b, main=False)
                    def norm_b(qb=qb):
                        zv = stage_b[:, :, :, 64]
                        op("dve", lambda e: e.tensor_tensor(out=zz[:, 0:8].rearrange("p (q h) -> p q h", h=2), in0=zv[:, :, 0:2], in1=zv[:, :, 2:4], op=ALU.add), reads=[b_stb], writes=[b_zz])
                        op("dve", lambda e: e.tensor_tensor(out=zz[:, 0:8].rearrange("p (q h) -> p q h", h=2), in0=zz[:, 0:8].rearrange("p (q h) -> p q h", h=2), in1=zv[:, :, 4:6], op=ALU.add), reads=[b_stb, b_zz], writes=[b_zz])
                        op("dve", lambda e: e.reciprocal(out=zz[:, 8:16], in_=zz[:, 0:8]), reads=[b_zz], writes=[b_zz])
                        for g in range(3):
                            op("dve", (lambda e, g=g: e.tensor_tensor(
                                out=bo_tm[:, :, g * 128:(g + 1) * 128].rearrange("p q (h d) -> p q h d", d=64),
                                in0=stage_b[:, :, 2 * g:2 * g + 2, 0:64],
                                in1=zz[:, 8:16].rearrange("p (q h) -> p q h", h=2).unsqueeze(3).to_broadcast([128, 4, 2, 64]), op=ALU.mult)),
                               reads=[b_stb, b_zz], writes=[b_botm])
                        pv = bf(psb[7][:, :])
                        for qs in range(4):
                            for c in range(3):
                                op("pe", (lambda e, qs=qs, c=c: e.transpose(pv[:, c * 128:(c + 1) * 128], bo_tm[:, qs, c * 128:(c + 1) * 128], ident[:, :])),
                                   reads=[b_botm, b_ident], writes=[b_ps[7]])
                            op("act", (lambda e, qs=qs: e.activation(out=boT_st[:, :, qs * 128:(qs + 1) * 128], in_=pv[:, 0:384].rearrange("p (c q) -> p c q", q=128), func=AF.Copy)),
                               reads=[b_ps[7]], writes=[b_boT])
                        op("pool", (lambda e, qb=qb: e.dma_start(out=BOT[si][:, :, qb * TB:(qb + 1) * TB].rearrange("c p s -> p c s"), in_=boT_st[:, :, :])),
                           reads=[b_boT], writes=[b_BOT], dma="d_boT")
                    lag15.push(norm_b, main=False)

                dmin, dmax, _, _ = GEO["c"]
                Wc = geo_W("c")
                for h in range(4):
                    lag15.flush()
                    sl = slots[h % 3]
                    bsl = b_slot[h % 3]
                    ktc = sl[:, 0:SL]
                    qtc = sl[:, SL:2 * SL]
                    op("sp", (lambda e, h=h, ktc=ktc: e.dma_start(out=ktc, in_=KT[si][3 + h, :, :])), reads=[b_KT], writes=[bsl], dma=f"d_slot{h % 3}")
                    op("sp", (lambda e, h=h, qtc=qtc: e.dma_start(out=qtc, in_=QT[si][3 + h, :, :])), reads=[b_QT], writes=[bsl], dma=f"d_slot{h % 3}")
                    soff = STRIP_OFF["c"] + h * Wc
                    for qb in range(NB):
                        for kt in range(NT):
                            dl = min(max(kt * 128 - qb * TB, dmin), dmax)
                            c0 = dmax - dl
                            tiles = []
                            for mp in range(2):
                                pvs = []
                                for qs in range(4):
                                    idx = mp * 4 + qs
                                    bk = idx // 3
                                    col = (idx % 3) * 129
                                    pvs.append((psb[bk][:, col:col + 129], bk, qs, vcall[:, kt, h * 129:(h + 1) * 129], (kt == 0 and idx % 3 == 0), (kt == NT - 1)))
                                tiles.append((ktc[mp * 64:(mp + 1) * 64, kt * 128:(kt + 1) * 128], qtc[mp * 64:(mp + 1) * 64, qb * TB:(qb + 1) * TB],
                                              strips[:, soff + c0:soff + c0 + 512], pvs, [bsl, b_vcall]))
                            score_pair(tiles)

                        def evac_c():
                            op("dve", lambda e: e.tensor_copy(out=stage_c[:, 0:3, 0, :], in_=psb[0][:, 0:387].rearrange("p (q c) -> p q c", c=129)), reads=[b_ps[0]], writes=[b_stc])
                            op("dve", lambda e: e.tensor_copy(out=stage_c[:, 3, 0, :], in_=psb[1][:, 0:129]), reads=[b_ps[1]], writes=[b_stc])
                            op("dve", lambda e: e.tensor_copy(out=stage_c[:, 0:2, 1, :], in_=psb[1][:, 129:387].rearrange("p (q c) -> p q c", c=129)), reads=[b_ps[1]], writes=[b_stc])
                            op("dve", lambda e: e.tensor_copy(out=stage_c[:, 2:4, 1, :], in_=psb[2][:, 0:258].rearrange("p (q c) -> p q c", c=129)), reads=[b_ps[2]], writes=[b_stc])
                        lag15.push(evac_c, main=False)
                        def norm_c(h=h, qb=qb):
                            rz = zz[:, 16:24].rearrange("p (q m) -> p q m", m=2)
                            op("dve", lambda e: e.reciprocal(out=rz, in_=stage_c[:, :, :, 128]), reads=[b_stc], writes=[b_zz])
                            op("dve", lambda e: e.tensor_scalar(out=zz[:, 24:28], in0=rz[:, :, 1], scalar1=lamt[:, 4:5], scalar2=None, op0=ALU.mult), reads=[b_zz, b_lam], writes=[b_zz])
                            op("dve", lambda e: e.tensor_tensor(out=o_t[:, :, :], in0=stage_c[:, :, 0, 0:128], in1=rz[:, :, 0].unsqueeze(2).to_broadcast([128, 4, 128]), op=ALU.mult),
                               reads=[b_stc, b_zz], writes=[b_ot])
                            op("dve", lambda e: e.tensor_tensor(out=t_t[:, :, :], in0=stage_c[:, :, 1, 0:128], in1=zz[:, 24:28].unsqueeze(2).to_broadcast([128, 4, 128]), op=ALU.mult),
                               reads=[b_stc, b_zz], writes=[b_tt])
                            op("dve", lambda e: e.tensor_tensor(out=o_t[:, :, :], in0=o_t[:, :, :], in1=t_t[:, :, :], op=ALU.add), reads=[b_ot, b_tt], writes=[b_ot])
                            op("act", lambda e: e.activation(out=q_t[:, :, :], in_=o_t[:, :, :], func=AF.Square), reads=[b_ot], writes=[b_qt])
                            op("dve", lambda e: e.tensor_reduce(out=zz[:, 28:32], in_=q_t[:, :, :], axis=AX.X, op=ALU.add), reads=[b_qt], writes=[b_zz])
                            op("act", lambda e: e.activation(out=zz[:, 28:32], in_=zz[:, 28:32], func=AF.Sqrt, scale=1.0 / 128, bias=epsr[:, 0:1]), reads=[b_zz, b_eps], writes=[b_zz])
                            op("dve", lambda e: e.reciprocal(out=zz[:, 28:32], in_=zz[:, 28:32]), reads=[b_zz], writes=[b_zz])
                            op("dve", lambda e: e.tensor_tensor(out=o_t[:, :, :], in0=o_t[:, :, :], in1=zz[:, 28:32].unsqueeze(2).to_broadcast([128, 4, 128]), op=ALU.mult),
                               reads=[b_ot, b_zz], writes=[b_ot])
                            op("dve", lambda e: e.tensor_tensor(out=co_tm[:, :, :], in0=o_t[:, :, :], in1=sgt[:, :].unsqueeze(1).to_broadcast([128, 4, 128]), op=ALU.mult),
                               reads=[b_ot, b_sgt], writes=[b_cotm])
                            pv = bf(psb[7][:, :])
                            for qs in range(4):
                                op("pe", (lambda e, qs=qs: e.transpose(pv[:, qs * 128:(qs + 1) * 128], co_tm[:, qs, :], ident[:, :])), reads=[b_cotm, b_ident], writes=[b_ps[7]])
                            op("act", lambda e: e.activation(out=coT_st[:, :], in_=pv[:, 0:512], func=AF.Copy), reads=[b_ps[7]], writes=[b_coT])
                            op("pool", (lambda e, h=h, qb=qb: e.dma_start(out=COT[si][h, :, qb * TB:(qb + 1) * TB], in_=coT_st[:, :])), reads=[b_coT], writes=[b_COT], dma="d_coT")
                        lag15.push(norm_c, main=False)
                lag15.flush()
                S.barrier()

                al = Alloc()
                xt = [al([D], F32) for _ in range(4)]
                hb = [al([D], BF16) for _ in range(4)]
                hT = al([8, TB], BF16)
                g1t = al([D], F32)
                g2t = al([D], F32)
                junk = al([D], F32)
                lng = al([512], F32); lnb = al([512], F32)
                sgb = al([4, 128], F32)
                wgn = al([8, 128], F32)
                wgnb = al([8, 128], BF16)
                wgT = al([8, 128], BF16)
                gv = [al([512], F32) for _ in range(2)]
                vn = al([4, 512], BF16)
                uT = al([4, TB], BF16)
                tmpa = [al([TB], F32) for _ in range(2)]
                aT = al([4, TB], BF16)
                bcT = al([7, TB], BF16)
                sg = [al([TB], F32) for _ in range(3)]
                mm_ = [al([TB], F32) for _ in range(2)]
                mT = al([8, TB], BF16)
                sgf = [al([TB], BF16) for _ in range(3)]
                actT = al([22, TB], BF16)
                bst = al([8], F32)
                ring_n = 5
                SLOTB = 11 * 512
                ring = [al([SLOTB], BF16) for _ in range(ring_n)]
                b_xt = [S.buf(f"xt{t}") for t in range(4)]
                b_hb = [S.buf(f"hb{t}") for t in range(4)]
                b_hT = S.buf("hT"); b_g1 = S.buf("g1"); b_g2 = S.buf("g2"); b_junk = S.buf("junk")
                b_ss = [S.buf(f"ss{t}") for t in range(4)]
                b_ln = S.buf("ln"); b_sgb = S.buf("sgb"); b_wg = S.buf("wg"); b_wgT = S.buf("wgT")
                b_gv = [S.buf("gv0"), S.buf("gv1")]; b_vn = [S.buf(f"vn{t}") for t in range(4)]; b_uT = S.buf("uT")
                b_tmpa = [S.buf("tmpa0"), S.buf("tmpa1")]; b_aT = S.buf("aT"); b_bcT = S.buf("bcT")
                b_sg = [S.buf(f"sg{i}") for i in range(3)]; b_mm = [S.buf("mm0"), S.buf("mm1")]; b_mT = S.buf("mT")
                b_sgf = [S.buf(f"sgf{i}") for i in range(3)]; b_actT = S.buf("actT"); b_bst = [S.buf(f"bst{t}") for t in range(4)]
                b_ring = [S.buf(f"ring{i}") for i in range(ring_n)]
                b_BOT = S.buf("BOTd2"); b_COT = S.buf("COTd2")

                op("sp", (lambda e, l=l: e.dma_start(out=g1t[:, :], in_=norm1_g[l:l + 1, :].partition_broadcast(128))), writes=[b_g1], dma="d_g1")
                op("sp", (lambda e, l=l: e.dma_start(out=g2t[:, :], in_=norm2_g[l:l + 1, :].partition_broadcast(128))), writes=[b_g2], dma="d_g2")
                op("sp", (lambda e, l=l: e.dma_start(out=lng[:, :], in_=sgu_ln_g[l:l + 1, :].partition_broadcast(128))), writes=[b_ln], dma="d_ln")
                op("sp", (lambda e, l=l: e.dma_start(out=lnb[:, :], in_=sgu_ln_b[l:l + 1, :].partition_broadcast(128))), writes=[b_ln], dma="d_ln")
                for par in range(2):
                    src = bass.AP(sgu_b.tensor, l * 8 * 128 + par * 128, [[0, 64], [256, 4], [1, 128]])
                    op("sp", (lambda e, par=par, src=src: e.dma_start(out=sgb[par * 64:(par + 1) * 64, :, :], in_=src)), writes=[b_sgb], dma="d_sgb")
                op("sp", (lambda e, l=l: e.dma_start(out=wgn[:, :, :], in_=sgu_w[l, :, :, :].rearrange("g p q -> p g q"))), writes=[b_wg], dma="d_wgn")
                op("dve", lambda e: e.tensor_copy(out=wgnb[:, :, :], in_=wgn[:, :, :]), reads=[b_wg], writes=[b_wg])
                pv = bf(psb[7][:, :])
                for g in range(8):
                    op("pe", (lambda e, g=g: e.transpose(pv[:, g * 128:(g + 1) * 128], wgnb[:, g, :], ident[:, :])), reads=[b_wg, b_ident], writes=[b_ps[7]])
                op("dve", lambda e: e.tensor_copy(out=wgT[:, :, :], in_=pv.rearrange("p (g q) -> p g q", q=128)), reads=[b_ps[7]], writes=[b_wgT])

                pieces = []
                for tb in range(NB):
                    pieces.append([(("in", l), 8, 512, lambda l=l: wb_in[l, :, 512:1024], 0)])
                    pieces.append([(("in", l), 8, 512, lambda l=l: wb_in[l, :, 0:512], 0)])
                    for hf in range(2):
                        c0 = hf * 512
                        pieces.append([(("pa", l), 4, 512, lambda l=l, c0=c0: wb_pa[l, :, c0:c0 + 512], 0),
                                       (("pb", l), 3, 512, lambda l=l, c0=c0: wb_pb[l, :, c0:c0 + 512], 4),
                                       (("pc", l), 4, 512, lambda l=l, c0=c0: wb_pc[l, :, c0:c0 + 512], 7)])
                        for gi in range(3):
                            cc = 3712 + gi * 1024 + c0
                            pieces.append([(("in", l), 8, 512, lambda l=l, cc=cc: wb_in[l, :, cc:cc + 512], 0)])
                    for hf in range(2):
                        pieces.append([(("o", l), 8, 512, lambda l=l, hf=hf: wb_o[l, :, hf * 512:(hf + 1) * 512], 0)])
                    for f in range(11):
                        pieces.append([(("gu", l), 8, 256, lambda l=l, f=f: wb_gu[l, :, f * 256:(f + 1) * 256], 0),
                                       (("gu", l), 8, 256, lambda l=l, f=f: wb_gu[l, :, DFF + f * 256:DFF + (f + 1) * 256], 8)])
                    for hf in range(2):
                        for kh in range(2):
                            pieces.append([(("dn", l), 11, 512, lambda l=l, hf=hf, kh=kh: wb_dn[l, kh * 1408:(kh + 1) * 1408, hf * 512:(hf + 1) * 512], 0)])
                wstate = {"next_load": 0, "next_acq": 0, "held": []}

                def w_issue():
                    i = wstate["next_load"]
                    if i >= len(pieces):
                        return
                    slot = i % ring_n
                    for (wkey, nk, ncol, srcf, k0) in pieces[i]:
                        dst = ring[slot][:, k0 * ncol:(k0 + nk) * ncol].rearrange("p (k n) -> p k n", n=ncol)
                        src = srcf().rearrange("(k p) n -> p k n", p=128)
                        op("sp", (lambda e, dst=dst, src=src: e.dma_start(out=dst, in_=src)), reads=[b_wb[wkey]], writes=[b_ring[slot]], dma=f"d_ring{slot}")
                    wstate["next_load"] += 1

                def w_acquire():
                    i = wstate["next_acq"]
                    wstate["next_acq"] += 1
                    assert i < wstate["next_load"], "weight ring underflow"
                    slot = i % ring_n
                    return ring[slot], b_ring[slot]

                def w_release():
                    w_issue()

                for _ in range(ring_n):
                    w_issue()

                for tb in range(NB):
                    s0 = tb * TB
                    for t in range(4):
                        norm_tile(t, s0 + t * 128, xsrc[gidx], g1t, b_g1, 6 + (t % 2), part="a")
                    for t in range(4):
                        norm_tile(t, s0 + t * 128, xsrc[gidx], g1t, b_g1, 6 + (t % 2), part="b")
                    op("pool", (lambda e, s0=s0: e.dma_start(out=bcT[:, 0:3, :], in_=BOT[si][:, :, s0:s0 + TB].rearrange("c p s -> p c s"))), reads=[b_BOT], writes=[b_bcT], dma="d_bcT")
                    op("pool", (lambda e, s0=s0: e.dma_start(out=bcT[:, 3:7, :], in_=COT[si][:, :, s0:s0 + TB].rearrange("c p s -> p c s"))), reads=[b_COT], writes=[b_bcT], dma="d_bcT")
                    wv, bwv = w_acquire()
                    wv3 = wv[:, 0:8 * 512].rearrange("p (k n) -> p k n", n=512)
                    for t in range(4):
                        pb_i = nxt()
                        for k in range(8):
                            op("pe", (lambda e, k=k, t=t, pb_i=pb_i: e.matmul(psb[pb_i][:, :], hT[:, k, t * 128:(t + 1) * 128], wv3[:, k, :], start=(k == 0), stop=(k == 7))),
                               reads=[b_hT, bwv], writes=[b_ps[pb_i]])
                        j = t % 2
                        op("act", (lambda e, pb_i=pb_i, j=j: e.activation(out=gv[j][:, :], in_=psb[pb_i][:, :], func=AF.Gelu_apprx_tanh)), reads=[b_ps[pb_i]], writes=[b_gv[j]])
                        op("dve", (lambda e, j=j, t=t: e.bn_stats(out=bst[:, 0:6], in_=gv[j][:, :])), reads=[b_gv[j]], writes=[b_bst[0]])
                        op("dve", (lambda e, t=t: e.bn_aggr(out=small[:, 8 + 2 * t:10 + 2 * t], in_=bst[:, 0:6])), reads=[b_bst[0]], writes=[b_bst[1]])
                        op("act", (lambda e, t=t: e.activation(out=small[:, 9 + 2 * t:10 + 2 * t], in_=small[:, 9 + 2 * t:10 + 2 * t], func=AF.Sqrt, scale=1.0, bias=epsr[:, 1:2])),
                           reads=[b_bst[1], b_eps], writes=[b_bst[1]])
                        op("dve", (lambda e, t=t: e.reciprocal(out=small[:, 9 + 2 * t:10 + 2 * t], in_=small[:, 9 + 2 * t:10 + 2 * t])), reads=[b_bst[1]], writes=[b_bst[1]])
                        op("dve", (lambda e, j=j, t=t: e.tensor_scalar(out=gv[j][:, :], in0=gv[j][:, :], scalar1=small[:, 8 + 2 * t:9 + 2 * t], scalar2=small[:, 9 + 2 * t:10 + 2 * t],
                                                                    op0=ALU.subtract, op1=ALU.mult)), reads=[b_gv[j], b_bst[1]], writes=[b_gv[j]])
                        op("dve", (lambda e, j=j: e.tensor_tensor(out=gv[j][:, :], in0=gv[j][:, :], in1=lng[:, :], op=ALU.mult)), reads=[b_gv[j], b_ln], writes=[b_gv[j]])
                        op("dve", (lambda e, j=j, t=t: e.tensor_tensor(out=vn[:, t, :], in0=gv[j][:, :], in1=lnb[:, :], op=ALU.add)), reads=[b_gv[j], b_ln], writes=[b_vn[t]])
                    w_release()
                    wu, bwu = w_acquire()
                    wu3 = wu[:, 0:8 * 512].rearrange("p (k n) -> p k n", n=512)
                    for c in range(4):
                        pb_i = nxt()
                        for k in range(8):
                            op("pe", (lambda e, k=k, c=c, pb_i=pb_i: e.matmul(psb[pb_i][:, :], wu3[:, k, c * 128:(c + 1) * 128], hT[:, k, :], start=(k == 0), stop=(k == 7))),
                               reads=[b_hT, bwu], writes=[b_ps[pb_i]])
                        op("act", (lambda e, c=c, pb_i=pb_i: e.activation(out=uT[:, c, :], in_=psb[pb_i][:, :], func=AF.Gelu_apprx_tanh)), reads=[b_ps[pb_i]], writes=[b_uT])
                    w_release()
                    for j in range(4):
                        pa_i = nxt(); pb_i = nxt()
                        for t in range(4):
                            op("pe", (lambda e, j=j, t=t, pa_i=pa_i: e.matmul(psb[pa_i][:, t * 128:(t + 1) * 128], vn[:, t, j * 128:(j + 1) * 128], wgT[:, 2 * j, :], start=True, stop=True)),
                               reads=[b_vn[t], b_wgT], writes=[b_ps[pa_i]])
                        for t in range(4):
                            op("pe", (lambda e, j=j, t=t, pb_i=pb_i: e.matmul(psb[pb_i][:, t * 128:(t + 1) * 128], vn[:, t, j * 128:(j + 1) * 128], wgT[:, 2 * j + 1, :], start=True, stop=True)),
                               reads=[b_vn[t], b_wgT], writes=[b_ps[pb_i]])
                        jj = j % 2
                        op("dve", (lambda e, j=j, jj=jj, pa_i=pa_i: e.tensor_tensor(out=tmpa[jj][0:64, :].rearrange("p (t q) -> p t q", q=128),
                                                                              in0=psb[pa_i][0:64, :].rearrange("p (t q) -> p t q", q=128),
                                                                              in1=sgb[0:64, j, :].unsqueeze(1).to_broadcast([64, 4, 128]), op=ALU.add)),
                           reads=[b_ps[pa_i], b_sgb], writes=[b_tmpa[jj]])
                        op("dve", (lambda e, j=j, jj=jj, pb_i=pb_i: e.tensor_tensor(out=tmpa[jj][64:128, :].rearrange("p (t q) -> p t q", q=128),
                                                                              in0=psb[pb_i][64:128, :].rearrange("p (t q) -> p t q", q=128),
                                                                              in1=sgb[64:128, j, :].unsqueeze(1).to_broadcast([64, 4, 128]), op=ALU.add)),
                           reads=[b_ps[pb_i], b_sgb], writes=[b_tmpa[jj]])
                        op("dve", (lambda e, j=j, jj=jj: e.tensor_tensor(out=aT[:, j, :], in0=tmpa[jj][:, :], in1=uT[:, j, :], op=ALU.mult)), reads=[b_tmpa[jj], b_uT], writes=[b_aT])
                    for hf in range(2):
                        wp, bwp = w_acquire()
                        wp3 = wp[:, 0:11 * 512].rearrange("p (k n) -> p k n", n=512)
                        wg_ = []
                        for gi in range(3):
                            w_, bw_ = w_acquire()
                            wg_.append((w_[:, 0:8 * 512].rearrange("p (k n) -> p k n", n=512), bw_))
                        for oc in range(4):
                            cs = slice(oc * 128, (oc + 1) * 128)
                            gbanks = []
                            for gi in range(3):
                                pg = nxt()
                                gbanks.append(pg)
                                w3, bw3 = wg_[gi]
                                for k in range(8):
                                    op("pe", (lambda e, k=k, pg=pg, w3=w3, cs=cs: e.matmul(psb[pg][:, :], w3[:, k, cs], hT[:, k, :], start=(k == 0), stop=(k == 7))),
                                       reads=[b_hT, bw3], writes=[b_ps[pg]])
                                op("act", (lambda e, gi=gi, pg=pg: e.activation(out=sg[gi][:, :], in_=psb[pg][:, :], func=AF.Sigmoid)), reads=[b_ps[pg]], writes=[b_sg[gi]])
                            pbanks = []
                            for bi_, (k0, nk, srcT, koff, bsrc) in enumerate(((0, 4, aT, 0, b_aT), (4, 3, bcT, 0, b_bcT), (7, 4, bcT, 3, b_bcT))):
                                pp = nxt()
                                pbanks.append(pp)
                                for k in range(nk):
                                    op("pe", (lambda e, k=k, pp=pp, k0=k0, nk=nk, srcT=srcT, koff=koff, cs=cs: e.matmul(
                                        psb[pp][:, :], wp3[:, k0 + k, cs], srcT[:, koff + k, :], start=(k == 0), stop=(k == nk - 1))),
                                       reads=[bsrc, bwp], writes=[b_ps[pp]])
                            op("dve", (lambda e, pbanks=pbanks: e.tensor_tensor(out=mm_[0][:, :], in0=psb[pbanks[0]][:, :], in1=sg[0][:, :], op=ALU.mult)),
                               reads=[b_ps[pbanks[0]], b_sg[0]], writes=[b_mm[0]])
                            op("dve", (lambda e, pbanks=pbanks: e.tensor_tensor(out=mm_[1][:, :], in0=psb[pbanks[1]][:, :], in1=sg[1][:, :], op=ALU.mult)),
                               reads=[b_ps[pbanks[1]], b_sg[1]], writes=[b_mm[1]])
                            op("dve", lambda e: e.tensor_tensor(out=mm_[0][:, :], in0=mm_[0][:, :], in1=mm_[1][:, :], op=ALU.add), reads=[b_mm[0], b_mm[1]], writes=[b_mm[0]])
                            op("dve", (lambda e, pbanks=pbanks: e.tensor_tensor(out=mm_[1][:, :], in0=psb[pbanks[2]][:, :], in1=sg[2][:, :], op=ALU.mult)),
                               reads=[b_ps[pbanks[2]], b_sg[2]], writes=[b_mm[1]])
                            op("dve", (lambda e, hf=hf, oc=oc: e.tensor_tensor(out=mT[:, hf * 4 + oc, :], in0=mm_[0][:, :], in1=mm_[1][:, :], op=ALU.add)),
                               reads=[b_mm[0], b_mm[1]], writes=[b_mT])
                        for _ in range(4):
                            w_release()
                    for hf in range(2):
                        wo, bwo = w_acquire()
                        wo3 = wo[:, 0:8 * 512].rearrange("p (k n) -> p k n", n=512)
                        for t in range(4):
                            pb_i = nxt()
                            for k in range(8):
                                op("pe", (lambda e, k=k, t=t, pb_i=pb_i, wo3=wo3: e.matmul(psb[pb_i][:, :], mT[:, k, t * 128:(t + 1) * 128], wo3[:, k, :], start=(k == 0), stop=(k == 7))),
                                   reads=[b_mT, bwo], writes=[b_ps[pb_i]])
                            op("dve", (lambda e, t=t, hf=hf, pb_i=pb_i: e.tensor_tensor(out=xt[t][:, hf * 512:(hf + 1) * 512], in0=xt[t][:, hf * 512:(hf + 1) * 512], in1=psb[pb_i][:, :], op=ALU.add)),
                               reads=[b_ps[pb_i], b_xt[t]], writes=[b_xt[t]])
                        w_release()
                    for t in range(4):
                        hbi = t % 4
                        op("act", (lambda e, t=t: e.activation(out=junk[:, :], in_=xt[t][:, :], func=AF.Square, accum_out=small[:, t:t + 1])), reads=[b_xt[t]], writes=[b_junk, b_ss[t]])
                        op("act", (lambda e, t=t: e.activation(out=small[:, t:t + 1], in_=small[:, t:t + 1], func=AF.Sqrt, scale=1.0 / D, bias=epsr[:, 0:1])), reads=[b_ss[t], b_eps], writes=[b_ss[t]])
                        op("dve", (lambda e, t=t: e.reciprocal(out=small[:, t:t + 1], in_=small[:, t:t + 1])), reads=[b_ss[t]], writes=[b_ss[t]])
                        op("dve", (lambda e, t=t, hbi=hbi: e.scalar_tensor_tensor(out=hb[hbi][:, :], in0=xt[t][:, :], scalar=small[:, t:t + 1], in1=g2t[:, :], op0=ALU.mult, op1=ALU.mult)),
                           reads=[b_xt[t], b_ss[t], b_g2], writes=[b_hb[hbi]])
                    for t in range(4):
                        hbi = t % 4
                        trbank = 6 + (t % 2)
                        pv = bf(psb[trbank][:, :])
                        for c in range(8):
                            op("pe", (lambda e, c=c, pv=pv, hbi=hbi: e.transpose(pv[:, c * 128:(c + 1) * 128], hb[hbi][:, c * 128:(c + 1) * 128], ident[:, :])),
                               reads=[b_hb[hbi], b_ident], writes=[b_ps[trbank]])
                        op("act", (lambda e, t=t, pv=pv: e.activation(out=hT[:, :, t * 128:(t + 1) * 128], in_=pv.rearrange("p (c q) -> p c q", q=128), func=AF.Copy)),
                           reads=[b_ps[trbank]], writes=[b_hT])
                    for f in range(11):
                        wgu, bwgu = w_acquire()
                        wgu3 = wgu[:, 0:16 * 256].rearrange("p (k n) -> p k n", n=256)
                        for ch in range(2):
                            pg = nxt(); pu = nxt()
                            for k in range(8):
                                op("pe", (lambda e, k=k, ch=ch, pg=pg: e.matmul(psb[pg][:, :], wgu3[:, k, ch * 128:(ch + 1) * 128], hT[:, k, :], start=(k == 0), stop=(k == 7))),
                                   reads=[b_hT, bwgu], writes=[b_ps[pg]])
                            for k in range(8):
                                op("pe", (lambda e, k=k, ch=ch, pu=pu: e.matmul(psb[pu][:, :], wgu3[:, 8 + k, ch * 128:(ch + 1) * 128], hT[:, k, :], start=(k == 0), stop=(k == 7))),
                                   reads=[b_hT, bwgu], writes=[b_ps[pu]])
                            sj = (f * 2 + ch) % 3
                            op("act", (lambda e, pg=pg, sj=sj: e.activation(out=sgf[sj][:, :], in_=psb[pg][:, :], func=AF.Silu)), reads=[b_ps[pg]], writes=[b_sgf[sj]])
                            op("dve", (lambda e, pu=pu, sj=sj, f=f, ch=ch: e.tensor_tensor(out=actT[:, f * 2 + ch, :], in0=psb[pu][:, :], in1=sgf[sj][:, :], op=ALU.mult)),
                               reads=[b_ps[pu], b_sgf[sj]], writes=[b_actT])
                        w_release()
                    for hf in range(2):
                        wda, bwda = w_acquire()
                        wdb, bwdb = w_acquire()
                        wd3 = [wda[:, 0:11 * 512].rearrange("p (k n) -> p k n", n=512), wdb[:, 0:11 * 512].rearrange("p (k n) -> p k n", n=512)]
                        for t in range(4):
                            pb_i = nxt()
                            for k in range(22):
                                op("pe", (lambda e, k=k, t=t, pb_i=pb_i, wd3=wd3: e.matmul(psb[pb_i][:, :], actT[:, k, t * 128:(t + 1) * 128], wd3[k // 11][:, k % 11, :], start=(k == 0), stop=(k == 21))),
                                   reads=[b_actT, bwda, bwdb], writes=[b_ps[pb_i]])
                            op("dve", (lambda e, t=t, hf=hf, pb_i=pb_i: e.tensor_tensor(out=xt[t][:, hf * 512:(hf + 1) * 512], in0=xt[t][:, hf * 512:(hf + 1) * 512], in1=psb[pb_i][:, :], op=ALU.add)),
                               reads=[b_ps[pb_i], b_xt[t]], writes=[b_xt[t]])
                        w_release()
                        w_release()
                    for t in range(4):
                        tok0 = s0 + t * 128
                        op("pool", (lambda e, t=t, tok0=tok0: e.dma_start(out=ydst[gidx, tok0:tok0 + 128, :], in_=xt[t][:, :])), reads=[b_xt[t]], dma=f"d_x{t}")
                S.barrier()
        S.barrier()
        S.emit()
    return nc


N_CORES = 8
_PROGRAM_CACHE = {}


def _lam_inits(depth):
    return [0.8 - 0.6 * math.exp(-0.3 * l) for l in range(depth)]


def kernel(x_prompt, x_sample, rel_bias, norm1_g, w_in, sgu_ln_g, sgu_ln_b, sgu_w, sgu_b,
           qn_b, kn_b, qn_c, kn_c, lam_q1, lam_k1, lam_q2, lam_k2, subln_g,
           w_pa, w_pb, w_pc, w_o, norm2_g, w_gu, w_down):
    x_prompt = np.asarray(x_prompt, dtype=np.float32)
    x_sample = np.asarray(x_sample, dtype=np.float32)
    depth = int(np.asarray(w_in).shape[0])
    nb_p, s_p = x_prompt.shape[0], x_prompt.shape[1]
    nb_s, s_s = x_sample.shape[0], x_sample.shape[1]
    ncores = N_CORES
    assert nb_p % ncores == 0 and nb_s % ncores == 0
    pp, ps_ = nb_p // ncores, nb_s // ncores
    seq_lens = [("p", i, s_p) for i in range(pp)] + [("s", i, s_s) for i in range(ps_)]
    key = (tuple(seq_lens), depth)
    if key not in _PROGRAM_CACHE:
        _PROGRAM_CACHE[key] = build_program(seq_lens, depth, _lam_inits(depth))
    nc = _PROGRAM_CACHE[key]
    oh, mask, ident = host_constants()
    shared = dict(rel_bias=rel_bias, norm1_g=norm1_g, norm2_g=norm2_g, w_in=w_in, sgu_ln_g=sgu_ln_g, sgu_ln_b=sgu_ln_b,
                  sgu_w=sgu_w, sgu_b=sgu_b, qn_b=qn_b, kn_b=kn_b, qn_c=qn_c, kn_c=kn_c, lam_q1=lam_q1, lam_k1=lam_k1,
                  lam_q2=lam_q2, lam_k2=lam_k2, subln_g=subln_g, w_pa=w_pa, w_pb=w_pb, w_pc=w_pc, w_o=w_o, w_gu=w_gu,
                  w_down=w_down, c_oh=oh, c_mask=mask, c_ident=ident)
    shared = {k: np.ascontiguousarray(np.asarray(v, dtype=np.float32)) for k, v in shared.items()}
    in_maps = []
    for c in range(ncores):
        m = dict(shared)
        m["xp"] = np.ascontiguousarray(x_prompt[c * pp:(c + 1) * pp])
        m["xs"] = np.ascontiguousarray(x_sample[c * ps_:(c + 1) * ps_])
        in_maps.append(m)
    res = run_bass_kernel_spmd(nc, in_maps, core_ids=list(range(ncores)))
    yp = np.concatenate([np.asarray(r["yp"]) for r in res.results], axis=0).astype(np.float32)
    ys = np.concatenate([np.asarray(r["ys"]) for r in res.results], axis=0).astype(np.float32)
    return (yp, ys)
```

```python
import contextlib
import math
import numpy as np
import ml_dtypes
import concourse.bass as bass
import concourse.mybir as mybir
from concourse.bass_utils import run_bass_kernel_spmd

F32 = mybir.dt.float32
BF16 = mybir.dt.bfloat16
AF = mybir.ActivationFunctionType
ALU = mybir.AluOpType
AX = mybir.AxisListType

D = 1024
INC = 6784
DFF = 2816
NBUCK = 32
TB = 512
RMS_EPS = 1e-6
LN_EPS = 1e-5
GEO = {"c": (-768, 1152, 1, None), "b0": (-128, 512, 1, 64), "b1": (-256, 640, 4, 256), "b2": (-1024, 1408, 16, 1024)}
GEO_ORDER = ["c", "b0", "b1", "b2"]
GEO_NH = {"c": 4, "b0": 2, "b1": 2, "b2": 2}


def geo_W(g):
    return GEO[g][1] - GEO[g][0] + 512


def geo_L(g):
    return geo_W(g) + 127


STRIP_OFF = {}
_o = 0
for _g in GEO_ORDER:
    STRIP_OFF[_g] = _o
    _o += GEO_NH[_g] * geo_W(_g)
STRIP_COLS = _o
LMAX = max(geo_L(g) for g in GEO_ORDER)


def _rel_bucket_np(rel):
    nb = NBUCK // 2
    max_exact = nb // 2
    n = np.abs(rel)
    sign_off = np.where(rel > 0, nb, 0)
    nf = np.maximum(n, 1).astype(np.float32)
    large = max_exact + (np.log(nf / np.float32(max_exact)) / np.float32(math.log(1024 / max_exact))
                         * np.float32(nb - max_exact)).astype(np.int32)
    large = np.minimum(large, nb - 1)
    return sign_off + np.where(n < max_exact, n, large)


def host_constants():
    oh = np.zeros((4, NBUCK, LMAX), np.float32)
    mask = np.zeros((4, LMAX), np.float32)
    for gi, g in enumerate(GEO_ORDER):
        dmin, dmax, dil, hw = GEO[g]
        L = geo_L(g)
        rhi = dmax + 127
        rel = rhi - np.arange(L)
        b = _rel_bucket_np(rel)
        oh[gi, b, np.arange(L)] = 1.0
        if hw is None:
            mask[gi, :L] = 1.0
        else:
            mask[gi, :L] = ((rel % dil == 0) & (np.abs(rel) <= hw)).astype(np.float32)
    ident = np.eye(128, dtype=np.float32)
    return oh, mask, ident


class Buf:
    __slots__ = ("name", "w", "r")

    def __init__(self, name):
        self.name = name
        self.w = None
        self.r = {}


class _Rec:
    def __init__(self):
        self.call = None

    def __getattr__(self, name):
        def f(*a, **k):
            self.call = (name, a, k)
            return self
        return f


class Sched:
    COMPUTE = ("pe", "act", "dve", "pool")

    def __init__(self, nc, stack):
        self.nc = nc
        self.stack = stack
        self.q = {k: [] for k in ("pe", "act", "dve", "pool", "sp")}
        self.sems = {}
        self.tick = {}
        self.seen = {k: {} for k in self.q}
        for k in self.COMPUTE:
            self._sem(k)
        self.bufs = []

    def _sem(self, key):
        if key not in self.sems:
            self.sems[key] = self.stack.enter_context(self.nc.semaphore("s_" + key))
            self.tick[key] = 0
        return self.sems[key]

    def buf(self, name):
        b = Buf(name)
        self.bufs.append(b)
        return b

    def op(self, eng, fn, reads=(), writes=(), dma=None):
        deps = {}
        for b in reads:
            if b.w is not None:
                k, t = b.w
                if deps.get(k, 0) < t:
                    deps[k] = t
        for b in writes:
            if b.w is not None:
                k, t = b.w
                if deps.get(k, 0) < t:
                    deps[k] = t
            for k, t in b.r.items():
                if deps.get(k, 0) < t:
                    deps[k] = t
        waits = []
        seen = self.seen[eng]
        for k, t in deps.items():
            if dma is None and k == eng:
                if eng == "pe" or t < self.tick[eng] - 1:
                    continue
                if seen.get(k, 0) >= t:
                    continue
            elif seen.get(k, 0) >= t:
                continue
            seen[k] = t
            waits.append((k, t))
        if dma is not None:
            self._sem(dma)
            self.tick[dma] += 16
            ev = (dma, self.tick[dma])
            inc = (dma, 16)
        else:
            self.tick[eng] += 1
            ev = (eng, self.tick[eng])
            inc = (eng, 1)
        rec = _Rec()
        fn(rec)
        assert rec.call is not None
        self.q[eng].append((waits, rec.call, inc))
        for b in writes:
            b.w = ev
            b.r = {}
        for b in reads:
            if b in writes:
                continue
            k, t = ev
            if b.r.get(k, 0) < t:
                b.r[k] = t
        return ev

    def barrier(self):
        for eng in self.q:
            waits = []
            seen = self.seen[eng]
            for k, t in self.tick.items():
                if t > 0 and seen.get(k, 0) < t and k != eng:
                    seen[k] = t
                    waits.append((k, t))
            if waits:
                self.q[eng].append((waits, None, None))
        for b in self.bufs:
            b.w = None
            b.r = {}

    def emit(self):
        nc = self.nc
        sems = self.sems

        def run(e, items):
            for waits, fn, inc in items:
                for k, t in waits:
                    e.wait_ge(sems[k], t)
                if fn is not None:
                    name, a, k = fn
                    getattr(e, name)(*a, **k).then_inc(sems[inc[0]], inc[1])

        with nc.Block() as block:
            @block.tensor
            def _(e):
                run(e, self.q["pe"])

            @block.scalar
            def _(e):
                run(e, self.q["act"])

            @block.vector
            def _(e):
                run(e, self.q["dve"])

            @block.gpsimd
            def _(e):
                run(e, self.q["pool"])

            @block.sync
            def _(e):
                run(e, self.q["sp"])


class Lag:
    def __init__(self, la):
        self.la = la
        self.q = []

    def push(self, fn, main=True):
        self.q.append((fn, main))
        if main:
            while sum(1 for _, m in self.q if m) > self.la:
                f, _ = self.q.pop(0)
                f()
            while self.q and not self.q[0][1]:
                f, _ = self.q.pop(0)
                f()

    def flush(self):
        while self.q:
            f, _ = self.q.pop(0)
            f()


def build_program(seq_lens, depth, lam_inits):
    nc = bass.Bass("TRN2", target_bir_lowering=False)
    n_p = sum(1 for g, _, _ in seq_lens if g == "p")
    n_s = sum(1 for g, _, _ in seq_lens if g == "s")
    S_p = max([s for g, _, s in seq_lens if g == "p"], default=128)
    S_s = max([s for g, _, s in seq_lens if g == "s"], default=128)
    SMAX = max(s for _, _, s in seq_lens)
    L = depth

    def din(name, shape, dt=F32):
        return nc.dram_tensor(name, list(shape), dt, kind="ExternalInput").ap()

    def dscr(name, shape, dt=BF16):
        return nc.dram_tensor(name, list(shape), dt, kind="Internal").ap()

    xin = {"p": din("xp", [max(n_p, 1), S_p, D]), "s": din("xs", [max(n_s, 1), S_s, D])}
    yout = {"p": nc.dram_tensor("yp", [max(n_p, 1), S_p, D], F32, kind="ExternalOutput").ap(),
            "s": nc.dram_tensor("ys", [max(n_s, 1), S_s, D], F32, kind="ExternalOutput").ap()}
    rel_bias = din("rel_bias", [NBUCK, 10])
    norm1_g = din("norm1_g", [L, D]); norm2_g = din("norm2_g", [L, D])
    w_in = din("w_in", [L, D, INC])
    sgu_ln_g = din("sgu_ln_g", [L, 512]); sgu_ln_b = din("sgu_ln_b", [L, 512])
    sgu_w = din("sgu_w", [L, 8, 128, 128]); sgu_b = din("sgu_b", [L, 8, 128])
    qn_b = din("qn_b", [L, 64]); kn_b = din("kn_b", [L, 64]); qn_c = din("qn_c", [L, 64]); kn_c = din("kn_c", [L, 64])
    lam_q1 = din("lam_q1", [L, 64]); lam_k1 = din("lam_k1", [L, 64]); lam_q2 = din("lam_q2", [L, 64]); lam_k2 = din("lam_k2", [L, 64])
    subln_g = din("subln_g", [L, 128])
    w_pa = din("w_pa", [L, 512, D]); w_pb = din("w_pb", [L, 384, D]); w_pc = din("w_pc", [L, 512, D])
    w_o = din("w_o", [L, D, D]); w_gu = din("w_gu", [L, D, 2 * DFF]); w_down = din("w_down", [L, DFF, D])
    c_oh = din("c_oh", [4, NBUCK, LMAX]); c_mask = din("c_mask", [4, LMAX]); c_ident = din("c_ident", [128, 128])

    wb_in = dscr("wb_in", [L, D, INC]); wb_pa = dscr("wb_pa", [L, 512, D]); wb_pb = dscr("wb_pb", [L, 384, D])
    wb_pc = dscr("wb_pc", [L, 512, D]); wb_o = dscr("wb_o", [L, D, D]); wb_gu = dscr("wb_gu", [L, D, 2 * DFF])
    wb_dn = dscr("wb_dn", [L, DFF, D])
    rrep = dscr("rrep", [10, 128, LMAX]); strd = dscr("strd", [128, STRIP_COLS])
    NSEQ = len(seq_lens)
    QT = [dscr(f"QT{i}", [7, 128, s]) for i, (_, _, s) in enumerate(seq_lens)]
    KT = [dscr(f"KT{i}", [7, 128, s]) for i, (_, _, s) in enumerate(seq_lens)]
    VB = [dscr(f"VB{i}", [s // 128, 128, 390]) for i, (_, _, s) in enumerate(seq_lens)]
    VC = [dscr(f"VC{i}", [s // 128, 128, 516]) for i, (_, _, s) in enumerate(seq_lens)]
    BOT = [dscr(f"BOT{i}", [3, 128, s]) for i, (_, _, s) in enumerate(seq_lens)]
    COT = [dscr(f"COT{i}", [4, 128, s]) for i, (_, _, s) in enumerate(seq_lens)]

    with contextlib.ExitStack() as st:
        S = Sched(nc, st)
        op = S.op

        def sbt(name, shape, dt=F32):
            return st.enter_context(nc.sbuf_tensor(name, list(shape), dt))

        ident = sbt("ident", [128, 128], BF16); b_ident = S.buf("ident")
        identf = sbt("identf", [128, 128], F32)
        epsr = sbt("epsr", [128, 2], F32); b_eps = S.buf("eps")
        small = sbt("small", [128, 64], F32)
        lamt = sbt("lamt", [128, 8], F32); b_lam = S.buf("lam")
        ksc = sbt("ksc", [128, 4], F32); b_ksc = S.buf("ksc")
        RBYTES = 204 * 1024
        R = sbt("R", [128, RBYTES // 2], BF16)
        psb = [st.enter_context(nc.psum_tensor(f"ps{i}", [128, 512], F32)) for i in range(8)]
        b_ps = [S.buf(f"ps{i}") for i in range(8)]

        class Alloc:
            def __init__(self):
                self.off = 0

            def __call__(self, shape, dt):
                n = int(np.prod(shape))
                size = 2 if dt == BF16 else 4
                nb = (n * size + 63) // 64 * 64
                off = self.off
                self.off += nb
                assert self.off <= RBYTES, ("region overflow", self.off)
                a = R[:, off // 2: off // 2 + n * size // 2]
                if dt == F32:
                    a = a.bitcast(F32)
                if len(shape) == 2:
                    a = a.rearrange("p (a b) -> p a b", b=shape[1])
                elif len(shape) == 3:
                    a = a.rearrange("p (a b c) -> p a b c", b=shape[1], c=shape[2])
                elif len(shape) == 4:
                    a = a.rearrange("p (a b c d) -> p a b c d", b=shape[1], c=shape[2], d=shape[3])
                return a

        def bf(psap):
            return psap.bitcast(BF16)

        op("sp", lambda e: e.dma_start(out=identf[:], in_=c_ident[:, :]), writes=[b_ident], dma="d_ident")
        op("dve", lambda e: e.tensor_copy(out=ident[:], in_=identf[:]), reads=[b_ident], writes=[b_ident])
        op("dve", lambda e: e.memset(epsr[:, 0:1], RMS_EPS), writes=[b_eps])
        op("dve", lambda e: e.memset(epsr[:, 1:2], LN_EPS), writes=[b_eps])

        b_wb = {}
        cv_jobs = {}
        for l in range(L):
            jobs = []
            for nm, src, dst, rows in (("in", w_in, wb_in, D), ("pa", w_pa, wb_pa, 512), ("pb", w_pb, wb_pb, 384),
                                       ("pc", w_pc, wb_pc, 512), ("o", w_o, wb_o, D), ("gu", w_gu, wb_gu, D),
                                       ("dn", w_down, wb_dn, DFF)):
                b = S.buf(f"wb_{nm}{l}")
                b_wb[(nm, l)] = b
                for r0 in range(0, rows, 128):
                    jobs.append((src, dst, r0, b))
            cv_jobs[l] = jobs

        def issue_cv(l, lo, hi):
            for (src, dst, r0, b) in cv_jobs[l][lo:hi]:
                op("pool", (lambda e, src=src, dst=dst, r0=r0: e.dma_start(out=dst[l, r0:r0 + 128, :], in_=src[l, r0:r0 + 128, :])),
                   writes=[b], dma=f"d_cv{l}")

        issue_cv(0, 0, len(cv_jobs[0]))

        al = Alloc()
        tab = al([10], F32)
        tabb = al([128], F32)
        oht = al([LMAX], F32)
        mkt = al([LMAX], F32)
        rv = al([LMAX], BF16)
        evt = al([512], F32); b_evt = S.buf('evt')
        b_tab = S.buf("tab"); b_tabb = S.buf("tabb"); b_oht = S.buf("oht"); b_mkt = S.buf("mkt"); b_rv = S.buf("rv")
        b_rrep = S.buf("rrep"); b_strd = S.buf("strd")
        op("sp", lambda e: e.dma_start(out=tab[0:NBUCK, :], in_=rel_bias[:, :]), writes=[b_tab], dma="d_tab")
        strip_heads = {"c": [6, 7, 8, 9], "b0": [0, 1], "b1": [2, 3], "b2": [4, 5]}
        ri = 0
        for gi, g in enumerate(GEO_ORDER):
            Lg = geo_L(g); Wg = geo_W(g)
            op("sp", (lambda e, gi=gi, Lg=Lg: e.dma_start(out=oht[0:NBUCK, 0:Lg], in_=c_oh[gi, :, 0:Lg])), writes=[b_oht], dma="d_oht")
            op("sp", (lambda e, gi=gi, Lg=Lg: e.dma_start(out=mkt[:, 0:Lg], in_=c_mask[gi:gi + 1, 0:Lg].partition_broadcast(128))), writes=[b_mkt], dma="d_misc2")
            for hi, hcol in enumerate(strip_heads[g]):
                op("dve", (lambda e, hcol=hcol: e.tensor_copy(out=tabb[0:NBUCK, :], in_=tab[0:NBUCK, hcol:hcol + 1].to_broadcast([NBUCK, 128]))),
                   reads=[b_tab], writes=[b_tabb])
                for c0 in range(0, Lg, 512):
                    c1 = min(Lg, c0 + 512)
                    op("pe", (lambda e, c0=c0, c1=c1: e.matmul(psb[0][:, 0:c1 - c0], tabb[0:NBUCK, :], oht[0:NBUCK, c0:c1], start=True, stop=True)),
                       reads=[b_tabb, b_oht], writes=[b_ps[0]])
                    op("act", (lambda e, c0=c0, c1=c1: e.activation(out=evt[:, 0:c1 - c0], in_=psb[0][:, 0:c1 - c0], func=AF.Exp)),
                       reads=[b_ps[0]], writes=[b_evt])
                    op("dve", (lambda e, c0=c0, c1=c1: e.tensor_tensor(out=rv[:, c0:c1], in0=evt[:, 0:c1 - c0], in1=mkt[:, c0:c1], op=ALU.mult)),
                       reads=[b_evt, b_mkt], writes=[b_rv])
                op("sp", (lambda e, ri=ri, Lg=Lg: e.dma_start(out=rrep[ri, :, 0:Lg], in_=rv[:, 0:Lg])), reads=[b_rv], writes=[b_rrep], dma="d_rrep")
                soff = STRIP_OFF[g] + hi * Wg
                src = bass.AP(rrep.tensor, ri * 128 * LMAX + 127, [[LMAX - 1, 128], [1, Wg]])
                op("sp", (lambda e, soff=soff, Wg=Wg, src=src: e.dma_start(out=strd[:, soff:soff + Wg], in_=src)),
                   reads=[b_rrep], writes=[b_strd], dma="d_strd")
                ri += 1
        S.barrier()

        for si, (grp, gidx, SL) in enumerate(seq_lens):
            NT = SL // 128
            NB = SL // TB
            for l in range(L):
                xsrc = xin[grp] if l == 0 else yout[grp]
                ydst = yout[grp]

                al = Alloc()
                xt = [al([D], F32) for _ in range(4)]
                hb = [al([D], BF16) for _ in range(4)]
                hT = al([8, TB], BF16)
                g1t = al([D], F32)
                wqkv = al([8, 2688], BF16)
                junk = al([D], F32)
                sq = [al([512], F32) for _ in range(4)]
                qn = [al([512], BF16) for _ in range(4)]
                qst = al([7, TB], BF16)
                kst = al([7, TB], BF16)
                vbst = al([4, 6, 65], BF16)
                vcst = al([4, 4, 129], BF16)
                svs = [al([8], F32) for _ in range(4)]
                b_xt = [S.buf(f"xt{t}") for t in range(4)]
                b_hb = [S.buf(f"hb{t}") for t in range(4)]
                b_hT = S.buf("hT"); b_g1 = S.buf("g1"); b_wqkv = S.buf("wqkv"); b_junk = S.buf("junk")
                b_sq = [S.buf(f"sq{i}") for i in range(4)]; b_qn = [S.buf(f"qn{i}") for i in range(4)]
                b_qst = S.buf("qst"); b_kst = S.buf("kst"); b_vbst = S.buf("vbst"); b_vcst = S.buf("vcst")
                b_ss = [S.buf(f"ss{t}") for t in range(4)]
                b_qs = [S.buf(f"qs{i}") for i in range(4)]
                lag1 = Lag(2)
                qkc = [0]
                b_QT = S.buf("QTd"); b_KT = S.buf("KTd"); b_VB = S.buf("VBd"); b_VC = S.buf("VCd")
                b_BOT = S.buf("BOTd"); b_COT = S.buf("COTd")

                op("sp", (lambda e, l=l: e.dma_start(out=g1t[:, :], in_=norm1_g[l:l + 1, :].partition_broadcast(128))), writes=[b_g1], dma="d_g1")
                op("sp", (lambda e, l=l: e.dma_start(out=wqkv[:, :, :], in_=wb_in[l, :, 1024:3712].rearrange("(c p) n -> p c n", p=128))),
                   reads=[b_wb[("in", l)]], writes=[b_wqkv], dma="d_wqkv")
                for half in range(2):
                    for j, src in enumerate((qn_b, kn_b, qn_c, kn_c)):
                        op("sp", (lambda e, l=l, half=half, j=j, src=src: e.dma_start(
                            out=lamt[half * 64:(half + 1) * 64, j:j + 1], in_=src[l:l + 1, :].rearrange("o d -> d o"))),
                           writes=[b_lam], dma="d_lam")
                op("dve", lambda e: e.tensor_tensor(out=ksc[:, 0:1], in0=lamt[:, 0:1], in1=lamt[:, 1:2], op=ALU.mult), reads=[b_lam], writes=[b_ksc])
                op("dve", lambda e: e.tensor_tensor(out=ksc[:, 1:2], in0=lamt[:, 2:3], in1=lamt[:, 3:4], op=ALU.mult), reads=[b_lam], writes=[b_ksc])
                op("dve", lambda e: e.tensor_scalar(out=ksc[:, 0:2], in0=ksc[:, 0:2], scalar1=0.125, scalar2=None, op0=ALU.mult), reads=[b_ksc], writes=[b_ksc])
                op("dve", lambda e: e.memset(vbst[:, :, :, 64:65], 1.0), writes=[b_vbst])
                op("dve", lambda e: e.memset(vcst[:, :, :, 128:129], 1.0), writes=[b_vcst])

                rot = [0]

                def nxt(n=6):
                    i = rot[0] % n
                    rot[0] += 1
                    return i

                def norm_tile(t, tok0, src_ap, gt, b_g, trbank, ldq="pool", part="ab"):
                    hbi = t % 4
                    if "a" in part:
                        norm_a(t, tok0, src_ap, gt, b_g, ldq, hbi, xt, b_xt, f"d_x{t}")
                    if "b" in part:
                        norm_b_(t, trbank, hbi)

                def norm_a(t, tok0, src_ap, gt, b_g, ldq, hbi, xt, b_xt, key):
                    op(ldq, (lambda e: e.dma_start(out=xt[t][:, :], in_=src_ap[tok0:tok0 + 128, :])), writes=[b_xt[t]], dma=key)
                    op("act", (lambda e: e.activation(out=junk[:, :], in_=xt[t][:, :], func=AF.Square, accum_out=small[:, t:t + 1])),
                       reads=[b_xt[t]], writes=[b_junk, b_ss[t]])
                    op("act", (lambda e: e.activation(out=small[:, t:t + 1], in_=small[:, t:t + 1], func=AF.Sqrt, scale=1.0 / D, bias=epsr[:, 0:1])),
                       reads=[b_ss[t], b_eps], writes=[b_ss[t]])
                    op("dve", (lambda e: e.reciprocal(out=small[:, t:t + 1], in_=small[:, t:t + 1])), reads=[b_ss[t]], writes=[b_ss[t]])
                    op("dve", (lambda e: e.scalar_tensor_tensor(out=hb[hbi][:, :], in0=xt[t][:, :], scalar=small[:, t:t + 1], in1=gt[:, :],
                                                                 op0=ALU.mult, op1=ALU.mult)),
                       reads=[b_xt[t], b_ss[t], b_g], writes=[b_hb[hbi]])

                def norm_b_(t, trbank, hbi):
                    pv = bf(psb[trbank][:, :])
                    for c in range(8):
                        op("pe", (lambda e, c=c: e.transpose(pv[:, c * 128:(c + 1) * 128], hb[hbi][:, c * 128:(c + 1) * 128], ident[:, :])),
                           reads=[b_hb[hbi], b_ident], writes=[b_ps[trbank]])
                    op("act", (lambda e: e.activation(out=hT[:, :, t * 128:(t + 1) * 128], in_=pv.rearrange("p (c q) -> p c q", q=128), func=AF.Copy)),
                       reads=[b_ps[trbank]], writes=[b_hT])

                blocks = [("q", 0, 3, 0, 0), ("k", 384, 3, 0, 0), ("v", 768, 3, 0, 0),
                          ("q", 1152, 4, 3, 1), ("k", 1664, 4, 3, 1), ("v", 2176, 4, 3, 1)]
                for tb in range(NB):
                    for t in range(4):
                        norm_tile(t, tb * TB + t * 128, xsrc[gidx], g1t, b_g1, 6 + (t % 2), ldq="sp", part="a")
                    for t in range(4):
                        norm_tile(t, tb * TB + t * 128, xsrc[gidx], g1t, b_g1, 6 + (t % 2), ldq="sp", part="b")
                    for t in range(4):
                        for bi, (kind, c0, nch, ch0, isc) in enumerate(blocks):
                            ncol = nch * 128
                            pb_i = nxt()
                            pso = psb[pb_i]
                            for k in range(8):
                                op("pe", (lambda e, k=k, pso=pso, c0=c0, ncol=ncol, t=t: e.matmul(
                                    pso[:, 0:ncol], hT[:, k, t * 128:(t + 1) * 128], wqkv[:, k, c0:c0 + ncol], start=(k == 0), stop=(k == 7))),
                                   reads=[b_hT, b_wqkv], writes=[b_ps[pb_i]])
                            if kind == "v":
                                if isc == 0:
                                    op("dve", (lambda e, pso=pso, t=t: e.tensor_copy(out=vbst[:, t, :, 0:64], in_=pso[:, 0:384].rearrange("p (h d) -> p h d", d=64))),
                                       reads=[b_ps[pb_i]], writes=[b_vbst])
                                else:
                                    op("act", (lambda e, pso=pso, t=t: e.activation(out=vcst[:, t, :, 0:128], in_=pso[:, 0:512].rearrange("p (h d) -> p h d", d=128), func=AF.Copy)),
                                       reads=[b_ps[pb_i]], writes=[b_vcst])
                                continue
                            nh = ncol // 64
                            j = qkc[0] % 4
                            qkc[0] += 1
                            sv = svs[j]
                            op("act", (lambda e, pso=pso, ncol=ncol, j=j: e.activation(out=sq[j][:, 0:ncol], in_=pso[:, 0:ncol], func=AF.Square)),
                               reads=[b_ps[pb_i]], writes=[b_sq[j]])
                            op("dve", (lambda e, ncol=ncol, j=j, sv=sv, nh=nh: e.tensor_reduce(out=sv[:, 0:nh], in_=sq[j][:, 0:ncol].rearrange("p (h d) -> p h d", d=64), axis=AX.X, op=ALU.add)),
                               reads=[b_sq[j]], writes=[b_qs[j]])
                            op("act", (lambda e, sv=sv, nh=nh: e.activation(out=sv[:, 0:nh], in_=sv[:, 0:nh], func=AF.Sqrt, scale=1.0 / 64, bias=epsr[:, 0:1])),
                               reads=[b_qs[j], b_eps], writes=[b_qs[j]])
                            op("dve", (lambda e, sv=sv, nh=nh: e.reciprocal(out=sv[:, 0:nh], in_=sv[:, 0:nh])), reads=[b_qs[j]], writes=[b_qs[j]])
                            if isc == 0:
                                op("dve", (lambda e, pso=pso, ncol=ncol, j=j, sv=sv, nh=nh: e.tensor_tensor(
                                    out=qn[j][:, 0:ncol].rearrange("p (h d) -> p h d", d=64), in0=pso[:, 0:ncol].rearrange("p (h d) -> p h d", d=64),
                                    in1=sv[:, 0:nh].unsqueeze(2).to_broadcast([128, nh, 64]), op=ALU.mult)),
                                   reads=[b_ps[pb_i], b_qs[j]], writes=[b_qn[j]])
                            else:
                                op("dve", (lambda e, pso=pso, j=j, sv=sv: e.tensor_tensor(
                                    out=qn[j][:, 0:512].rearrange("p (h m d) -> p m h d", h=4, m=2),
                                    in0=pso[:, 0:512].rearrange("p (m h d) -> p m h d", m=2, h=4),
                                    in1=sv[:, 0:8].rearrange("p (m h) -> p m h", m=2).unsqueeze(3).to_broadcast([128, 2, 4, 64]), op=ALU.mult)),
                                   reads=[b_ps[pb_i], b_qs[j]], writes=[b_qn[j]])
                            def tr_part(kind=kind, nch=nch, ch0=ch0, isc=isc, t=t, j=j, bi=bi):
                                trb = 6 + (bi % 2)
                                pv = bf(psb[trb][:, :])
                                for c in range(nch):
                                    src = qn[j][:, c * 128:(c + 1) * 128]
                                    op("pe", (lambda e, c=c, src=src, pv=pv: e.transpose(pv[:, c * 128:(c + 1) * 128], src, ident[:, :])),
                                       reads=[b_qn[j], b_ident], writes=[b_ps[trb]])
                                dstt = qst if kind == "q" else kst
                                b_dst = b_qst if kind == "q" else b_kst
                                if kind == "q":
                                    op("dve", (lambda e: e.tensor_copy(
                                        out=dstt[:, ch0:ch0 + nch, t * 128:(t + 1) * 128], in_=pv[:, 0:nch * 128].rearrange("p (c q) -> p c q", q=128))),
                                       reads=[b_ps[trb]], writes=[b_dst])
                                else:
                                    op("act", (lambda e: e.activation(
                                        out=dstt[:, ch0:ch0 + nch, t * 128:(t + 1) * 128], in_=pv[:, 0:nch * 128].rearrange("p (c q) -> p c q", q=128),
                                        func=AF.Copy, scale=ksc[:, isc:isc + 1])),
                                       reads=[b_ps[trb], b_ksc], writes=[b_dst])
                            lag1.push(tr_part)

                    def tb_stores(tb=tb):
                        s0 = tb * TB
                        op("pool", (lambda e: e.dma_start(out=QT[si][:, :, s0:s0 + TB].rearrange("c p s -> p c s"), in_=qst[:, :, :])), reads=[b_qst], writes=[b_QT], dma="d_qst")
                        op("pool", (lambda e: e.dma_start(out=KT[si][:, :, s0:s0 + TB].rearrange("c p s -> p c s"), in_=kst[:, :, :])), reads=[b_kst], writes=[b_KT], dma="d_kst")
                        op("pool", (lambda e: e.dma_start(out=VB[si][tb * 4:(tb + 1) * 4, :, :].rearrange("t p c -> p t c"), in_=vbst[:, :, :, :].rearrange("p t h d -> p t (h d)"))),
                           reads=[b_vbst], writes=[b_VB], dma="d_vbst")
                        op("pool", (lambda e: e.dma_start(out=VC[si][tb * 4:(tb + 1) * 4, :, :].rearrange("t p c -> p t c"), in_=vcst[:, :, :, :].rearrange("p t h d -> p t (h d)"))),
                           reads=[b_vcst], writes=[b_VC], dma="d_vcst")
                    lag1.push(tb_stores, main=False)
                lag1.flush()
                S.barrier()

                al = Alloc()
                strips = al([STRIP_COLS], BF16)
                vcall = al([NT, 516], BF16)
                slotA = al([3 * SL], BF16)
                slotB = al([3 * SL], BF16)
                slotC = al([NT * 390], BF16)
                pt = [al([512], BF16) for _ in range(6)]
                stage_b = al([4, 6, 65], F32)
                stage_c = al([4, 2, 129], F32)
                o_t = al([4, 128], F32); t_t = al([4, 128], F32); q_t = al([4, 128], F32)
                bo_tm = al([4, 384], BF16); co_tm = al([4, 128], BF16)
                boT_st = al([3, TB], BF16); coT_st = al([TB], BF16)
                sgt = al([128], F32)
                zz = al([64], F32)
                b_strips = S.buf("strips"); b_vcall = S.buf("vcall")
                b_slot = [S.buf("slotA"), S.buf("slotB"), S.buf("slotC")]
                b_pt = [S.buf(f"pt{i}") for i in range(6)]
                b_stb = S.buf("stage_b"); b_stc = S.buf("stage_c")
                b_ot = S.buf("o_t"); b_tt = S.buf("t_t"); b_qt = S.buf("q_t")
                b_botm = S.buf("bo_tm"); b_cotm = S.buf("co_tm"); b_boT = S.buf("boT_st"); b_coT = S.buf("coT_st")
                b_sgt = S.buf("sgt"); b_zz = S.buf("zz")
                slots = [slotA, slotB, slotC]

                op("sp", lambda e: e.dma_start(out=strips[:, :], in_=strd[:, :]), reads=[b_strd], writes=[b_strips], dma="d_strips")
                for t0 in range(0, NT, 8):
                    op("sp", (lambda e, t0=t0: e.dma_start(out=vcall[:, t0:min(NT, t0 + 8), :], in_=VC[si][t0:min(NT, t0 + 8), :, :].rearrange("t p c -> p t c"))),
                       reads=[b_VC], writes=[b_vcall], dma="d_vcall")
                for j, src in enumerate((lam_q1, lam_k1, lam_q2, lam_k2)):
                    op("sp", (lambda e, l=l, j=j, src=src: e.dma_start(out=o_t[:, j, 0:64], in_=src[l:l + 1, :].partition_broadcast(128))),
                       writes=[b_ot], dma="d_lamv")
                op("dve", lambda e: e.tensor_tensor(out=t_t[:, 0, 0:64], in0=o_t[:, 0, 0:64], in1=o_t[:, 1, 0:64], op=ALU.mult), reads=[b_ot], writes=[b_tt])
                op("dve", lambda e: e.tensor_tensor(out=t_t[:, 1, 0:64], in0=o_t[:, 2, 0:64], in1=o_t[:, 3, 0:64], op=ALU.mult), reads=[b_ot], writes=[b_tt])
                op("dve", lambda e: e.tensor_reduce(out=zz[:, 0:2], in_=t_t[:, 0:2, 0:64], axis=AX.X, op=ALU.add), reads=[b_tt], writes=[b_zz])
                op("act", lambda e: e.activation(out=zz[:, 2:4], in_=zz[:, 0:2], func=AF.Exp), reads=[b_zz], writes=[b_zz])
                li = float(lam_inits[l])
                op("dve", lambda e: e.tensor_tensor(out=lamt[:, 4:5], in0=zz[:, 3:4], in1=zz[:, 2:3], op=ALU.subtract), reads=[b_zz], writes=[b_lam])
                op("dve", lambda e: e.tensor_scalar(out=lamt[:, 4:5], in0=lamt[:, 4:5], scalar1=-li, scalar2=None, op0=ALU.add), reads=[b_lam], writes=[b_lam])
                op("sp", (lambda e, l=l: e.dma_start(out=sgt[:, :], in_=subln_g[l:l + 1, :].partition_broadcast(128))), writes=[b_sgt], dma="d_sgt")
                op("dve", lambda e: e.tensor_scalar(out=sgt[:, :], in0=sgt[:, :], scalar1=1.0 - li, scalar2=None, op0=ALU.mult), reads=[b_sgt], writes=[b_sgt])

                srot = [0]
                prot = [0]
                arot = [0]
                lag15 = Lag(2)

                def score_pair(tiles):
                    pr = srot[0] % 2
                    pq = srot[0] % 3
                    srot[0] += 1
                    info = []
                    for j, (lhsT, rhs, strip_win, pvs, extra_reads) in enumerate(tiles):
                        sb_i = 3 + 2 * pr + j
                        pi = 2 * pq + j
                        op("pe", (lambda e, sb_i=sb_i, lhsT=lhsT, rhs=rhs: e.matmul(psb[sb_i][:, :], lhsT, rhs, start=True, stop=True)), reads=extra_reads, writes=[b_ps[sb_i]])
                        info.append((sb_i, pi, strip_win, pvs, extra_reads))
                    for (sb_i, pi, strip_win, pvs, extra_reads) in info:
                        op("act", (lambda e, sb_i=sb_i, pi=pi: e.activation(out=pt[pi][:, :], in_=psb[sb_i][:, :], func=AF.Exp)), reads=[b_ps[sb_i]], writes=[b_pt[pi]])
                    for (sb_i, pi, strip_win, pvs, extra_reads) in info:
                        op("dve", (lambda e, pi=pi, strip_win=strip_win: e.tensor_tensor(out=pt[pi][:, :], in0=pt[pi][:, :], in1=strip_win, op=ALU.mult)), reads=[b_pt[pi], b_strips], writes=[b_pt[pi]])

                    def pv_part():
                        for (sb_i, pi, strip_win, pvs, extra_reads) in info:
                            for (oap, bi_, qs, rap, stt, stp) in pvs:
                                op("pe", (lambda e, oap=oap, qs=qs, rap=rap, stt=stt, stp=stp, pi=pi: e.matmul(oap, pt[pi][:, qs * 128:(qs + 1) * 128], rap, start=stt, stop=stp, skip_group_check=True)),
                                   reads=[b_pt[pi]] + extra_reads, writes=[b_ps[bi_]])
                    lag15.push(pv_part)

                ktb = slotA.rearrange("p (c s) -> p c s", s=SL)
                qtb = slotB.rearrange("p (c s) -> p c s", s=SL)
                vbt = slotC.rearrange("p (t c) -> p t c", c=390)
                op("sp", lambda e: e.dma_start(out=ktb, in_=KT[si][0:3, :, :].rearrange("c p s -> p c s")), reads=[b_KT], writes=[b_slot[0]], dma="d_slot0")
                op("sp", lambda e: e.dma_start(out=qtb, in_=QT[si][0:3, :, :].rearrange("c p s -> p c s")), reads=[b_QT], writes=[b_slot[1]], dma="d_slot1")
                for t0 in range(0, NT, 8):
                    op("sp", (lambda e, t0=t0: e.dma_start(out=vbt[:, t0:min(NT, t0 + 8), :], in_=VB[si][t0:min(NT, t0 + 8), :, :].rearrange("t p c -> p t c"))),
                       reads=[b_VB], writes=[b_slot[2]], dma="d_slot2")
                for qb in range(NB):
                    for g in range(3):
                        gname = f"b{g}"
                        dmin, dmax, _, _ = GEO[gname]
                        Wg = geo_W(gname)
                        kts = [kt for kt in range(NT) if dmin <= kt * 128 - qb * TB <= dmax]
                        for i, kt in enumerate(kts):
                            c0 = dmax - (kt * 128 - qb * TB)
                            tiles = []
                            for hp in range(2):
                                m = g * 2 + hp
                                soff = STRIP_OFF[gname] + hp * Wg
                                pvs = [(psb[hp][:, qs * 65:(qs + 1) * 65], hp, qs, vbt[:, kt, m * 65:(m + 1) * 65],
                                        (i == 0 and qs == 0), (i == len(kts) - 1)) for qs in range(4)]
                                tiles.append((ktb[hp * 64:(hp + 1) * 64, g, kt * 128:(kt + 1) * 128], qtb[hp * 64:(hp + 1) * 64, g, qb * TB:(qb + 1) * TB],
                                              strips[:, soff + c0:soff + c0 + 512], pvs, [b_slot[0], b_slot[1], b_slot[2]]))
                            score_pair(tiles)

                        def evac_b(g=g):
                            for hp in range(2):
                                m = g * 2 + hp
                                op("dve", (lambda e, hp=hp, m=m: e.tensor_copy(out=stage_b[:, :, m, :], in_=psb[hp][:, 0:260].rearrange("p (q c) -> p q c", c=65))),
                                   reads=[b_ps[hp]], writes=[b_stb])
                        lag15.push(evac_b, main=False)
                    def norm_b(qb=qb):
                        zv = stage_b[:, :, :, 64]
                        op("dve", lambda e: e.tensor_tensor(out=zz[:, 0:8].rearrange("p (q h) -> p q h", h=2), in0=zv[:, :, 0:2], in1=zv[:, :, 2:4], op=ALU.add), reads=[b_stb], writes=[b_zz])
                        op("dve", lambda e: e.tensor_tensor(out=zz[:, 0:8].rearrange("p (q h) -> p q h", h=2), in0=zz[:, 0:8].rearrange("p (q h) -> p q h", h=2), in1=zv[:, :, 4:6], op=ALU.add), reads=[b_stb, b_zz], writes=[b_zz])
                        op("dve", lambda e: e.reciprocal(out=zz[:, 8:16], in_=zz[:, 0:8]), reads=[b_zz], writes=[b_zz])
                        for g in range(3):
                            op("dve", (lambda e, g=g: e.tensor_tensor(
                                out=bo_tm[:, :, g * 128:(g + 1) * 128].rearrange("p q (h d) -> p q h d", d=64),
                                in0=stage_b[:, :, 2 * g:2 * g + 2, 0:64],
                                in1=zz[:, 8:16].rearrange("p (q h) -> p q h", h=2).unsqueeze(3).to_broadcast([128, 4, 2, 64]), op=ALU.mult)),
                               reads=[b_stb, b_zz], writes=[b_botm])
                        pv = bf(psb[7][:, :])
                        for qs in range(4):
                            for c in range(3):
                                op("pe", (lambda e, qs=qs, c=c: e.transpose(pv[:, c * 128:(c + 1) * 128], bo_tm[:, qs, c * 128:(c + 1) * 128], ident[:, :])),
                                   reads=[b_botm, b_ident], writes=[b_ps[7]])
                            op("act", (lambda e, qs=qs: e.activation(out=boT_st[:, :, qs * 128:(qs + 1) * 128], in_=pv[:, 0:384].rearrange("p (c q) -> p c q", q=128), func=AF.Copy)),
                               reads=[b_ps[7]], writes=[b_boT])
                        op("pool", (lambda e, qb=qb: e.dma_start(out=BOT[si][:, :, qb * TB:(qb + 1) * TB].rearrange("c p s -> p c s"), in_=boT_st[:, :, :])),
                           reads=[b_boT], writes=[b_BOT], dma="d_boT")
                    lag15.push(norm_b, main=False)

                dmin, dmax, _, _ = GEO["c"]
                Wc = geo_W("c")
                for h in range(4):
                    lag15.flush()
                    sl = slots[h % 3]
                    bsl = b_slot[h % 3]
                    ktc = sl[:, 0:SL]
                    qtc = sl[:, SL:2 * SL]
                    op("sp", (lambda e, h=h, ktc=ktc: e.dma_start(out=ktc, in_=KT[si][3 + h, :, :])), reads=[b_KT], writes=[bsl], dma=f"d_slot{h % 3}")
                    op("sp", (lambda e, h=h, qtc=qtc: e.dma_start(out=qtc, in_=QT[si][3 + h, :, :])), reads=[b_QT], writes=[bsl], dma=f"d_slot{h % 3}")
                    soff = STRIP_OFF["c"] + h * Wc
                    for qb in range(NB):
                        for kt in range(NT):
                            dl = min(max(kt * 128 - qb * TB, dmin), dmax)
                            c0 = dmax - dl
                            tiles = []
                            for mp in range(2):
                                pvs = []
                                for qs in range(4):
                                    idx = mp * 4 + qs
                                    bk = idx // 3
                                    col = (idx % 3) * 129
                                    pvs.append((psb[bk][:, col:col + 129], bk, qs, vcall[:, kt, h * 129:(h + 1) * 129], (kt == 0 and idx % 3 == 0), (kt == NT - 1)))
                                tiles.append((ktc[mp * 64:(mp + 1) * 64, kt * 128:(kt + 1) * 128], qtc[mp * 64:(mp + 1) * 64, qb * TB:(qb + 1) * TB],
                                              strips[:, soff + c0:soff + c0 + 512], pvs, [bsl, b_vcall]))
                            score_pair(tiles)

                        def evac_c():
                            op("dve", lambda e: e.tensor_copy(out=stage_c[:, 0:3, 0, :], in_=psb[0][:, 0:387].rearrange("p (q c) -> p q c", c=129)), reads=[b_ps[0]], writes=[b_stc])
                            op("dve", lambda e: e.tensor_copy(out=stage_c[:, 3, 0, :], in_=psb[1][:, 0:129]), reads=[b_ps[1]], writes=[b_stc])
                            op("dve", lambda e: e.tensor_copy(out=stage_c[:, 0:2, 1, :], in_=psb[1][:, 129:387].rearrange("p (q c) -> p q c", c=129)), reads=[b_ps[1]], writes=[b_stc])
                            op("dve", lambda e: e.tensor_copy(out=stage_c[:, 2:4, 1, :], in_=psb[2][:, 0:258].rearrange("p (q c) -> p q c", c=129)), reads=[b_ps[2]], writes=[b_stc])
                        lag15.push(evac_c, main=False)
                        def norm_c(h=h, qb=qb):
                            rz = zz[:, 16:24].rearrange("p (q m) -> p q m", m=2)
                            op("dve", lambda e: e.reciprocal(out=rz, in_=stage_c[:, :, :, 128]), reads=[b_stc], writes=[b_zz])
                            op("dve", lambda e: e.tensor_scalar(out=zz[:, 24:28], in0=rz[:, :, 1], scalar1=lamt[:, 4:5], scalar2=None, op0=ALU.mult), reads=[b_zz, b_lam], writes=[b_zz])
                            op("dve", lambda e: e.tensor_tensor(out=o_t[:, :, :], in0=stage_c[:, :, 0, 0:128], in1=rz[:, :, 0].unsqueeze(2).to_broadcast([128, 4, 128]), op=ALU.mult),
                               reads=[b_stc, b_zz], writes=[b_ot])
                            op("dve", lambda e: e.tensor_tensor(out=t_t[:, :, :], in0=stage_c[:, :, 1, 0:128], in1=zz[:, 24:28].unsqueeze(2).to_broadcast([128, 4, 128]), op=ALU.mult),
                               reads=[b_stc, b_zz], writes=[b_tt])
                            op("dve", lambda e: e.tensor_tensor(out=o_t[:, :, :], in0=o_t[:, :, :], in1=t_t[:, :, :], op=ALU.add), reads=[b_ot, b_tt], writes=[b_ot])
                            op("act", lambda e: e.activation(out=q_t[:, :, :], in_=o_t[:, :, :], func=AF.Square), reads=[b_ot], writes=[b_qt])
                            op("dve", lambda e: e.tensor_reduce(out=zz[:, 28:32], in_=q_t[:, :, :], axis=AX.X, op=ALU.add), reads=[b_qt], writes=[b_zz])
                            op("act", lambda e: e.activation(out=zz[:, 28:32], in_=zz[:, 28:32], func=AF.Sqrt, scale=1.0 / 128, bias=epsr[:, 0:1]), reads=[b_zz, b_eps], writes=[b_zz])
                            op("dve", lambda e: e.reciprocal(out=zz[:, 28:32], in_=zz[:, 28:32]), reads=[b_zz], writes=[b_zz])
                            op("dve", lambda e: e.tensor_tensor(out=o_t[:, :, :], in0=o_t[:, :, :], in1=zz[:, 28:32].unsqueeze(2).to_broadcast([128, 4, 128]), op=ALU.mult),
                               reads=[b_ot, b_zz], writes=[b_ot])
                            op("dve", lambda e: e.tensor_tensor(out=co_tm[:, :, :], in0=o_t[:, :, :], in1=sgt[:, :].unsqueeze(1).to_broadcast([128, 4, 128]), op=ALU.mult),
                               reads=[b_ot, b_sgt], writes=[b_cotm])
                            pv = bf(psb[7][:, :])
                            for qs in range(4):
                                op("pe", (lambda e, qs=qs: e.transpose(pv[:, qs * 128:(qs + 1) * 128], co_tm[:, qs, :], ident[:, :])), reads=[b_cotm, b_ident], writes=[b_ps[7]])
                            op("act", lambda e: e.activation(out=coT_st[:, :], in_=pv[:, 0:512], func=AF.Copy), reads=[b_ps[7]], writes=[b_coT])
                            op("pool", (lambda e, h=h, qb=qb: e.dma_start(out=COT[si][h, :, qb * TB:(qb + 1) * TB], in_=coT_st[:, :])), reads=[b_coT], writes=[b_COT], dma="d_coT")
                        lag15.push(norm_c, main=False)
                lag15.flush()
                S.barrier()

                al = Alloc()
                xt2 = [[al([D], F32) for _ in range(4)] for _ in range(2)]
                hb = [al([D], BF16) for _ in range(4)]
                hT = al([8, TB], BF16)
                g1t = al([D], F32)
                g2t = al([D], F32)
                junk = al([D], F32)
                lng = al([512], F32); lnb = al([512], F32)
                sgb = al([4, 128], F32)
                wgn = al([8, 128], F32)
                wgnb = al([8, 128], BF16)
                wgT = al([8, 128], BF16)
                gv = [al([512], F32) for _ in range(4)]
                vn = al([4, 512], BF16)
                uT = al([4, TB], BF16)
                tmpa = [al([TB], F32) for _ in range(2)]
                aT = al([4, TB], BF16)
                bcT = al([7, TB], BF16)
                sg = [al([TB], F32) for _ in range(3)]
                mm_ = [al([TB], F32) for _ in range(2)]
                mT = al([8, TB], BF16)
                sgf = [al([TB], BF16) for _ in range(3)]
                actT = al([22, TB], BF16)
                bst = al([24], F32)
                ring_n = 5
                SLOTB = 11 * 512
                ring = [al([SLOTB], BF16) for _ in range(ring_n)]
                b_xt2 = [[S.buf(f"xt{j}_{t}") for t in range(4)] for j in range(2)]
                b_hb = [S.buf(f"hb{t}") for t in range(4)]
                b_hT = S.buf("hT"); b_g1 = S.buf("g1"); b_g2 = S.buf("g2"); b_junk = S.buf("junk")
                b_ss = [S.buf(f"ss{t}") for t in range(4)]
                b_ln = S.buf("ln"); b_sgb = S.buf("sgb"); b_wg = S.buf("wg"); b_wgT = S.buf("wgT")
                b_gv = [S.buf(f"gv{t}") for t in range(4)]; b_vn = [S.buf(f"vn{t}") for t in range(4)]; b_uT = S.buf("uT")
                b_tmpa = [S.buf("tmpa0"), S.buf("tmpa1")]; b_aT = S.buf("aT"); b_bcT = S.buf("bcT")
                b_sg = [S.buf(f"sg{i}") for i in range(3)]; b_mm = [S.buf("mm0"), S.buf("mm1")]; b_mT = S.buf("mT")
                b_sgf = [S.buf(f"sgf{i}") for i in range(3)]; b_actT = S.buf("actT"); b_bst = [S.buf(f"bst{t}") for t in range(4)]
                b_ring = [S.buf(f"ring{i}") for i in range(ring_n)]
                b_BOT = S.buf("BOTd2"); b_COT = S.buf("COTd2")

                op("sp", (lambda e, l=l: e.dma_start(out=g1t[:, :], in_=norm1_g[l:l + 1, :].partition_broadcast(128))), writes=[b_g1], dma="d_g1")
                op("sp", (lambda e, l=l: e.dma_start(out=g2t[:, :], in_=norm2_g[l:l + 1, :].partition_broadcast(128))), writes=[b_g2], dma="d_g2")
                op("sp", (lambda e, l=l: e.dma_start(out=lng[:, :], in_=sgu_ln_g[l:l + 1, :].partition_broadcast(128))), writes=[b_ln], dma="d_ln")
                op("sp", (lambda e, l=l: e.dma_start(out=lnb[:, :], in_=sgu_ln_b[l:l + 1, :].partition_broadcast(128))), writes=[b_ln], dma="d_ln")
                for par in range(2):
                    src = bass.AP(sgu_b.tensor, l * 8 * 128 + par * 128, [[0, 64], [256, 4], [1, 128]])
                    op("sp", (lambda e, par=par, src=src: e.dma_start(out=sgb[par * 64:(par + 1) * 64, :, :], in_=src)), writes=[b_sgb], dma="d_sgb")
                op("sp", (lambda e, l=l: e.dma_start(out=wgn[:, :, :], in_=sgu_w[l, :, :, :].rearrange("g p q -> p g q"))), writes=[b_wg], dma="d_wgn")
                op("dve", lambda e: e.tensor_copy(out=wgnb[:, :, :], in_=wgn[:, :, :]), reads=[b_wg], writes=[b_wg])
                pv = bf(psb[7][:, :])
                for g in range(8):
                    op("pe", (lambda e, g=g: e.transpose(pv[:, g * 128:(g + 1) * 128], wgnb[:, g, :], ident[:, :])), reads=[b_wg, b_ident], writes=[b_ps[7]])
                op("dve", lambda e: e.tensor_copy(out=wgT[:, :, :], in_=pv.rearrange("p (g q) -> p g q", q=128)), reads=[b_ps[7]], writes=[b_wgT])

                pieces = []
                for tb in range(NB):
                    pieces.append([(("in", l), 8, 512, lambda l=l: wb_in[l, :, 512:1024], 0)])
                    pieces.append([(("in", l), 8, 512, lambda l=l: wb_in[l, :, 0:512], 0)])
                    for hf in range(2):
                        c0 = hf * 512
                        pieces.append([(("pa", l), 4, 512, lambda l=l, c0=c0: wb_pa[l, :, c0:c0 + 512], 0),
                                       (("pb", l), 3, 512, lambda l=l, c0=c0: wb_pb[l, :, c0:c0 + 512], 4),
                                       (("pc", l), 4, 512, lambda l=l, c0=c0: wb_pc[l, :, c0:c0 + 512], 7)])
                        for gi in range(3):
                            cc = 3712 + gi * 1024 + c0
                            pieces.append([(("in", l), 8, 512, lambda l=l, cc=cc: wb_in[l, :, cc:cc + 512], 0)])
                    for hf in range(2):
                        pieces.append([(("o", l), 8, 512, lambda l=l, hf=hf: wb_o[l, :, hf * 512:(hf + 1) * 512], 0)])
                    for f in range(11):
                        pieces.append([(("gu", l), 8, 256, lambda l=l, f=f: wb_gu[l, :, f * 256:(f + 1) * 256], 0),
                                       (("gu", l), 8, 256, lambda l=l, f=f: wb_gu[l, :, DFF + f * 256:DFF + (f + 1) * 256], 8)])
                    for hf in range(2):
                        for kh in range(2):
                            pieces.append([(("dn", l), 11, 512, lambda l=l, hf=hf, kh=kh: wb_dn[l, kh * 1408:(kh + 1) * 1408, hf * 512:(hf + 1) * 512], 0)])
                wstate = {"next_load": 0, "next_acq": 0, "held": []}

                def w_issue():
                    i = wstate["next_load"]
                    if i >= len(pieces):
                        return
                    slot = i % ring_n
                    for (wkey, nk, ncol, srcf, k0) in pieces[i]:
                        dst = ring[slot][:, k0 * ncol:(k0 + nk) * ncol].rearrange("p (k n) -> p k n", n=ncol)
                        src = srcf().rearrange("(k p) n -> p k n", p=128)
                        op("sp", (lambda e, dst=dst, src=src: e.dma_start(out=dst, in_=src)), reads=[b_wb[wkey]], writes=[b_ring[slot]], dma=f"d_ring{slot}")
                    wstate["next_load"] += 1

                def w_acquire():
                    i = wstate["next_acq"]
                    wstate["next_acq"] += 1
                    assert i < wstate["next_load"], "weight ring underflow"
                    slot = i % ring_n
                    return ring[slot], b_ring[slot]

                def w_release():
                    w_issue()

                for _ in range(ring_n):
                    w_issue()

                for tb in range(NB):
                    s0 = tb * TB
                    xset = tb % 2
                    xt = xt2[xset]
                    b_xt = b_xt2[xset]
                    if tb == 0:
                        for t in range(4):
                            norm_a(t, s0 + t * 128, xsrc[gidx], g1t, b_g1, "pool", t % 4, xt, b_xt, f"d_x{xset}_{t}")
                    for t in range(4):
                        norm_b_(t, 6 + (t % 2), t % 4)
                    op("pool", (lambda e, s0=s0: e.dma_start(out=bcT[:, 0:3, :], in_=BOT[si][:, :, s0:s0 + TB].rearrange("c p s -> p c s"))), reads=[b_BOT], writes=[b_bcT], dma="d_bcT")
                    op("pool", (lambda e, s0=s0: e.dma_start(out=bcT[:, 3:7, :], in_=COT[si][:, :, s0:s0 + TB].rearrange("c p s -> p c s"))), reads=[b_COT], writes=[b_bcT], dma="d_bcT")
                    wv, bwv = w_acquire()
                    wv3 = wv[:, 0:8 * 512].rearrange("p (k n) -> p k n", n=512)
                    for t in range(4):
                        pb_i = nxt()
                        for k in range(8):
                            op("pe", (lambda e, k=k, t=t, pb_i=pb_i: e.matmul(psb[pb_i][:, :], hT[:, k, t * 128:(t + 1) * 128], wv3[:, k, :], start=(k == 0), stop=(k == 7))),
                               reads=[b_hT, bwv], writes=[b_ps[pb_i]])
                        op("act", (lambda e, pb_i=pb_i, t=t: e.activation(out=gv[t][:, :], in_=psb[pb_i][:, :], func=AF.Gelu_apprx_tanh)), reads=[b_ps[pb_i]], writes=[b_gv[t]])
                    for t in range(4):
                        op("dve", (lambda e, t=t: e.bn_stats(out=bst[:, 6 * t:6 * t + 6], in_=gv[t][:, :])), reads=[b_gv[t]], writes=[b_bst[t]])
                    for t in range(4):
                        op("dve", (lambda e, t=t: e.bn_aggr(out=small[:, 8 + 2 * t:10 + 2 * t], in_=bst[:, 6 * t:6 * t + 6])), reads=[b_bst[t]], writes=[b_bst[t]])
                    varv = small[:, 8:16].rearrange("p (t k) -> p t k", k=2)[:, :, 1]
                    op("act", (lambda e: e.activation(out=varv, in_=varv, func=AF.Sqrt, scale=1.0, bias=epsr[:, 1:2])), reads=b_bst + [b_eps], writes=b_bst)
                    op("dve", (lambda e: e.reciprocal(out=varv, in_=varv)), reads=b_bst, writes=b_bst)
                    for t in range(4):
                        op("dve", (lambda e, t=t: e.tensor_scalar(out=gv[t][:, :], in0=gv[t][:, :], scalar1=small[:, 8 + 2 * t:9 + 2 * t], scalar2=small[:, 9 + 2 * t:10 + 2 * t],
                                                                    op0=ALU.subtract, op1=ALU.mult)), reads=[b_gv[t], b_bst[t]], writes=[b_gv[t]])
                    for t in range(4):
                        op("dve", (lambda e, t=t: e.tensor_tensor(out=gv[t][:, :], in0=gv[t][:, :], in1=lng[:, :], op=ALU.mult)), reads=[b_gv[t], b_ln], writes=[b_gv[t]])
                    for t in range(4):
                        op("dve", (lambda e, t=t: e.tensor_tensor(out=vn[:, t, :], in0=gv[t][:, :], in1=lnb[:, :], op=ALU.add)), reads=[b_gv[t], b_ln], writes=[b_vn[t]])
                    w_release()
                    wu, bwu = w_acquire()
                    wu3 = wu[:, 0:8 * 512].rearrange("p (k n) -> p k n", n=512)
                    for c in range(4):
                        pb_i = nxt()
                        for k in range(8):
                            op("pe", (lambda e, k=k, c=c, pb_i=pb_i: e.matmul(psb[pb_i][:, :], wu3[:, k, c * 128:(c + 1) * 128], hT[:, k, :], start=(k == 0), stop=(k == 7))),
                               reads=[b_hT, bwu], writes=[b_ps[pb_i]])
                        op("act", (lambda e, c=c, pb_i=pb_i: e.activation(out=uT[:, c, :], in_=psb[pb_i][:, :], func=AF.Gelu_apprx_tanh)), reads=[b_ps[pb_i]], writes=[b_uT])
                    w_release()
                    for j in range(4):
                        pa_i = nxt(); pb_i = nxt()
                        for t in range(4):
                            op("pe", (lambda e, j=j, t=t, pa_i=pa_i: e.matmul(psb[pa_i][:, t * 128:(t + 1) * 128], vn[:, t, j * 128:(j + 1) * 128], wgT[:, 2 * j, :], start=True, stop=True)),
                               reads=[b_vn[t], b_wgT], writes=[b_ps[pa_i]])
                        for t in range(4):
                            op("pe", (lambda e, j=j, t=t, pb_i=pb_i: e.matmul(psb[pb_i][:, t * 128:(t + 1) * 128], vn[:, t, j * 128:(j + 1) * 128], wgT[:, 2 * j + 1, :], start=True, stop=True)),
                               reads=[b_vn[t], b_wgT], writes=[b_ps[pb_i]])
                        jj = j % 2
                        op("dve", (lambda e, j=j, jj=jj, pa_i=pa_i: e.tensor_tensor(out=tmpa[jj][0:64, :].rearrange("p (t q) -> p t q", q=128),
                                                                              in0=psb[pa_i][0:64, :].rearrange("p (t q) -> p t q", q=128),
                                                                              in1=sgb[0:64, j, :].unsqueeze(1).to_broadcast([64, 4, 128]), op=ALU.add)),
                           reads=[b_ps[pa_i], b_sgb], writes=[b_tmpa[jj]])
                        op("dve", (lambda e, j=j, jj=jj, pb_i=pb_i: e.tensor_tensor(out=tmpa[jj][64:128, :].rearrange("p (t q) -> p t q", q=128),
                                                                              in0=psb[pb_i][64:128, :].rearrange("p (t q) -> p t q", q=128),
                                                                              in1=sgb[64:128, j, :].unsqueeze(1).to_broadcast([64, 4, 128]), op=ALU.add)),
                           reads=[b_ps[pb_i], b_sgb], writes=[b_tmpa[jj]])
                        op("dve", (lambda e, j=j, jj=jj: e.tensor_tensor(out=aT[:, j, :], in0=tmpa[jj][:, :], in1=uT[:, j, :], op=ALU.mult)), reads=[b_tmpa[jj], b_uT], writes=[b_aT])
                    for hf in range(2):
                        wp, bwp = w_acquire()
                        wp3 = wp[:, 0:11 * 512].rearrange("p (k n) -> p k n", n=512)
                        wg_ = []
                        for gi in range(3):
                            w_, bw_ = w_acquire()
                            wg_.append((w_[:, 0:8 * 512].rearrange("p (k n) -> p k n", n=512), bw_))
                        for oc in range(4):
                            cs = slice(oc * 128, (oc + 1) * 128)
                            gbanks = []
                            for gi in range(3):
                                pg = nxt()
                                gbanks.append(pg)
                                w3, bw3 = wg_[gi]
                                for k in range(8):
                                    op("pe", (lambda e, k=k, pg=pg, w3=w3, cs=cs: e.matmul(psb[pg][:, :], w3[:, k, cs], hT[:, k, :], start=(k == 0), stop=(k == 7))),
                                       reads=[b_hT, bw3], writes=[b_ps[pg]])
                                op("act", (lambda e, gi=gi, pg=pg: e.activation(out=sg[gi][:, :], in_=psb[pg][:, :], func=AF.Sigmoid)), reads=[b_ps[pg]], writes=[b_sg[gi]])
                            pbanks = []
                            for bi_, (k0, nk, srcT, koff, bsrc) in enumerate(((0, 4, aT, 0, b_aT), (4, 3, bcT, 0, b_bcT), (7, 4, bcT, 3, b_bcT))):
                                pp = nxt()
                                pbanks.append(pp)
                                for k in range(nk):
                                    op("pe", (lambda e, k=k, pp=pp, k0=k0, nk=nk, srcT=srcT, koff=koff, cs=cs: e.matmul(
                                        psb[pp][:, :], wp3[:, k0 + k, cs], srcT[:, koff + k, :], start=(k == 0), stop=(k == nk - 1))),
                                       reads=[bsrc, bwp], writes=[b_ps[pp]])
                            op("dve", (lambda e, pbanks=pbanks: e.tensor_tensor(out=mm_[0][:, :], in0=psb[pbanks[0]][:, :], in1=sg[0][:, :], op=ALU.mult)),
                               reads=[b_ps[pbanks[0]], b_sg[0]], writes=[b_mm[0]])
                            op("dve", (lambda e, pbanks=pbanks: e.tensor_tensor(out=mm_[1][:, :], in0=psb[pbanks[1]][:, :], in1=sg[1][:, :], op=ALU.mult)),
                               reads=[b_ps[pbanks[1]], b_sg[1]], writes=[b_mm[1]])
                            op("dve", lambda e: e.tensor_tensor(out=mm_[0][:, :], in0=mm_[0][:, :], in1=mm_[1][:, :], op=ALU.add), reads=[b_mm[0], b_mm[1]], writes=[b_mm[0]])
                            op("dve", (lambda e, pbanks=pbanks: e.tensor_tensor(out=mm_[1][:, :], in0=psb[pbanks[2]][:, :], in1=sg[2][:, :], op=ALU.mult)),
                               reads=[b_ps[pbanks[2]], b_sg[2]], writes=[b_mm[1]])
                            op("dve", (lambda e, hf=hf, oc=oc: e.tensor_tensor(out=mT[:, hf * 4 + oc, :], in0=mm_[0][:, :], in1=mm_[1][:, :], op=ALU.add)),
                               reads=[b_mm[0], b_mm[1]], writes=[b_mT])
                        for _ in range(4):
                            w_release()
                    for hf in range(2):
                        wo, bwo = w_acquire()
                        wo3 = wo[:, 0:8 * 512].rearrange("p (k n) -> p k n", n=512)
                        for t in range(4):
                            pb_i = nxt()
                            for k in range(8):
                                op("pe", (lambda e, k=k, t=t, pb_i=pb_i, wo3=wo3: e.matmul(psb[pb_i][:, :], mT[:, k, t * 128:(t + 1) * 128], wo3[:, k, :], start=(k == 0), stop=(k == 7))),
                                   reads=[b_mT, bwo], writes=[b_ps[pb_i]])
                            op("dve", (lambda e, t=t, hf=hf, pb_i=pb_i: e.tensor_tensor(out=xt[t][:, hf * 512:(hf + 1) * 512], in0=xt[t][:, hf * 512:(hf + 1) * 512], in1=psb[pb_i][:, :], op=ALU.add)),
                               reads=[b_ps[pb_i], b_xt[t]], writes=[b_xt[t]])
                        w_release()
                    for t in range(4):
                        hbi = t % 4
                        op("act", (lambda e, t=t: e.activation(out=junk[:, :], in_=xt[t][:, :], func=AF.Square, accum_out=small[:, t:t + 1])), reads=[b_xt[t]], writes=[b_junk, b_ss[t]])
                        op("act", (lambda e, t=t: e.activation(out=small[:, t:t + 1], in_=small[:, t:t + 1], func=AF.Sqrt, scale=1.0 / D, bias=epsr[:, 0:1])), reads=[b_ss[t], b_eps], writes=[b_ss[t]])
                        op("dve", (lambda e, t=t: e.reciprocal(out=small[:, t:t + 1], in_=small[:, t:t + 1])), reads=[b_ss[t]], writes=[b_ss[t]])
                        op("dve", (lambda e, t=t, hbi=hbi: e.scalar_tensor_tensor(out=hb[hbi][:, :], in0=xt[t][:, :], scalar=small[:, t:t + 1], in1=g2t[:, :], op0=ALU.mult, op1=ALU.mult)),
                           reads=[b_xt[t], b_ss[t], b_g2], writes=[b_hb[hbi]])
                    for t in range(4):
                        hbi = t % 4
                        trbank = 6 + (t % 2)
                        pv = bf(psb[trbank][:, :])
                        for c in range(8):
                            op("pe", (lambda e, c=c, pv=pv, hbi=hbi: e.transpose(pv[:, c * 128:(c + 1) * 128], hb[hbi][:, c * 128:(c + 1) * 128], ident[:, :])),
                               reads=[b_hb[hbi], b_ident], writes=[b_ps[trbank]])
                        op("act", (lambda e, t=t, pv=pv: e.activation(out=hT[:, :, t * 128:(t + 1) * 128], in_=pv.rearrange("p (c q) -> p c q", q=128), func=AF.Copy)),
                           reads=[b_ps[trbank]], writes=[b_hT])
                    for f in range(11):
                        wgu, bwgu = w_acquire()
                        wgu3 = wgu[:, 0:16 * 256].rearrange("p (k n) -> p k n", n=256)
                        for ch in range(2):
                            pg = nxt(); pu = nxt()
                            for k in range(8):
                                op("pe", (lambda e, k=k, ch=ch, pg=pg: e.matmul(psb[pg][:, :], wgu3[:, k, ch * 128:(ch + 1) * 128], hT[:, k, :], start=(k == 0), stop=(k == 7))),
                                   reads=[b_hT, bwgu], writes=[b_ps[pg]])
                            for k in range(8):
                                op("pe", (lambda e, k=k, ch=ch, pu=pu: e.matmul(psb[pu][:, :], wgu3[:, 8 + k, ch * 128:(ch + 1) * 128], hT[:, k, :], start=(k == 0), stop=(k == 7))),
                                   reads=[b_hT, bwgu], writes=[b_ps[pu]])
                            sj = (f * 2 + ch) % 3
                            op("act", (lambda e, pg=pg, sj=sj: e.activation(out=sgf[sj][:, :], in_=psb[pg][:, :], func=AF.Silu)), reads=[b_ps[pg]], writes=[b_sgf[sj]])
                            op("dve", (lambda e, pu=pu, sj=sj, f=f, ch=ch: e.tensor_tensor(out=actT[:, f * 2 + ch, :], in0=psb[pu][:, :], in1=sgf[sj][:, :], op=ALU.mult)),
                               reads=[b_ps[pu], b_sgf[sj]], writes=[b_actT])
                        w_release()
                    if tb + 1 < NB:
                        xn = xt2[1 - xset]
                        bxn = b_xt2[1 - xset]
                        for t in range(4):
                            norm_a(t, s0 + TB + t * 128, xsrc[gidx], g1t, b_g1, "pool", t % 4, xn, bxn, f"d_x{1 - xset}_{t}")
                    if si == 0 and l + 1 < L:
                        nj = len(cv_jobs[l + 1])
                        issue_cv(l + 1, (tb * nj) // NB, ((tb + 1) * nj) // NB)
                    for hf in range(2):
                        wda, bwda = w_acquire()
                        wdb, bwdb = w_acquire()
                        wd3 = [wda[:, 0:11 * 512].rearrange("p (k n) -> p k n", n=512), wdb[:, 0:11 * 512].rearrange("p (k n) -> p k n", n=512)]
                        for t in range(4):
                            pb_i = nxt()
                            for k in range(22):
                                op("pe", (lambda e, k=k, t=t, pb_i=pb_i, wd3=wd3: e.matmul(psb[pb_i][:, :], actT[:, k, t * 128:(t + 1) * 128], wd3[k // 11][:, k % 11, :], start=(k == 0), stop=(k == 21))),
                                   reads=[b_actT, bwda, bwdb], writes=[b_ps[pb_i]])
                            op("dve", (lambda e, t=t, hf=hf, pb_i=pb_i: e.tensor_tensor(out=xt[t][:, hf * 512:(hf + 1) * 512], in0=xt[t][:, hf * 512:(hf + 1) * 512], in1=psb[pb_i][:, :], op=ALU.add)),
                               reads=[b_ps[pb_i], b_xt[t]], writes=[b_xt[t]])
                        w_release()
                        w_release()
                    for t in range(4):
                        tok0 = s0 + t * 128
                        op("pool", (lambda e, t=t, tok0=tok0: e.dma_start(out=ydst[gidx, tok0:tok0 + 128, :], in_=xt[t][:, :])), reads=[b_xt[t]], dma=f"d_x{xset}_{t}")
                S.barrier()
        S.barrier()
        S.emit()
    return nc


N_CORES = 8
_PROGRAM_CACHE = {}


def _lam_inits(depth):
    return [0.8 - 0.6 * math.exp(-0.3 * l) for l in range(depth)]


def kernel(x_prompt, x_sample, rel_bias, norm1_g, w_in, sgu_ln_g, sgu_ln_b, sgu_w, sgu_b,
           qn_b, kn_b, qn_c, kn_c, lam_q1, lam_k1, lam_q2, lam_k2, subln_g,
           w_pa, w_pb, w_pc, w_o, norm2_g, w_gu, w_down):
    x_prompt = np.asarray(x_prompt, dtype=np.float32)
    x_sample = np.asarray(x_sample, dtype=np.float32)
    depth = int(np.asarray(w_in).shape[0])
    nb_p, s_p = x_prompt.shape[0], x_prompt.shape[1]
    nb_s, s_s = x_sample.shape[0], x_sample.shape[1]
    ncores = N_CORES
    assert nb_p % ncores == 0 and nb_s % ncores == 0
    pp, ps_ = nb_p // ncores, nb_s // ncores
    seq_lens = [("p", i, s_p) for i in range(pp)] + [("s", i, s_s) for i in range(ps_)]
    key = (tuple(seq_lens), depth)
    if key not in _PROGRAM_CACHE:
        _PROGRAM_CACHE[key] = build_program(seq_lens, depth, _lam_inits(depth))
    nc = _PROGRAM_CACHE[key]
    oh, mask, ident = host_constants()
    shared = dict(rel_bias=rel_bias, norm1_g=norm1_g, norm2_g=norm2_g, w_in=w_in, sgu_ln_g=sgu_ln_g, sgu_ln_b=sgu_ln_b,
                  sgu_w=sgu_w, sgu_b=sgu_b, qn_b=qn_b, kn_b=kn_b, qn_c=qn_c, kn_c=kn_c, lam_q1=lam_q1, lam_k1=lam_k1,
                  lam_q2=lam_q2, lam_k2=lam_k2, subln_g=subln_g, w_pa=w_pa, w_pb=w_pb, w_pc=w_pc, w_o=w_o, w_gu=w_gu,
                  w_down=w_down, c_oh=oh, c_mask=mask, c_ident=ident)
    shared = {k: np.ascontiguousarray(np.asarray(v, dtype=np.float32)) for k, v in shared.items()}
    in_maps = []
    for c in range(ncores):
        m = dict(shared)
        m["xp"] = np.ascontiguousarray(x_prompt[c * pp:(c + 1) * pp])
        m["xs"] = np.ascontiguousarray(x_sample[c * ps_:(c + 1) * ps_])
        in_maps.append(m)
    res = run_bass_kernel_spmd(nc, in_maps, core_ids=list(range(ncores)))
    yp = np.concatenate([np.asarray(r["yp"]) for r in res.results], axis=0).astype(np.float32)
    ys = np.concatenate([np.asarray(r["ys"]) for r in res.results], axis=0).astype(np.float32)
    return (yp, ys)
```
